# Optimizing a Trainium2 kernel written in Bass

```python
import math
import jax, jax.numpy as jnp
from jax import lax
import numpy as np

D_MODEL = 1024
BATCH = 2
SEQ = 8192
DEPTH = 4

N_META = 16
BLOCK = 128
FRONT = BLOCK
BR_WIDTH = D_MODEL // 2
N_BRANCH = 4
HG_HEADS = 4
HG_DK = BR_WIDTH // HG_HEADS
SB_HEADS = 8
SB_DH = BR_WIDTH // SB_HEADS
POOL_WINDOWS = (2, 4, 8, 16)
POOL_GROUP = BR_WIDTH // len(POOL_WINDOWS)
CONV_WIDTH = 3
D_FF = 4 * D_MODEL
N_SPLIT = 11
IN_COLS = N_SPLIT * BR_WIDTH + N_BRANCH * D_MODEL
EPS = 1e-6

kernel_name = "hybrid_hgrn2_pool_stickbreak_shortconv"

F32 = jnp.float32


def rmsnorm(x, g):
    xf = x.astype(F32)
    y = xf * lax.rsqrt(jnp.mean(xf * xf, axis=-1, keepdims=True) + EPS)
    return (y * g.astype(F32)).astype(x.dtype)


def hgrn2_mixer(q, f_logit, i, g, lb, valid, norm_g):
    B_, L, _ = q.shape
    n_chunks = L // BLOCK
    vmask = valid[None, :, None]
    xf = f_logit.astype(F32)
    log_f = jnp.where(vmask, jnp.log(lb + (1.0 - lb) * jax.nn.sigmoid(xf)), 0.0)
    k = jnp.where(vmask, (1.0 - lb) * jax.nn.sigmoid(-xf), 0.0)
    qf = q.astype(F32)
    vf = jnp.where(vmask, i.astype(F32), 0.0)

    def to_chunks(a):
        return a.reshape(B_, n_chunks, BLOCK, HG_HEADS, HG_DK).transpose(1, 0, 2, 3, 4)

    causal = jnp.tril(jnp.ones((BLOCK, BLOCK), bool))[None, :, :, None, None]

    def step(S, inp):
        qc, kc, ic, lfc = inp
        b = jnp.cumsum(lfc, axis=1)
        o_inter = jnp.einsum('bthk,bhkv->bthv', qc * jnp.exp(b), S)
        diff = b[:, :, None] - b[:, None, :]
        decay = jnp.exp(jnp.where(causal, diff, -jnp.inf))
        scores = jnp.einsum('bthk,btshk,bshk->bhts', qc, decay, kc)
        o_intra = jnp.einsum('bhts,bshv->bthv', scores, ic)
        b_last = b[:, -1]
        S_new = jnp.exp(b_last)[..., None] * S + jnp.einsum(
            'bshk,bshv->bhkv', kc * jnp.exp(b_last[:, None] - b), ic)
        return S_new, o_inter + o_intra

    S0 = jnp.zeros((B_, HG_HEADS, HG_DK, HG_DK), F32)
    _, o = lax.scan(step, S0, (to_chunks(qf), to_chunks(k), to_chunks(vf), to_chunks(log_f)))
    o = o.transpose(1, 0, 2, 3, 4).reshape(B_, L, HG_HEADS, HG_DK)
    o = o * lax.rsqrt(jnp.mean(o * o, axis=-1, keepdims=True) + EPS)
    o = o * norm_g.astype(F32).reshape(HG_HEADS, HG_DK)
    o = o.reshape(B_, L, BR_WIDTH) * jax.nn.sigmoid(g.astype(F32))
    return o.astype(q.dtype)


def multiscale_pool(v, valid, pool_w, pool_scale):
    B_, L, _ = v.shape
    vf = jnp.where(valid[None, :, None], v.astype(F32), 0.0)
    cs = jnp.cumsum(vf, axis=1)
    cnt = jnp.cumsum(valid.astype(F32))
    means = []
    for gi, w in enumerate(POOL_WINDOWS):
        c = cs[:, :, gi * POOL_GROUP:(gi + 1) * POOL_GROUP]
        c_prev = jnp.pad(c, ((0, 0), (w, 0), (0, 0)))[:, :L]
        n_win = jnp.maximum(cnt - jnp.pad(cnt, (w, 0))[:L], 1.0)
        means.append((c - c_prev) / n_win[None, :, None])
    u = (jnp.concatenate(means, axis=-1) - vf).reshape(B_, L, len(POOL_WINDOWS), POOL_GROUP)
    y = jnp.einsum('blgc,gcd->blgd', u, pool_w.astype(F32)).reshape(B_, L, BR_WIDTH)
    return (y * pool_scale.astype(F32)).astype(v.dtype)


def stick_breaking_attention(q, k, v, valid):
    B_, L, _ = q.shape
    n_blk = L // BLOCK
    scale = 1.0 / math.sqrt(SB_DH)
    qh = q.astype(F32).reshape(B_, n_blk, BLOCK, SB_HEADS, SB_DH).transpose(1, 0, 2, 3, 4)
    kh = k.astype(F32).reshape(B_, L, SB_HEADS, SB_DH)
    vh = v.astype(F32).reshape(B_, L, SB_HEADS, SB_DH)
    key_pos = jnp.arange(L)

    def one_block(args):
        q_blk, start = args
        q_pos = start + jnp.arange(BLOCK)
        mask = (key_pos[None, :] < q_pos[:, None]) & valid[None, :]
        z = jnp.einsum('bqhd,bkhd->bhqk', q_blk, kh) * scale
        log_keep = jnp.where(mask, jax.nn.log_sigmoid(-z), 0.0)
        between = lax.cumsum(log_keep, axis=3, reverse=True) - log_keep
        A = jnp.where(mask, jnp.exp(jax.nn.log_sigmoid(z) + between), 0.0)
        return jnp.einsum('bhqk,bkhd->bqhd', A, vh)

    o = lax.map(one_block, (qh, jnp.arange(n_blk) * BLOCK))
    return o.transpose(1, 0, 2, 3, 4).reshape(B_, L, BR_WIDTH).astype(q.dtype)


def short_conv_mixer(h_in, b_gate, c_gate, conv_w, valid):
    u = jnp.where(valid[None, :, None], c_gate * h_in, 0.0).astype(h_in.dtype)
    y = lax.conv_general_dilated(
        u, conv_w.astype(h_in.dtype)[:, None, :], window_strides=(1,),
        padding=[(CONV_WIDTH - 1, 0)], dimension_numbers=('NWC', 'WIO', 'NWC'),
        feature_group_count=BR_WIDTH)
    return b_gate * y


def hybrid_layer(x, valid, lb, norm1_g, w_in, hg_norm_g, pool_w, pool_scale, conv_w,
                 w_branch, w_o, norm2_g, w_up, w_down):
    B_, L, _ = x.shape
    h = rmsnorm(x, norm1_g)
    proj = jnp.einsum('bld,dc->blc', h, w_in)
    split_at = [BR_WIDTH * (j + 1) for j in range(N_SPLIT)]
    (hg_q, hg_f, hg_i, hg_g, pool_v, sb_q, sb_k, sb_v,
     sc_h, sc_b, sc_c, gate_logits) = jnp.split(proj, split_at, axis=-1)

    branches = (
        hgrn2_mixer(hg_q, hg_f, hg_i, hg_g, lb, valid, hg_norm_g),
        multiscale_pool(pool_v, valid, pool_w, pool_scale),
        stick_breaking_attention(sb_q, sb_k, sb_v, valid),
        short_conv_mixer(sc_h, sc_b, sc_c, conv_w, valid),
    )
    gates = jax.nn.sigmoid(gate_logits.astype(F32)).reshape(B_, L, N_BRANCH, D_MODEL)
    mixed = jnp.zeros((B_, L, D_MODEL), F32)
    for n in range(N_BRANCH):
        mixed = mixed + gates[:, :, n] * jnp.einsum('blw,wd->bld', branches[n], w_branch[n]).astype(F32)
    x = x + jnp.einsum('bld,de->ble', mixed.astype(x.dtype), w_o)

    h2 = rmsnorm(x, norm2_g)
    act = jnp.square(jax.nn.relu(jnp.einsum('bld,df->blf', h2, w_up)))
    return x + jnp.einsum('blf,fd->bld', act, w_down)


def setup_inputs(seed: int = 0) -> dict:
    key = jax.random.key(seed)
    ks = jax.random.split(key, 16)

    def nrm(k, shape, fan_in):
        return jax.random.normal(k, shape, F32) * (fan_in ** -0.5)

    def gain(k, shape, s=0.02):
        return 1.0 + s * jax.random.normal(k, shape, F32)

    return {
        "x": jax.random.normal(ks[0], (BATCH, SEQ, D_MODEL), F32),
        "meta_tokens": jax.random.normal(ks[1], (N_META, D_MODEL), F32),
        "lb_logits": 0.5 * jax.random.normal(ks[2], (DEPTH, BR_WIDTH), F32),
        "norm1_g": gain(ks[3], (DEPTH, D_MODEL)),
        "w_in": nrm(ks[4], (DEPTH, D_MODEL, IN_COLS), D_MODEL),
        "hg_norm_g": gain(ks[5], (DEPTH, BR_WIDTH)),
        "pool_w": nrm(ks[6], (DEPTH, len(POOL_WINDOWS), POOL_GROUP, POOL_GROUP), POOL_GROUP),
        "pool_scale": gain(ks[7], (DEPTH, BR_WIDTH), 0.1),
        "conv_w": nrm(ks[8], (DEPTH, CONV_WIDTH, BR_WIDTH), CONV_WIDTH),
        "w_branch": nrm(ks[9], (DEPTH, N_BRANCH, BR_WIDTH, D_MODEL), BR_WIDTH),
        "w_o": nrm(ks[10], (DEPTH, D_MODEL, D_MODEL), D_MODEL),
        "norm2_g": gain(ks[11], (DEPTH, D_MODEL)),
        "w_up": nrm(ks[12], (DEPTH, D_MODEL, D_FF), D_MODEL),
        "w_down": nrm(ks[13], (DEPTH, D_FF, D_MODEL), D_FF),
        "final_norm_g": gain(ks[14], (D_MODEL,)),
    }


def reference(x, meta_tokens, lb_logits, norm1_g, w_in, hg_norm_g, pool_w, pool_scale, conv_w,
              w_branch, w_o, norm2_g, w_up, w_down, final_norm_g):
    B_, S_, _ = x.shape
    dt = x.dtype
    lead = jnp.concatenate([jnp.zeros((FRONT - N_META, D_MODEL), dt), meta_tokens.astype(dt)], axis=0)
    h = jnp.concatenate([jnp.broadcast_to(lead[None], (B_, FRONT, D_MODEL)), x], axis=1)
    L = FRONT + S_
    valid = jnp.arange(L) >= (FRONT - N_META)
    cum = jnp.cumsum(jax.nn.softmax(lb_logits.astype(F32), axis=0), axis=0)
    lower_bounds = cum - cum[0]
    for layer in range(DEPTH):
        h = hybrid_layer(h, valid, lower_bounds[layer], norm1_g[layer], w_in[layer], hg_norm_g[layer],
                         pool_w[layer], pool_scale[layer], conv_w[layer], w_branch[layer], w_o[layer],
                         norm2_g[layer], w_up[layer], w_down[layer])
    return rmsnorm(h[:, FRONT:], final_norm_g)
```

```python
from contextlib import ExitStack
import numpy as np
import ml_dtypes
import concourse.bass as bass
import concourse.mybir as mybir
from concourse.bass_utils import run_bass_kernel_spmd

F32 = mybir.dt.float32
BF16 = mybir.dt.bfloat16
ALU = mybir.AluOpType
AF = mybir.ActivationFunctionType

D = 1024
NCK = 8
NMETA = 16
TT = 512
DEPTH = 4
EPS = 1e-6
POOL_WINDOWS = (2, 4, 8, 16)
SEM_CH = 30000
PARTS = {"conv", "pool", "hgrn", "attn"}


class Prog:
    ENGS = ("pe", "act", "dve", "pool", "sp")

    def __init__(self, nc, state=None):
        self.nc = nc
        self.ops = {e: [] for e in self.ENGS}
        self.state = state if state is not None else {"cnt": {e: 0 for e in self.ENGS}, "dma_cnt": {}, "handles": {},
                                                      "es": ExitStack()}
        self.cnt = self.state["cnt"]
        self.lastw = {}
        self.readers = {}
        self.known = {e: {} for e in self.ENGS}
        self.dma_cnt = self.state["dma_cnt"]
        self.semkeys = []
        self.semset = set()
        self.out_tokens = []
        self.init = {}
        self.excl = set(["M0", "M1", "Z0", "Z1", "G", "CS", "O", "TB"] + ["cp%d" % i for i in range(8)])

    def _sem(self, key):
        if key not in self.semset:
            self.semset.add(key)
            self.semkeys.append(key)
        return key

    LIMIT = None
    nops = 0

    def op(self, eng, fn, reads=(), writes=(), dma_key=None, is_out=False, inc=16):
        Prog.nops += 1
        if Prog.LIMIT is not None and Prog.nops > Prog.LIMIT and not is_out:
            return None
        deps = []
        for k in reads:
            t = self.lastw.get(k)
            if t is not None:
                deps.append(t)
            if k in self.excl:
                deps.extend(r for r in self.readers.get(k, ()) if r[2] != eng)
        for k in writes:
            t = self.lastw.get(k)
            if t is not None:
                deps.append(t)
            deps.extend(self.readers.get(k, ()))
        waits = {}
        kn = self.known[eng]
        for (sk, val, deng) in deps:
            if deng == eng and dma_key is None and eng == "pe":
                continue
            if kn.get(sk, 0) >= val:
                continue
            if waits.get(sk, 0) < val:
                waits[sk] = val
        for sk, val in waits.items():
            kn[sk] = val
        if dma_key is not None:
            sk = self._sem(("dma", dma_key))
            n = self.dma_cnt.get(sk, 0) + 1
            self.dma_cnt[sk] = n
            done = (sk, inc * n, "dma")
        else:
            idx = self.cnt[eng]
            self.cnt[eng] += 1
            sk = self._sem(("eng", eng, idx // SEM_CH))
            done = (sk, idx % SEM_CH + 1, eng)
            inc = 1
        self.ops[eng].append((list(waits.items()), fn, sk, inc))
        for k in writes:
            self.lastw[k] = done
            self.readers[k] = []
        for k in reads:
            self.readers.setdefault(k, []).append(done)
        if is_out:
            self.out_tokens.append(done)
        return done

    def emit(self):
        nc = self.nc
        final_waits = {}
        for (sk, val, _e) in self.out_tokens:
            if final_waits.get(sk, 0) < val:
                final_waits[sk] = val
        with ExitStack() as es:
            sems = self.state["handles"]
            for sk in self.semkeys:
                if sk not in sems:
                    sems[sk] = self.state["es"].enter_context(nc.semaphore("s%d" % len(sems)))
            block = es.enter_context(nc.Block())

            def run(engname, handle):
                if engname in self.init:
                    self.init[engname](handle, es)
                for (waits, fn, sk, inc) in self.ops[engname]:
                    for wk, wv in waits:
                        handle.wait_ge(sems[wk], wv)
                    fn(handle).then_inc(sems[sk], inc)
                if engname == "sp":
                    for wk, wv in final_waits.items():
                        handle.wait_ge(sems[wk], wv)

            @block.tensor
            def _(e):
                run("pe", e)

            @block.scalar
            def _(e):
                run("act", e)

            @block.vector
            def _(e):
                run("dve", e)

            @block.gpsimd
            def _(e):
                run("pool", e)

            @block.sync
            def _(e):
                run("sp", e)


class Ctx:
    def __init__(self, nc, P, es):
        self.nc, self.P, self.es = nc, P, es

    def sb(self, name, shape, dt):
        return self.es.enter_context(self.nc.sbuf_tensor(name, shape, dt))

    def ps(self, name, shape, dt=F32):
        return self.es.enter_context(self.nc.psum_tensor(name, shape, dt))

    def mm(self, out, lhsT, rhs, start, stop, r, w, skip=False):
        if skip:
            self.P.op("pe", lambda e: e.matmul(out, lhsT, rhs, start=start, stop=stop, skip_group_check=True),
                      reads=r, writes=w)
        else:
            self.P.op("pe", lambda e: e.matmul(out, lhsT, rhs, start=start, stop=stop), reads=r, writes=w)

    def tr(self, out, in_, ident, r, w):
        self.P.op("pe", lambda e: e.transpose(out, in_, ident), reads=r, writes=w)

    def act(self, out, in_, func, r, w, bias=None, scale=None):
        kw = {}
        if bias is not None:
            kw["bias"] = bias
        if scale is not None:
            kw["scale"] = scale
        self.P.op("act", lambda e: e.activation(out=out, in_=in_, func=func, **kw), reads=r, writes=w)

    def tt(self, eng, out, in0, in1, op, r, w):
        self.P.op(eng, lambda e: e.tensor_tensor(out=out, in0=in0, in1=in1, op=op), reads=r, writes=w)

    def ts(self, eng, out, in0, s1, op0, r, w, s2=None, op1=None):
        if op1 is None:
            self.P.op(eng, lambda e: e.tensor_scalar(out=out, in0=in0, scalar1=s1, scalar2=None, op0=op0),
                      reads=r, writes=w)
        else:
            self.P.op(eng, lambda e: e.tensor_scalar(out=out, in0=in0, scalar1=s1, scalar2=s2, op0=op0, op1=op1),
                      reads=r, writes=w)

    def stt(self, eng, out, in0, scalar, in1, op0, op1, r, w):
        self.P.op(eng, lambda e: e.scalar_tensor_tensor(out=out, in0=in0, scalar=scalar, in1=in1, op0=op0, op1=op1),
                  reads=r, writes=w)

    def copy(self, eng, out, in_, r, w):
        if eng == "act":
            self.P.op("act", lambda e: e.activation(out=out, in_=in_, func=AF.Copy), reads=r, writes=w)
        else:
            self.P.op(eng, lambda e: e.tensor_copy(out=out, in_=in_), reads=r, writes=w)

    def recip(self, out, in_, r, w):
        self.P.op("dve", lambda e: e.reciprocal(out=out, in_=in_), reads=r, writes=w)

    def memset(self, eng, ap, val, w):
        self.P.op(eng, lambda e: e.memset(ap, val), writes=w)

    def dma(self, eng, out, in_, r, w, key, is_out=False):
        self.P.op(eng, lambda e: e.dma_start(out=out, in_=(in_() if callable(in_) else in_)), reads=r, writes=w,
                  dma_key=key, is_out=is_out)


def seq_tiles(S):
    return [(0, NMETA)] + [(NMETA + TT * i, TT) for i in range(S // TT)]


def gblock(g):
    return (0, NMETA) if g == 0 else (NMETA + 128 * (g - 1), 128)


def host_consts(window):
    i = np.arange(128)
    c = {}
    c["ident"] = np.eye(128, dtype=np.float32)
    c["ones"] = np.ones((128, 128), np.float32)
    c["strict"] = (i[:, None] < i[None, :]).astype(np.float32)
    c["negun"] = -(i[:, None] >= i[None, :]).astype(np.float32)
    c["triu"] = (i[:, None] <= i[None, :]).astype(np.float32)
    w = window
    s, t = i[:, None], i[None, :]
    band = ((s <= t) & (s > t - w)).astype(np.float32)
    c["mdiag"] = band / w - np.eye(128, dtype=np.float32)
    c["moff"] = ((s - 128) > (t - w)).astype(np.float32) / w
    mf = np.zeros((128, 128), np.float32)
    mf[:16] = (s[:16] > (16 + t - w)).astype(np.float32) / w
    c["mofff"] = mf
    bm = np.zeros((128, 128), np.float32)
    bm[:16, :16] = band[:16, :16]
    c["bmeta"] = bm
    invn = np.zeros((128, 128), np.float32)
    invn[:, :16] = 1.0 / np.minimum(w, np.arange(16) + 1.0)[None, :]
    c["invn"] = invn
    names = ["ident", "ones", "strict", "negun", "triu", "mdiag", "moff", "mofff", "bmeta", "invn"]
    return names, np.concatenate([c[n] for n in names], axis=1)


CONST_NAMES = ["ident", "ones", "strict", "negun", "triu", "mdiag", "moff", "mofff", "bmeta", "invn"]


def emit_B(cx, S, layer, xT_ap, w_ap, g1_ap, lbl_ap, vec_ap, pw_ap, cst_ap, out_ap, pfx="", x_loader=None,
           after_store=None):
    P = cx.P
    Ltot = NMETA + S
    tiles = seq_tiles(S)
    NB = 1 + S // 128
    k = lambda s: pfx + s
    sk = lambda s: "B." + s

    w_bf = cx.sb(k("w_bf"), [128, NCK, 11 * 128], BF16)
    cst_f = cx.sb(k("cst_f"), [128, 10 * 128], F32)
    cst_b = cx.sb(k("cst_b"), [128, 10 * 128], BF16)
    g1 = cx.sb(k("g1"), [128, 8], F32)
    lbl = cx.sb(k("lbl"), [128, 4], F32)
    vec = cx.sb(k("vec"), [128, 8], F32)
    pw_b = cx.sb(k("pw_b"), [128, 128], BF16)
    sm = cx.sb(k("sm"), [128, 16], F32)
    kT = cx.sb(k("kT"), [128, Ltot], BF16)
    vc = cx.sb(k("vc"), [128, NB, 128], BF16)
    Sst = cx.sb(k("Sst"), [128, 128], F32)
    Ssc = cx.sb(k("Ssc"), [128, 128], BF16)
    ubuf = cx.sb(k("ubuf"), [128, TT + 2], F32)
    pv = cx.sb(k("pv"), [128, 5, 128], BF16)
    onesf = cx.sb(k("onesf"), [128, TT], F32)
    xt = [cx.sb(k("xt%d" % i), [128, NCK, TT], F32) for i in range(2)]
    sqb = cx.sb(k("sqb"), [128, NCK, TT], BF16)
    hT = cx.sb(k("hT"), [128, NCK, TT], BF16)
    rstd = cx.sb(k("rstd"), [128, TT], F32)
    proj = cx.sb(k("proj"), [128, 8, TT], F32)
    qT = cx.sb(k("qT"), [128, TT], BF16)
    itok = cx.sb(k("itok"), [128, 4, 128], BF16)
    brout = [cx.sb(k("brout%d" % i), [128, 4, TT], BF16) for i in range(2)]
    t1 = cx.sb(k("t1"), [128, TT], F32)
    t2 = cx.sb(k("t2"), [128, TT], F32)
    fval = cx.sb(k("fval"), [128, TT], F32)
    kk = cx.sb(k("kk"), [128, TT], F32)
    lf = cx.sb(k("lf"), [128, TT], F32)
    Bc = cx.sb(k("Bc"), [128, TT + 1], F32)
    cex = [cx.sb(k("cex%d" % i), [128, 64], F32) for i in range(3)]
    qe = cx.sb(k("qe"), [128, TT], BF16)
    ke = cx.sb(k("ke"), [128, TT], BF16)
    kdT = cx.sb(k("kdT"), [128, TT], BF16)
    kdtok = cx.sb(k("kdtok"), [128, 4, 128], BF16)
    scm = cx.sb(k("scm"), [128, 64], BF16)
    dsm = cx.sb(k("dsm"), [128, 4, 8], F32)
    osb = cx.sb(k("osb"), [128, TT], F32)
    osq = cx.sb(k("osq"), [128, TT], BF16)
    uTb = cx.sb(k("uTb"), [128, 128], BF16)
    uTf = cx.sb(k("uTf"), [128, 16], F32)
    Ef2 = cx.sb(k("Ef2"), [128, 2 * TT], F32)
    spb2 = cx.sb(k("spb2"), [128, 2 * TT], BF16)
    lg2 = cx.sb(k("lg2"), [128, 2 * TT], F32)
    Ab2 = cx.sb(k("Ab2"), [128, 2 * TT], BF16)
    Ef = [Ef2[:, i * TT:(i + 1) * TT] for i in range(2)]
    spb = [spb2[:, i * TT:(i + 1) * TT] for i in range(2)]
    lg = [lg2[:, i * TT:(i + 1) * TT] for i in range(2)]
    Ab = [Ab2[:, i * TT:(i + 1) * TT] for i in range(2)]
    h3 = lambda t_: t_[:, :].rearrange("p (h t) -> p h t", h=2)
    carry = [cx.sb(k("carry%d" % i), [128, TT], F32) for i in range(2)]
    ob = cx.sb(k("ob"), [128, 4 * 128], BF16)
    zbf = cx.sb(k("zbf"), [128, TT], BF16)
    M = [cx.ps(k("pM%d" % i), [128, TT]) for i in range(2)]
    Z = [cx.ps(k("pZ%d" % i), [128, TT]) for i in range(2)]
    G = cx.ps(k("pG"), [128, TT])
    CS = cx.ps(k("pCS"), [128, TT])
    O = cx.ps(k("pO"), [128, TT])
    TB = cx.ps(k("pTB"), [128, 2 * TT], BF16)

    def C(name, rows=128, cols=128, bf=True):
        i = CONST_NAMES.index(name)
        src = cst_b if bf else cst_f
        return src[0:rows, i * 128:i * 128 + cols]

    cx.dma("sp", cst_f[:], cst_ap, [], ["cst_f"], sk("cst_f"))
    cx.dma("pool", cst_b[:], cst_ap, [], ["cst_b"], sk("cst_b"))
    cx.dma("sp", g1[:], g1_ap, [], ["g1"], sk("g1"))
    cx.dma("sp", lbl[:], lbl_ap, [], ["lbl"], sk("lbl"))
    cx.dma("sp", vec[:], vec_ap, [], ["vec"], sk("vec"))
    cx.dma("pool", pw_b[:], pw_ap, [], ["pw_b"], sk("pw_b"))
    for c in range(NCK):
        cx.dma("pool", w_bf[:, c, :], w_ap[c * 128:(c + 1) * 128, :], [], [("w", c)], sk("w%d" % c))
    cx.memset("pool", onesf[:], 1.0, ["onesf"])
    cx.memset("pool", zbf[:], 0.0, ["zbf"])
    cx.memset("pool", Sst[:], 0.0, ["S"])
    cx.memset("pool", ubuf[:, 0:2], 0.0, ["ubuf"])
    cx.memset("pool", Bc[:, 0:1], 0.0, ["Bc"])
    P.op("dve", lambda e: e.reduce_max(out=sm[:, 0:1], in_=lbl[:], axis=mybir.AxisListType.X), reads=["lbl"], writes=["sm"])
    cx.ts("dve", sm[:, 1:2], sm[:, 0:1], -1.0, ALU.mult, ["sm"], ["sm"])
    cx.act(sm[:, 8:12], lbl[:], AF.Exp, ["sm", "lbl"], ["sm"], bias=sm[:, 1:2])
    P.op("dve", lambda e: e.reduce_sum(out=sm[:, 2:3], in_=sm[:, 8:12], axis=mybir.AxisListType.X), reads=["sm"], writes=["sm"])
    cx.recip(sm[:, 3:4], sm[:, 2:3], ["sm"], ["sm"])
    if layer == 0:
        cx.memset("dve", sm[:, 4:5], 0.0, ["sm"])
    else:
        P.op("dve", lambda e: e.reduce_sum(out=sm[:, 4:5], in_=sm[:, 9:9 + layer], axis=mybir.AxisListType.X), reads=["sm"], writes=["sm"])
        cx.tt("dve", sm[:, 4:5], sm[:, 4:5], sm[:, 3:4], ALU.mult, ["sm"], ["sm"])
    cx.ts("dve", sm[:, 5:6], sm[:, 4:5], -1.0, ALU.mult, ["sm"], ["sm"], s2=1.0, op1=ALU.add)
    lb_ap, oml_ap = sm[:, 4:5], sm[:, 5:6]

    def load_x(ti):
        off, T = tiles[ti]
        buf = xt[ti % 2]
        if x_loader is not None:
            x_loader(cx, ti, off, T, buf, ("xt", ti % 2), sk("xt%d" % (ti % 2)))
            return
        cx.dma("sp", buf[:, :, 0:T], xT_ap(off, T).rearrange("(c p) t -> p c t", p=128), ["xg"], [("xt", ti % 2)],
               sk("xt%d" % (ti % 2)))

    load_x(0)
    for ti, (off, T) in enumerate(tiles):
        if ti + 1 < len(tiles):
            load_x(ti + 1)
        x = xt[ti % 2]
        xk = ("xt", ti % 2)
        meta = (ti == 0)
        if meta:
            blocks = [(0, NMETA)]
            gb0 = 0
        else:
            blocks = [(128 * m, 128) for m in range(T // 128)]
            gb0 = 4 * (ti - 1) + 1
        bo = brout[ti % 2]
        bok = ("brout", ti % 2)


        cx.act(sqb[:, :, 0:T], x[:, :, 0:T], AF.Square, [xk], ["sqb"])
        for c in range(NCK):
            cx.mm(M[0][:, 0:T], C("ones"), sqb[:, c, 0:T], c == 0, c == NCK - 1, ["sqb", "cst_b"], ["M0"])
        cx.act(rstd[:, 0:T], M[0][:, 0:T], AF.Ln, ["M0"], ["rstd"], bias=EPS, scale=1.0 / D)
        cx.act(rstd[:, 0:T], rstd[:, 0:T], AF.Exp, ["rstd"], ["rstd"], scale=-0.5)
        for c in range(NCK):
            cx.stt("dve", hT[:, c, 0:T], x[:, c, 0:T], g1[:, c:c + 1], rstd[:, 0:T],
                   ALU.mult, ALU.mult, [xk, "rstd", "g1"], [("hT", c)])
        hTk = [("hT", c) for c in range(NCK)]
        wk = [("w", c) for c in range(NCK)]

        for n in range(8):
            pm = M[n % 2]
            pk = "M%d" % (n % 2)
            for c in range(NCK):
                cx.mm(pm[:, 0:T], w_bf[:, c, n * 128:(n + 1) * 128], hT[:, c, 0:T], c == 0, c == NCK - 1,
                      hTk + wk, [pk])
            if n == 3:
                cx.copy("act", qT[:, 0:T], pm[:, 0:T], [pk], ["qT"])
            elif n == 4:
                cx.act(kT[:, off:off + T], pm[:, 0:T], AF.Copy, [pk], [("kT", ti)], scale=0.125)
            else:
                cx.copy("act", proj[:, n, 0:T], pm[:, 0:T], [pk], [("proj", n)])

        for m, (bs, bl) in enumerate(blocks):
            pm = M[m % 2]
            pk = "M%d" % (m % 2)
            for c in range(NCK):
                cx.mm(pm[0:bl, 0:384], hT[:, c, bs:bs + bl], w_bf[:, c, 1024:1408], c == 0, c == NCK - 1,
                      hTk + wk, [pk])
            cx.copy("act", itok[0:bl, m, :], pm[0:bl, 0:128], [pk], [("itok", m)])
            cx.copy("dve", pv[0:bl, 1 + m, :], pm[0:bl, 128:256], [pk], [("pv", 1 + m)])
            cx.copy("act", vc[0:bl, gb0 + m, :], pm[0:bl, 256:384], [pk], [("vc", gb0 + m)])


        if "conv" in PARTS:
            cx.tt("pool", ubuf[:, 2:2 + T], proj[:, 7, 0:T], proj[:, 5, 0:T], ALU.mult, [("proj", 7), ("proj", 5)], ["ubuf"])
            cx.ts("pool", t2[:, 0:T], ubuf[:, 2:2 + T], vec[:, 4:5], ALU.mult, ["ubuf", "vec"], ["t2"])
            cx.stt("dve", t2[:, 0:T], ubuf[:, 1:1 + T], vec[:, 3:4], t2[:, 0:T], ALU.mult, ALU.add, ["ubuf", "vec", "t2"], ["t2"])
            cx.stt("dve", t2[:, 0:T], ubuf[:, 0:T], vec[:, 2:3], t2[:, 0:T], ALU.mult, ALU.add, ["ubuf", "vec", "t2"], ["t2"])
            cx.tt("pool", bo[:, 3, 0:T], t2[:, 0:T], proj[:, 6, 0:T], ALU.mult, ["t2", ("proj", 6)], [bok])
            cx.copy("pool", ubuf[:, 0:2], ubuf[:, T:T + 2], ["ubuf"], ["ubuf"])

        if "pool" in PARTS:
            for m, (bs, bl) in enumerate(blocks):
                pm = M[m % 2]
                pk = "M%d" % (m % 2)
                if meta:
                    cx.mm(pm[:, 0:16], pv[0:16, 1, :], C("bmeta", 16, 16), True, True, [("pv", 1), "cst_b"], [pk])
                    cx.mm(pm[:, 16:32], pv[0:16, 1, :], C("ident", 16, 16), True, True, [("pv", 1), "cst_b"], [pk])
                    cx.tt("dve", uTf[:, 0:16], pm[:, 0:16], C("invn", 128, 16, bf=False), ALU.mult, [pk, "cst_f"], ["uTf"])
                    cx.tt("dve", uTb[:, 0:16], uTf[:, 0:16], pm[:, 16:32], ALU.subtract, [pk, "uTf"], ["uTb"])
                else:
                    prev_rows = 16 if (ti == 1 and m == 0) else 128
                    moff = C("mofff", 16, 128) if (ti == 1 and m == 0) else C("moff")
                    cx.mm(pm[:, 0:128], pv[:, 1 + m, :], C("mdiag"), True, False, [("pv", 1 + m), "cst_b"], [pk])
                    cx.mm(pm[:, 0:128], pv[0:prev_rows, m, :], moff, False, True, [("pv", m), "cst_b"], [pk])
                    cx.copy("dve", uTb[:, 0:bl], pm[:, 0:bl], [pk], ["uTb"])
                cx.mm(pm[:, 128:128 + bl], pw_b[:], uTb[:, 0:bl], True, True, ["uTb", "pw_b"], [pk])
                cx.act(bo[:, 1, bs:bs + bl], pm[:, 128:128 + bl], AF.Copy, [pk, "vec"], [bok], scale=vec[:, 1:2])
            lastm = len(blocks) - 1
            lbl_rows = blocks[lastm][1]
            cx.copy("pool", pv[0:lbl_rows, 0, :], pv[0:lbl_rows, 1 + lastm, :], [("pv", 1 + lastm), ("pv", 0)], [("pv", 0)])

        if "hgrn" in PARTS:
            cx.act(t1[:, 0:T], proj[:, 1, 0:T], AF.Exp, [("proj", 1)], ["t1"], scale=-1.0)
            cx.ts("dve", t1[:, 0:T], t1[:, 0:T], 1.0, ALU.add, ["t1"], ["t1"])
            cx.recip(t1[:, 0:T], t1[:, 0:T], ["t1"], ["t1"])
            cx.ts("dve", fval[:, 0:T], t1[:, 0:T], oml_ap, ALU.mult, ["t1", "sm"], ["fval"], s2=lb_ap, op1=ALU.add)
            cx.act(lf[:, 0:T], fval[:, 0:T], AF.Ln, ["fval"], ["lf"])
            cx.ts("pool", kk[:, 0:T], fval[:, 0:T], -1.0, ALU.mult, ["fval"], ["kk"], s2=1.0, op1=ALU.add)
            P.op("dve", lambda e, T=T: e.tensor_tensor_scan(out=Bc[:, 1:1 + T], data0=onesf[:, 0:T], data1=lf[:, 0:T],
                                                            initial=0.0, op0=ALU.mult, op1=ALU.add),
                 reads=["lf", "onesf"], writes=["Bc"])
            CL = 16 if meta else 64
            nch = T // CL
            Bv = Bc[:, 1:1 + T].rearrange("p (c l) -> p c l", l=CL)
            Bp = Bc[:, 0:T].rearrange("p (c l) -> p c l", l=CL)[:, :, 0:1]
            Bm = Bv[:, :, CL // 2 - 1:CL // 2]
            Be = Bv[:, :, CL - 1:CL]
            v3 = lambda t_: t_[:, 0:T].rearrange("p (c l) -> p c l", l=CL)
            cx.tt("dve", dsm[:, 2, 0:nch].rearrange("p (c o) -> p c o", o=1), Bm, Bp, ALU.subtract, ["Bc"], ["dsm"])
            cx.tt("dve", dsm[:, 3, 0:nch].rearrange("p (c o) -> p c o", o=1), Be, Bp, ALU.subtract, ["Bc"], ["dsm"])
            cx.act(dsm[:, 0:2, 0:nch], dsm[:, 2:4, 0:nch], AF.Exp, ["dsm"], ["dsm"])
            cx.tt("dve", v3(lf), Bv, Bm.to_broadcast([128, nch, CL]), ALU.subtract, ["Bc", "lf"], ["lf"])
            cx.tt("dve", v3(fval), Bv, Be.to_broadcast([128, nch, CL]), ALU.subtract, ["Bc", "fval", "kk"], ["fval"])
            cx.act(t1[:, 0:T], lf[:, 0:T], AF.Exp, ["lf"], ["t1"])
            cx.tt("dve", qe[:, 0:T], t1[:, 0:T], proj[:, 0, 0:T], ALU.mult, ["t1", ("proj", 0)], ["qe"])
            cx.act(t2[:, 0:T], lf[:, 0:T], AF.Exp, ["lf"], ["t2"], scale=-1.0)
            cx.tt("pool", ke[:, 0:T], t2[:, 0:T], kk[:, 0:T], ALU.mult, ["t2", "kk"], ["ke"])
            cx.act(t1[:, 0:T], fval[:, 0:T], AF.Exp, ["fval"], ["t1"], scale=-1.0)
            cx.tt("pool", kdT[:, 0:T], t1[:, 0:T], kk[:, 0:T], ALU.mult, ["t1", "kk"], ["kdT"])
            for ci in range(nch):
                c0 = ci * CL
                m = c0 // 128
                r0 = c0 % 128
                cx.tr(TB[r0:r0 + CL, 0:128], kdT[:, c0:c0 + CL], C("ident"), ["kdT", "cst_b"], ["TB"])
                cx.copy("act", kdtok[r0:r0 + CL, m, :], TB[r0:r0 + CL, 0:128], ["TB"], ["kdtok"])
                zb = Z[ci % 2]
                zk = "Z%d" % (ci % 2)
                cx.mm(zb[r0:r0 + CL, 0:CL], ke[:, c0:c0 + CL], qe[:, c0:c0 + CL], True, True, ["ke", "qe"], [zk])
                cx.tt("dve", scm[r0:r0 + CL, 0:CL], zb[r0:r0 + CL, 0:CL], C("triu", 128, 128, bf=False)[r0:r0 + CL, r0:r0 + CL],
                      ALU.mult, [zk, "cst_f"], ["scm"])
                cx.ts("dve", Ssc[:], Sst[:], dsm[:, 0, ci:ci + 1], ALU.mult, ["S", "dsm"], ["Ssc"])
                cx.mm(G[:, c0:c0 + CL], Ssc[:], qe[:, c0:c0 + CL], True, False, ["Ssc", "qe"], ["G"])
                cx.mm(G[:, c0:c0 + CL], itok[r0:r0 + CL, m, :], scm[r0:r0 + CL, 0:CL], False, True, [("itok", m), "scm"], ["G"])
                mb = M[ci % 2]
                mk = "M%d" % (ci % 2)
                cx.mm(mb[:, 0:128], kdtok[r0:r0 + CL, m, :], itok[r0:r0 + CL, m, :], True, True, ["kdtok", ("itok", m)], [mk])
                cx.stt("dve", Sst[:], Sst[:], dsm[:, 1, ci:ci + 1], mb[:, 0:128], ALU.mult, ALU.add, ["S", "dsm", mk], ["S"])
            cx.copy("act", osb[:, 0:T], G[:, 0:T], ["G"], ["osb"])
            cx.act(osq[:, 0:T], osb[:, 0:T], AF.Square, ["osb"], ["osq"])
            cx.mm(CS[:, 0:T], C("ones"), osq[:, 0:T], True, True, ["osq", "cst_b"], ["CS"])
            cx.act(t2[:, 0:T], CS[:, 0:T], AF.Ln, ["CS"], ["t2"], bias=EPS, scale=1.0 / 128)
            cx.act(t2[:, 0:T], t2[:, 0:T], AF.Exp, ["t2"], ["t2"], scale=-0.5)
            cx.act(t1[:, 0:T], proj[:, 2, 0:T], AF.Exp, [("proj", 2)], ["t1"], scale=-1.0)
            cx.ts("dve", t1[:, 0:T], t1[:, 0:T], 1.0, ALU.add, ["t1"], ["t1"])
            cx.recip(t1[:, 0:T], t1[:, 0:T], ["t1"], ["t1"])
            cx.stt("dve", osb[:, 0:T], osb[:, 0:T], vec[:, 0:1], t2[:, 0:T], ALU.mult, ALU.mult, ["osb", "vec", "t2"], ["osb"])
            cx.tt("dve", bo[:, 0, 0:T], osb[:, 0:T], t1[:, 0:T], ALU.mult, ["osb", "t1"], [bok])

        if "attn" in PARTS:
            nsub = len(blocks)
            SW = blocks[0][1]
            last_gb = gb0 + nsub - 1
            cx.mm(O[:, 0:T], zbf[:, 0:128], zbf[:, 0:T], True, False, ["zbf"], ["O"])
            its = list(range(last_gb, -1, -1))
            n_it = len(its)
            for h in range(2):
                cx.memset("pool", carry[h][:, 0:T], 0.0, [("carry", h)])
            Zb = [[(Z[0], "Z0"), (G, "G")], [(Z[1], "Z1"), (M[0], "M0")]]
            CSb = [(CS, "CS"), (M[1], "M1")]

            def geom(i):
                kb = its[i]
                ks, KL = gblock(kb)
                kti = 0 if kb == 0 else (kb - 1) // 4 + 1
                diag = ks >= off
                qc0 = max(off, ks) - off
                return kb, ks, KL, kti, diag, qc0, T - qc0

            def phA1(i, h):
                kb, ks, KL, kti, diag, qc0, N = geom(i)
                hp = slice(64 * h, 64 * h + 64)
                zb, zk = Zb[h][i % 2]
                cx.mm(zb[0:KL, 0:N], kT[hp, ks:ks + KL], qT[hp, qc0:qc0 + N], True, True, [("kT", kti), "qT"], [zk])

            def phA(i, h):
                kb, ks, KL, kti, diag, qc0, N = geom(i)
                zb, zk = Zb[h][i % 2]
                ef, efk = Ef[h], ("Ef", h)
                cx.act(ef[0:KL, 0:N], zb[0:KL, 0:N], AF.Exp, [zk], [efk])

            def phA_both(i):
                kb, ks, KL, kti, diag, qc0, N = geom(i)
                cx.act(h3(spb2)[0:KL, :, 0:N], h3(Ef2)[0:KL, :, 0:N], AF.Ln, [("Ef", 0), ("Ef", 1)],
                       [("spb", 0), ("spb", 1)], bias=1.0)
                if diag:
                    for h in range(2):
                        sp, spk = spb[h], ("spb", h)
                        cx.tt("pool", sp[0:KL, 0:SW], sp[0:KL, 0:SW], C("strict", KL, SW), ALU.mult, [spk, "cst_b"], [spk])

            def phC_both(i):
                kb, ks, KL, kti, diag, qc0, N = geom(i)
                cx.act(h3(Ab2)[0:KL, :, 0:N], h3(lg2)[0:KL, :, 0:N], AF.Exp, [("lg", 0), ("lg", 1)],
                       [("Ab", 0), ("Ab", 1)])

            def phB(i, h):
                kb, ks, KL, kti, diag, qc0, N = geom(i)
                sp, spk = spb[h], ("spb", h)
                lgt, lgk = lg[h], ("lg", h)
                cr, crk = carry[h], ("carry", h)
                gb, gk = Zb[h][i % 2]
                cb, ck_ = CSb[h]
                cx.mm(gb[0:KL, 0:N], C("negun", KL, KL), sp[0:KL, 0:N], False, True, [spk, "cst_b"], [gk], skip=True)
                if kb > 0:
                    cx.mm(cb[:, 0:N], C("ones", KL, 128), sp[0:KL, 0:N], True, True, [spk, "cst_b"], [ck_])
                cx.tt("dve", lgt[0:KL, 0:N], gb[0:KL, 0:N], cr[0:KL, qc0:qc0 + N], ALU.subtract, [gk, crk], [lgk])
                if kb > 0:
                    cx.tt("dve", cr[:, qc0:qc0 + N], cr[:, qc0:qc0 + N], cb[:, 0:N], ALU.add, [ck_, crk], [crk])

            def phC(i, h):
                kb, ks, KL, kti, diag, qc0, N = geom(i)
                hp = slice(64 * h, 64 * h + 64)
                lgt, lgk = lg[h], ("lg", h)
                ab, abk = Ab[h], ("Ab", h)
                if diag:
                    cx.tt("pool", ab[0:KL, 0:SW], ab[0:KL, 0:SW], C("strict", KL, SW), ALU.mult, [abk, "cst_b"], [abk])
                cx.mm(O[hp, qc0:qc0 + N], vc[0:KL, kb, hp], ab[0:KL, 0:N], False, (kb == 0),
                      [abk, ("vc", kb)], ["O"])

            for t in range(n_it + 2):
                for h in range(2):
                    if t < n_it:
                        phA1(t, h)
                if 0 <= t - 2 < n_it:
                    phC_both(t - 2)
                for h in range(2):
                    if 0 <= t - 2 < n_it:
                        phC(t - 2, h)
                for h in range(2):
                    if 0 <= t - 1 < n_it:
                        phB(t - 1, h)
                for h in range(2):
                    if t < n_it:
                        phA(t, h)
                if t < n_it:
                    phA_both(t)
            cx.copy("dve", bo[:, 2, 0:T], O[:, 0:T], ["O"], [bok])

        cx.dma("sp", out_ap(off, T).rearrange("(n p) t -> p n t", p=128), bo[:, :, 0:T], [bok], [("brout_d", ti)],
               sk("bo%d" % (ti % 2)), is_out=True)
        if after_store is not None:
            after_store(cx, ti)


def emit_C(cx, layer, halves, x_in, br_in, g1_ap, g2_ap, gf_ap, cst_ap, wg_ap, wb_ap, wo_ap, wu_ap, wd_ap,
           x_out, final, pfx="", br_eng="sp", after_xout=None):
    P = cx.P
    k = lambda s: pfx + s
    sk = lambda s: "C." + s
    HT = max(sum(T for _, T in h) for h in halves)
    xh = cx.sb(k("xh"), [128, NCK, HT], F32)
    hh = cx.sb(k("hh"), [128, NCK, HT], BF16)
    big = cx.sb(k("big"), [128, 32, HT], BF16)
    mixb = cx.sb(k("mixb"), [128, NCK, HT], BF16)
    macc = cx.sb(k("macc"), [128, HT], F32)
    g1 = cx.sb(k("cg1"), [128, 8], F32)
    g2 = cx.sb(k("cg2"), [128, 8], F32)
    gf = cx.sb(k("cgf"), [128, 8], F32)
    ones_b = cx.sb(k("cones"), [128, 128], BF16)
    sqb = cx.sb(k("csqb"), [128, NCK, TT], BF16)
    rstd = cx.sb(k("crstd"), [128, TT], F32)
    gt = [cx.sb(k("gt%d" % i), [128, TT], F32) for i in range(2)]
    tmp = [cx.sb(k("ctmp%d" % i), [128, TT], F32) for i in range(2)]
    wg = [cx.sb(k("wg%d" % i), [128, NCK * 128], BF16) for i in range(3)]
    wb = [cx.sb(k("wb%d" % i), [128, 4 * 128], BF16) for i in range(3)]
    wd = [cx.sb(k("wd%d" % i), [128, 32 * 128], BF16) for i in range(2)]
    banks = [cx.ps(k("cp%d" % i), [128, TT]) for i in range(8)]
    rr = [0]

    def bank():
        i = rr[0] % 8
        rr[0] += 1
        return banks[i], "cp%d" % i

    ci = CONST_NAMES.index("ones")
    cx.dma("pool", ones_b[:], cst_ap[:, ci * 128:(ci + 1) * 128], [], ["cones"], sk("cones"))
    cx.dma("sp", g1[:], g1_ap, [], ["cg1"], sk("cg1"))
    cx.dma("sp", g2[:], g2_ap, [], ["cg2"], sk("cg2"))
    cx.dma("sp", gf[:], gf_ap, [], ["cgf"], sk("cgf"))
    wcnt = {"wg": 0, "wb": 0, "wd": 0}
    bigk = [("big", i) for i in range(4)]

    def loadw(kind, bufs, ap, n):
        i = wcnt[kind] % len(bufs)
        wcnt[kind] += 1
        cx.dma("pool", bufs[i][:, 0:n], ap, [], [(kind, i)], sk("%s%d" % (kind, i)))
        return bufs[i], (kind, i)

    def rms(tl, lo, g, T, outf):
        cx.act(sqb[:, :, 0:T], xh[:, :, lo:lo + T], AF.Square, ["xh"], ["csqb"])
        pb, pk = bank()
        for c in range(NCK):
            cx.mm(pb[:, 0:T], ones_b[:], sqb[:, c, 0:T], c == 0, c == NCK - 1, ["csqb", "cones"], [pk])
        cx.act(rstd[:, 0:T], pb[:, 0:T], AF.Ln, [pk], ["crstd"], bias=EPS, scale=1.0 / D)
        cx.act(rstd[:, 0:T], rstd[:, 0:T], AF.Exp, ["crstd"], ["crstd"], scale=-0.5)
        for c in range(NCK):
            o, ok = outf(c)
            cx.stt("dve", o, xh[:, c, lo:lo + T], g[:, c:c + 1], rstd[:, 0:T],
                   ALU.mult, ALU.mult, ["xh", "crstd", "cg1", "cg2", "cgf"], [ok])

    for hi, tiles in enumerate(halves):
        lo = 0
        ltiles = []
        for (off, T) in tiles:
            ltiles.append((lo, off, T))
            lo += T
        for ti2, (lo, off, T) in enumerate(ltiles):
            cx.dma("sp", xh[:, :, lo:lo + T], x_in(off, T).rearrange("(c p) t -> p c t", p=128), ["xint"], ["xh"], sk("xh"))
            cx.dma(br_eng, big[:, 0:16, lo:lo + T], br_in(off, T), ["brg"], [("big", ti2 % 4)], sk("big%d" % (ti2 % 4)))
        for (lo, off, T) in ltiles:
            rms(None, lo, g1, T, lambda c, lo=lo, T=T: (hh[:, c, lo:lo + T], "hh"))
        for dc in range(NCK):
            for n in range(4):
                wgb, wgk = loadw("wg", wg, wg_ap[dc, n], NCK * 128)
                wbb, wbk = loadw("wb", wb, wb_ap[dc, n], 4 * 128)
                for ti, (lo, off, T) in enumerate(ltiles):
                    pa, pak = bank()
                    pbk_ = bank()
                    pb, pbk = pbk_
                    for c in range(NCK):
                        cx.mm(pa[:, 0:T], wgb[:, c * 128:(c + 1) * 128], hh[:, c, lo:lo + T], c == 0, c == NCK - 1,
                              [wgk, "hh"], [pak])
                    for j in range(4):
                        cx.mm(pb[:, 0:T], wbb[:, j * 128:(j + 1) * 128], big[:, j * 4 + n, lo:lo + T], j == 0, j == 3,
                              [wbk] + bigk, [pbk])
                    g_, gk = gt[ti % 2], ("gt", ti % 2)
                    t_, tk = tmp[ti % 2], ("ctmp", ti % 2)
                    cx.act(g_[:, 0:T], pa[:, 0:T], AF.Sigmoid, [pak], [gk])
                    if n == 0:
                        cx.tt("dve", macc[:, lo:lo + T], g_[:, 0:T], pb[:, 0:T], ALU.mult, [gk, pbk], [("macc", ti)])
                    else:
                        cx.tt("dve", t_[:, 0:T], g_[:, 0:T], pb[:, 0:T], ALU.mult, [gk, pbk], [tk])
                        if n < 3:
                            cx.tt("dve", macc[:, lo:lo + T], macc[:, lo:lo + T], t_[:, 0:T], ALU.add,
                                  [tk, ("macc", ti)], [("macc", ti)])
                        else:
                            cx.tt("dve", mixb[:, dc, lo:lo + T], macc[:, lo:lo + T], t_[:, 0:T], ALU.add,
                                  [tk, ("macc", ti)], [("mixb", dc)])
        mixk = [("mixb", c) for c in range(NCK)]
        for dc in range(NCK):
            wgb, wgk = loadw("wg", wg, wo_ap[dc], NCK * 128)
            for ti, (lo, off, T) in enumerate(ltiles):
                pa, pak = bank()
                for c in range(NCK):
                    cx.mm(pa[:, 0:T], wgb[:, c * 128:(c + 1) * 128], mixb[:, c, lo:lo + T], c == 0, c == NCK - 1,
                          [wgk] + mixk, [pak])
                cx.tt("dve", xh[:, dc, lo:lo + T], xh[:, dc, lo:lo + T], pa[:, 0:T], ALU.add, [pak, "xh"], ["xh"])
        for (lo, off, T) in ltiles:
            rms(None, lo, g2, T, lambda c, lo=lo, T=T: (hh[:, c, lo:lo + T], "hh"))
        for f in range(32):
            wgb, wgk = loadw("wg", wg, wu_ap[f], NCK * 128)
            for ti, (lo, off, T) in enumerate(ltiles):
                pa, pak = bank()
                for c in range(NCK):
                    cx.mm(pa[:, 0:T], wgb[:, c * 128:(c + 1) * 128], hh[:, c, lo:lo + T], c == 0, c == NCK - 1,
                          [wgk, "hh"], [pak])
                g_, gk = gt[ti % 2], ("gt", ti % 2)
                cx.act(g_[:, 0:T], pa[:, 0:T], AF.Relu, [pak], [gk])
                cx.tt("dve", big[:, f, lo:lo + T], g_[:, 0:T], g_[:, 0:T], ALU.mult, [gk], bigk)
        for dc in range(NCK):
            wdb, wdk = loadw("wd", wd, wd_ap[dc], 32 * 128)
            for ti, (lo, off, T) in enumerate(ltiles):
                pa, pak = bank()
                for f in range(32):
                    cx.mm(pa[:, 0:T], wdb[:, f * 128:(f + 1) * 128], big[:, f, lo:lo + T], f == 0, f == 31,
                          [wdk] + bigk, [pak])
                cx.tt("dve", xh[:, dc, lo:lo + T], xh[:, dc, lo:lo + T], pa[:, 0:T], ALU.add, [pak, "xh"], ["xh"])
        for (lo, off, T) in ltiles:
            if final:
                rms(None, lo, gf, T, lambda c, lo=lo, T=T: (xh[:, c, lo:lo + T], "xh"))
            cx.dma("sp", x_out(off, T).rearrange("(c p) t -> p c t", p=128), xh[:, :, lo:lo + T], ["xh"],
                   [("xout", off)], sk("xo"), is_out=True)
            if after_xout is not None:
                after_xout(cx, off)


FM_SPLITS = [0, 1, 3, 5, 6, 8, 9, 10]
TM_SPLITS = [2, 4, 7]


def prep_B_inputs(layer, j, w_in, norm1_g, lb_logits, hg_norm_g, pool_w, pool_scale, conv_w):
    cols = []
    for sidx in FM_SPLITS + TM_SPLITS:
        cols.append(w_in[layer][:, sidx * 512 + j * 128: sidx * 512 + (j + 1) * 128])
    w = np.ascontiguousarray(np.concatenate(cols, axis=1), dtype=np.float32)
    g1 = np.ascontiguousarray(norm1_g[layer].reshape(NCK, 128).T, dtype=np.float32)
    lbl = np.ascontiguousarray(lb_logits[:, j * 128:(j + 1) * 128].T, dtype=np.float32)
    vec = np.zeros((128, 8), np.float32)
    sl = slice(j * 128, (j + 1) * 128)
    vec[:, 0] = hg_norm_g[layer, sl]
    vec[:, 1] = pool_scale[layer, sl]
    vec[:, 2] = conv_w[layer, 0, sl]
    vec[:, 3] = conv_w[layer, 1, sl]
    vec[:, 4] = conv_w[layer, 2, sl]
    pw = np.ascontiguousarray(pool_w[layer, j], dtype=np.float32)
    _, cst = host_consts(POOL_WINDOWS[j])
    return {"w": w, "g1": g1, "lbl": lbl, "vec": vec, "pw": pw, "cst": np.ascontiguousarray(cst)}


def build_B(S, layer):
    nc = bass.Bass("TRN2", target_bir_lowering=False)
    Ltot = NMETA + S
    xT = nc.dram_tensor("xT", [D, Ltot], F32, kind="ExternalInput").ap()
    w = nc.dram_tensor("w", [D, 11 * 128], F32, kind="ExternalInput").ap()
    g1 = nc.dram_tensor("g1", [128, 8], F32, kind="ExternalInput").ap()
    lbl = nc.dram_tensor("lbl", [128, 4], F32, kind="ExternalInput").ap()
    vec = nc.dram_tensor("vec", [128, 8], F32, kind="ExternalInput").ap()
    pw = nc.dram_tensor("pw", [128, 128], F32, kind="ExternalInput").ap()
    cst = nc.dram_tensor("cst", [128, 10 * 128], F32, kind="ExternalInput").ap()
    br = nc.dram_tensor("br", [512, Ltot], BF16, kind="ExternalOutput").ap()
    P = Prog(nc)
    with ExitStack() as es:
        cx = Ctx(nc, P, es)
        emit_B(cx, S, layer, lambda off, T: xT[:, off:off + T], w, g1, lbl, vec, pw, cst,
               lambda off, T: br[:, off:off + T], pfx="b_")
        P.emit()
    return nc


def prep_C_weights(layer, w_in, w_branch, w_o, w_up, w_down):
    wg = w_in[layer][:, 11 * 512:].reshape(NCK, 128, 4, NCK, 128)
    wg = np.ascontiguousarray(wg.transpose(3, 2, 1, 0, 4)).reshape(NCK, 4, 128, NCK * 128)
    wb = w_branch[layer].reshape(4, 4, 128, NCK, 128)
    wb = np.ascontiguousarray(wb.transpose(3, 0, 2, 1, 4)).reshape(NCK, 4, 128, 4 * 128)
    wo = w_o[layer].reshape(NCK, 128, NCK, 128)
    wo = np.ascontiguousarray(wo.transpose(2, 1, 0, 3)).reshape(NCK, 128, NCK * 128)
    wu = w_up[layer].reshape(NCK, 128, 32, 128)
    wu = np.ascontiguousarray(wu.transpose(2, 1, 0, 3)).reshape(32, 128, NCK * 128)
    wd = w_down[layer].reshape(32, 128, NCK, 128)
    wd = np.ascontiguousarray(wd.transpose(2, 1, 0, 3)).reshape(NCK, 128, 32 * 128)
    return {"wg": wg, "wb": wb, "wo": wo, "wu": wu, "wd": wd}


def c_halves(ntok_x):
    tiles = [(0, NMETA)] + [(NMETA + TT * i, TT) for i in range(ntok_x // TT)]
    nh = (len(tiles) + 1) // 2
    return [tiles[:nh], tiles[nh:]] if len(tiles) > nh else [tiles]


def build_C(ntok_x, layer, final):
    nc = bass.Bass("TRN2", target_bir_lowering=False)
    NT = NMETA + ntok_x
    x = nc.dram_tensor("x", [D, NT], F32, kind="ExternalInput").ap()
    br = nc.dram_tensor("brc", [16, 128, NT], BF16, kind="ExternalInput").ap()
    g1 = nc.dram_tensor("g1", [128, 8], F32, kind="ExternalInput").ap()
    g2 = nc.dram_tensor("g2", [128, 8], F32, kind="ExternalInput").ap()
    gf = nc.dram_tensor("gf", [128, 8], F32, kind="ExternalInput").ap()
    cst = nc.dram_tensor("cst", [128, 10 * 128], F32, kind="ExternalInput").ap()
    wg = nc.dram_tensor("wg", [NCK, 4, 128, NCK * 128], F32, kind="ExternalInput").ap()
    wb = nc.dram_tensor("wb", [NCK, 4, 128, 4 * 128], F32, kind="ExternalInput").ap()
    wo = nc.dram_tensor("wo", [NCK, 128, NCK * 128], F32, kind="ExternalInput").ap()
    wu = nc.dram_tensor("wu", [32, 128, NCK * 128], F32, kind="ExternalInput").ap()
    wd = nc.dram_tensor("wd", [NCK, 128, 32 * 128], F32, kind="ExternalInput").ap()
    xo = nc.dram_tensor("xo", [D, NT], F32, kind="ExternalOutput").ap()
    P = Prog(nc)
    with ExitStack() as es:
        cx = Ctx(nc, P, es)
        emit_C(cx, layer, c_halves(ntok_x), lambda off, T: x[:, off:off + T],
               lambda off, T: br[:, :, off:off + T].rearrange("c p t -> p c t"), g1, g2, gf, cst, wg, wb, wo, wu, wd,
               lambda off, T: xo[:, off:off + T], final, pfx="c_")
        P.emit()
    return nc


def _r8(v):
    return np.ascontiguousarray(np.asarray(v, np.float32).reshape(NCK, 128).T)


def kernel_unfused(x, meta_tokens, lb_logits, norm1_g, w_in, hg_norm_g, pool_w, pool_scale, conv_w,
           w_branch, w_o, norm2_g, w_up, w_down, final_norm_g):
    f = lambda a: np.asarray(a, dtype=np.float32)
    x, meta_tokens, lb_logits, norm1_g, w_in = f(x), f(meta_tokens), f(lb_logits), f(norm1_g), f(w_in)
    hg_norm_g, pool_w, pool_scale, conv_w = f(hg_norm_g), f(pool_w), f(pool_scale), f(conv_w)
    w_branch, w_o, norm2_g, w_up, w_down, final_norm_g = f(w_branch), f(w_o), f(norm2_g), f(w_up), f(w_down), f(final_norm_g)
    B_, S, _ = x.shape
    NQ = 8 // B_
    SQ = S // NQ
    cores = list(range(8))
    xT = [np.ascontiguousarray(np.concatenate([meta_tokens, x[b]], axis=0).T) for b in range(B_)]
    cst2 = np.ascontiguousarray(host_consts(2)[1])
    for layer in range(DEPTH):
        ncB = build_B(S, layer)
        in_maps = []
        for r in cores:
            b, j = r // NQ, r % NQ
            im = prep_B_inputs(layer, j, w_in, norm1_g, lb_logits, hg_norm_g, pool_w, pool_scale, conv_w)
            im["xT"] = xT[b]
            in_maps.append(im)
        resB = run_bass_kernel_spmd(ncB, in_maps, core_ids=cores).results
        brs = [np.asarray(resB[r]["br"]) for r in cores]
        final = layer == DEPTH - 1
        ncC = build_C(SQ, layer, final)
        wts = prep_C_weights(layer, w_in, w_branch, w_o, w_up, w_down)
        g1, g2, gf = _r8(norm1_g[layer]), _r8(norm2_g[layer]), _r8(final_norm_g)
        in_maps = []
        for r in cores:
            b, q = r // NQ, r % NQ
            cols = np.concatenate([np.arange(NMETA), NMETA + q * SQ + np.arange(SQ)])
            im = dict(wts)
            im["x"] = np.ascontiguousarray(xT[b][:, cols])
            brc = np.empty((16, 128, NMETA + SQ), dtype=brs[0].dtype)
            for n in range(4):
                for j in range(4):
                    brc[j * 4 + n] = brs[b * NQ + j][n * 128:(n + 1) * 128][:, cols]
            im["brc"] = brc
            im["g1"], im["g2"], im["gf"], im["cst"] = g1, g2, gf, cst2
            in_maps.append(im)
        resC = run_bass_kernel_spmd(ncC, in_maps, core_ids=cores).results
        for b in range(B_):
            new = np.empty_like(xT[b])
            new[:, 0:NMETA] = np.asarray(resC[b * NQ]["xo"])[:, 0:NMETA]
            for q in range(NQ):
                new[:, NMETA + q * SQ: NMETA + (q + 1) * SQ] = np.asarray(resC[b * NQ + q]["xo"])[:, NMETA:]
            xT[b] = new
    out = np.stack([np.ascontiguousarray(xT[b][:, NMETA:].T) for b in range(B_)], axis=0)
    return out.astype(np.float32)


I32 = mybir.dt.int32
GROUPS = [[0, 1, 2, 3], [4, 5, 6, 7]]


def build_fused(S, depth=DEPTH):
    nc = bass.Bass("TRN2", target_bir_lowering=False)
    NQ = 4
    SQ = S // NQ
    NK = SQ // TT
    NT = NMETA + SQ
    Ltot = NMETA + S
    NTB = S // TT
    ext = lambda name, shape, dt=F32: nc.dram_tensor(name, shape, dt, kind="ExternalInput").ap()
    x0 = ext("x0", [D, NT])
    qcol = ext("qcol", [1, 8], I32)
    lbl = ext("lbl", [128, 4])
    cstB = ext("cstB", [128, 10 * 128])
    gf = ext("gf", [128, 8])
    L = []
    for l in range(depth):
        L.append(dict(
            wB=ext("wB%d" % l, [D, 11 * 128]), g1=ext("g1_%d" % l, [128, 8]), vec=ext("vec%d" % l, [128, 8]),
            pw=ext("pw%d" % l, [128, 128]), g2=ext("g2_%d" % l, [128, 8]),
            wg=ext("wg%d" % l, [NCK, 4, 128, NCK * 128]), wb=ext("wb%d" % l, [NCK, 4, 128, 4 * 128]),
            wo=ext("wo%d" % l, [NCK, 128, NCK * 128]), wu=ext("wu%d" % l, [32, 128, NCK * 128]),
            wd=ext("wd%d" % l, [NCK, 128, 32 * 128])))
    xo = nc.dram_tensor("xo", [D, NT], F32, kind="ExternalOutput").ap()
    xm = nc.dram_tensor("xm_i", [D, NMETA], F32).ap()
    xx = nc.dram_tensor("xx_i", [NK, 2, 512, TT], F32).ap()
    xg = nc.dram_tensor("xg_i", [NK, 2, NQ * 512, TT], F32).ap()
    brm = nc.dram_tensor("brm_i", [512, NMETA], BF16).ap()
    brx = nc.dram_tensor("brx_i", [NTB, 512, TT], BF16).ap()
    brgm = nc.dram_tensor("brgm_i", [NQ * 512, NMETA], BF16).ap()
    brgx = nc.dram_tensor("brgx_i", [NTB, NQ * 512, TT], BF16).ap()

    def ag(P, src, dst, r, w, key):
        P.op("pool", lambda e: e.collective_compute("AllGather", ALU.bypass, replica_groups=GROUPS,
                                                    ins=[src.opt()], outs=[dst.opt()]),
             reads=r, writes=w, dma_key=key, inc=1)

    def x_tile_ap(k_):
        return xx[k_].rearrange("h r t -> (h r) t")

    def x_in(off, T):
        return xm if off == 0 else x_tile_ap((off - NMETA) // TT)

    def x_loader(cx, ti, off, T, buf, bkey, semkey):
        if ti == 0:
            cx.dma("sp", buf[:, :, 0:T], xm.rearrange("(c p) t -> p c t", p=128), ["xm"], [bkey], semkey)
            return
        g = (ti - 1) * TT
        q, k_ = g // SQ, (g % SQ) // TT
        for h in range(2):
            src = xg[k_, h, q * 512:(q + 1) * 512, :].rearrange("(c p) t -> p c t", p=128)
            cx.dma("sp", buf[:, h * 4:(h + 1) * 4, 0:T], src, [("xg", k_, h)], [bkey], semkey)

    def br_out(off, T):
        return brm if off == 0 else brx[(off - NMETA) // TT]

    halves = c_halves(SQ)
    vals = {}
    es_glob = ExitStack()
    state = None
    for l in range(depth):
        P = Prog(nc, state)
        state = P.state
        with ExitStack() as es:
            cx = Ctx(nc, P, es)
            if l == 0:
                cx.dma("sp", xm, x0[:, 0:NMETA], [], ["xm"], "cpm")
                for k_ in range(NK):
                    cx.dma("sp", x_tile_ap(k_), x0[:, NMETA + k_ * TT:NMETA + (k_ + 1) * TT], [], [("xx", k_)], "cpx%d" % (k_ % 2))
            if l == 0:
                for k_ in range(NK):
                    for h in range(2):
                        ag(P, xx[k_, h], xg[k_, h], [("xx", k_)], [("xg", k_, h)], "ccx")

            def after_store(cx_, ti):
                if ti == 0:
                    ag(cx_.P, brm, brgm, [("brout_d", 0)], [("brg", 0)], "ccb")
                else:
                    ag(cx_.P, brx[ti - 1], brgx[ti - 1], [("brout_d", ti)], [("brg", ti)], "ccb")

            emit_B(cx, S, l, None, L[l]["wB"], L[l]["g1"], lbl, L[l]["vec"], L[l]["pw"], cstB, br_out,
                   pfx="b%d_" % l, x_loader=x_loader, after_store=after_store)
            P.op("sp", lambda e: e.dma_start(out=brm[0:1, 0:2], in_=brm[0:1, 0:2]),
                 reads=[("brg", t_) for t_ in range(NTB + 1)], writes=[], dma_key="fin", is_out=True)
            P.emit()
        P = Prog(nc, state)

        br_eng = "sp" if l < 2 else "act"

        def init_eng(handle, es2, br_eng=br_eng):
            if br_eng in vals:
                return
            vals[br_eng] = {}
            for kk in range(NK):
                reg = es_glob.enter_context(handle.register("qreg_%s%d" % (br_eng, kk)))
                handle.reg_load(reg, qcol[0:1, kk:kk + 1])
                vals[br_eng][kk] = handle.snap(reg, min_val=0, max_val=NTB - 1)

        P.init[br_eng] = init_eng

        def br_in(off, T, br_eng=br_eng):
            if off == 0:
                return brgm.rearrange("(c p) t -> p c t", p=128)
            kk = (off - NMETA) // TT
            return lambda: brgx[bass.ds(vals[br_eng][kk], 1)].rearrange("o (c p) t -> p (o c) t", p=128)

        final = l == depth - 1

        def after_xout(cx_, off):
            if off == 0:
                return
            k_ = (off - NMETA) // TT
            for h in range(2):
                cx_.P.op("pool", lambda e, k_=k_, h=h: e.collective_compute(
                    "AllGather", ALU.bypass, replica_groups=GROUPS, ins=[xx[k_, h].opt()], outs=[xg[k_, h].opt()]),
                    reads=[("xout", off)], writes=[("xg", k_, h)], dma_key="ccx", inc=1, is_out=True)

        with ExitStack() as es:
            cx = Ctx(nc, P, es)
            emit_C(cx, l, halves, x_in, br_in, L[l]["g1"], L[l]["g2"], gf, cstB,
                   L[l]["wg"], L[l]["wb"], L[l]["wo"], L[l]["wu"], L[l]["wd"],
                   (lambda off, T: xo[:, off:off + T]) if final else x_in, final, pfx="c%d_" % l, br_eng=br_eng,
                   after_xout=None if final else after_xout)
            P.emit()
    return nc


def fused_inputs(x, meta_tokens, lb_logits, norm1_g, w_in, hg_norm_g, pool_w, pool_scale, conv_w,
                 w_branch, w_o, norm2_g, w_up, w_down, final_norm_g, depth=DEPTH):
    B_, S, _ = x.shape
    NQ = 8 // B_
    SQ = S // NQ
    gf = _r8(final_norm_g)
    cw = [prep_C_weights(l, w_in, w_branch, w_o, w_up, w_down) for l in range(depth)]
    in_maps = []
    for r in range(8):
        b, q = r // NQ, r % NQ
        im = {}
        im["x0"] = np.ascontiguousarray(np.concatenate([meta_tokens, x[b, q * SQ:(q + 1) * SQ]], axis=0).T)
        qc = np.zeros((1, 8), np.int32)
        for kk in range(SQ // TT):
            qc[0, kk] = q * (SQ // TT) + kk
        im["qcol"] = qc
        im["gf"] = gf
        for l in range(depth):
            pb = prep_B_inputs(l, q, w_in, norm1_g, lb_logits, hg_norm_g, pool_w, pool_scale, conv_w)
            im["wB%d" % l], im["g1_%d" % l], im["vec%d" % l], im["pw%d" % l] = pb["w"], pb["g1"], pb["vec"], pb["pw"]
            im["lbl"], im["cstB"] = pb["lbl"], pb["cst"]
            im["g2_%d" % l] = _r8(norm2_g[l])
            for kname in ("wg", "wb", "wo", "wu", "wd"):
                im["%s%d" % (kname, l)] = cw[l][kname]
        in_maps.append(im)
    return in_maps


def kernel(x, meta_tokens, lb_logits, norm1_g, w_in, hg_norm_g, pool_w, pool_scale, conv_w,
           w_branch, w_o, norm2_g, w_up, w_down, final_norm_g):
    f = lambda a: np.asarray(a, dtype=np.float32)
    args = [f(a) for a in (x, meta_tokens, lb_logits, norm1_g, w_in, hg_norm_g, pool_w, pool_scale, conv_w,
                           w_branch, w_o, norm2_g, w_up, w_down, final_norm_g)]
    x = args[0]
    B_, S, _ = x.shape
    NQ = 8 // B_
    SQ = S // NQ
    nc = build_fused(S)
    in_maps = fused_inputs(*args)
    res = run_bass_kernel_spmd(nc, in_maps, core_ids=list(range(8))).results
    out = np.empty((B_, S, D), np.float32)
    for r in range(8):
        b, q = r // NQ, r % NQ
        out[b, q * SQ:(q + 1) * SQ] = np.asarray(res[r]["xo"])[:, NMETA:].T
    return out
```

```python
from contextlib import ExitStack
import numpy as np
import ml_dtypes
import concourse.bass as bass
import concourse.mybir as mybir
from concourse.bass_utils import run_bass_kernel_spmd

F32 = mybir.dt.float32
BF16 = mybir.dt.bfloat16
ALU = mybir.AluOpType
AF = mybir.ActivationFunctionType

D = 1024
NCK = 8
NMETA = 16
TT = 512
DEPTH = 4
EPS = 1e-6
POOL_WINDOWS = (2, 4, 8, 16)
SEM_CH = 30000
PARTS = {"conv", "pool", "hgrn", "attn"}


class Prog:
    ENGS = ("pe", "act", "dve", "pool", "sp")

    def __init__(self, nc, state=None):
        self.nc = nc
        self.ops = {e: [] for e in self.ENGS}
        self.state = state if state is not None else {"cnt": {e: 0 for e in self.ENGS}, "dma_cnt": {}, "handles": {},
                                                      "es": ExitStack()}
        self.cnt = self.state["cnt"]
        self.lastw = {}
        self.readers = {}
        self.known = {e: {} for e in self.ENGS}
        self.dma_cnt = self.state["dma_cnt"]
        self.semkeys = []
        self.semset = set()
        self.out_tokens = []
        self.init = {}
        self.excl = set(["M0", "M1", "Z0", "Z1", "G", "CS", "O", "TB"] + ["cp%d" % i for i in range(8)])

    def _sem(self, key):
        if key not in self.semset:
            self.semset.add(key)
            self.semkeys.append(key)
        return key

    LIMIT = None
    nops = 0

    def op(self, eng, fn, reads=(), writes=(), dma_key=None, is_out=False, inc=16):
        Prog.nops += 1
        if Prog.LIMIT is not None and Prog.nops > Prog.LIMIT and not is_out:
            return None
        deps = []
        for k in reads:
            t = self.lastw.get(k)
            if t is not None:
                deps.append(t)
            if k in self.excl:
                deps.extend(r for r in self.readers.get(k, ()) if r[2] != eng)
        for k in writes:
            t = self.lastw.get(k)
            if t is not None:
                deps.append(t)
            deps.extend(self.readers.get(k, ()))
        waits = {}
        kn = self.known[eng]
        for (sk, val, deng) in deps:
            if deng == eng and dma_key is None and eng == "pe":
                continue
            if kn.get(sk, 0) >= val:
                continue
            if waits.get(sk, 0) < val:
                waits[sk] = val
        for sk, val in waits.items():
            kn[sk] = val
        if dma_key is not None:
            sk = self._sem(("dma", dma_key))
            n = self.dma_cnt.get(sk, 0) + 1
            self.dma_cnt[sk] = n
            done = (sk, inc * n, "dma")
        else:
            idx = self.cnt[eng]
            self.cnt[eng] += 1
            sk = self._sem(("eng", eng, idx // SEM_CH))
            done = (sk, idx % SEM_CH + 1, eng)
            inc = 1
        self.ops[eng].append((list(waits.items()), fn, sk, inc))
        for k in writes:
            self.lastw[k] = done
            self.readers[k] = []
        for k in reads:
            self.readers.setdefault(k, []).append(done)
        if is_out:
            self.out_tokens.append(done)
        return done

    def emit(self):
        nc = self.nc
        final_waits = {}
        for (sk, val, _e) in self.out_tokens:
            if final_waits.get(sk, 0) < val:
                final_waits[sk] = val
        with ExitStack() as es:
            sems = self.state["handles"]
            for sk in self.semkeys:
                if sk not in sems:
                    sems[sk] = self.state["es"].enter_context(nc.semaphore("s%d" % len(sems)))
            block = es.enter_context(nc.Block())

            def run(engname, handle):
                if engname in self.init:
                    self.init[engname](handle, es)
                for (waits, fn, sk, inc) in self.ops[engname]:
                    for wk, wv in waits:
                        handle.wait_ge(sems[wk], wv)
                    fn(handle).then_inc(sems[sk], inc)
                if engname == "sp":
                    for wk, wv in final_waits.items():
                        handle.wait_ge(sems[wk], wv)

            @block.tensor
            def _(e):
                run("pe", e)

            @block.scalar
            def _(e):
                run("act", e)

            @block.vector
            def _(e):
                run("dve", e)

            @block.gpsimd
            def _(e):
                run("pool", e)

            @block.sync
            def _(e):
                run("sp", e)


class Ctx:
    def __init__(self, nc, P, es):
        self.nc, self.P, self.es = nc, P, es

    def sb(self, name, shape, dt):
        return self.es.enter_context(self.nc.sbuf_tensor(name, shape, dt))

    def ps(self, name, shape, dt=F32):
        return self.es.enter_context(self.nc.psum_tensor(name, shape, dt))

    def mm(self, out, lhsT, rhs, start, stop, r, w, skip=False):
        if skip:
            self.P.op("pe", lambda e: e.matmul(out, lhsT, rhs, start=start, stop=stop, skip_group_check=True),
                      reads=r, writes=w)
        else:
            self.P.op("pe", lambda e: e.matmul(out, lhsT, rhs, start=start, stop=stop), reads=r, writes=w)

    def tr(self, out, in_, ident, r, w):
        self.P.op("pe", lambda e: e.transpose(out, in_, ident), reads=r, writes=w)

    def act(self, out, in_, func, r, w, bias=None, scale=None):
        kw = {}
        if bias is not None:
            kw["bias"] = bias
        if scale is not None:
            kw["scale"] = scale
        self.P.op("act", lambda e: e.activation(out=out, in_=in_, func=func, **kw), reads=r, writes=w)

    def tt(self, eng, out, in0, in1, op, r, w):
        self.P.op(eng, lambda e: e.tensor_tensor(out=out, in0=in0, in1=in1, op=op), reads=r, writes=w)

    def ts(self, eng, out, in0, s1, op0, r, w, s2=None, op1=None):
        if op1 is None:
            self.P.op(eng, lambda e: e.tensor_scalar(out=out, in0=in0, scalar1=s1, scalar2=None, op0=op0),
                      reads=r, writes=w)
        else:
            self.P.op(eng, lambda e: e.tensor_scalar(out=out, in0=in0, scalar1=s1, scalar2=s2, op0=op0, op1=op1),
                      reads=r, writes=w)

    def stt(self, eng, out, in0, scalar, in1, op0, op1, r, w):
        self.P.op(eng, lambda e: e.scalar_tensor_tensor(out=out, in0=in0, scalar=scalar, in1=in1, op0=op0, op1=op1),
                  reads=r, writes=w)

    def copy(self, eng, out, in_, r, w):
        if eng == "act":
            self.P.op("act", lambda e: e.activation(out=out, in_=in_, func=AF.Copy), reads=r, writes=w)
        else:
            self.P.op(eng, lambda e: e.tensor_copy(out=out, in_=in_), reads=r, writes=w)

    def recip(self, out, in_, r, w):
        self.P.op("dve", lambda e: e.reciprocal(out=out, in_=in_), reads=r, writes=w)

    def memset(self, eng, ap, val, w):
        self.P.op(eng, lambda e: e.memset(ap, val), writes=w)

    def dma(self, eng, out, in_, r, w, key, is_out=False):
        self.P.op(eng, lambda e: e.dma_start(out=out, in_=(in_() if callable(in_) else in_)), reads=r, writes=w,
                  dma_key=key, is_out=is_out)


def seq_tiles(S):
    return [(0, NMETA)] + [(NMETA + TT * i, TT) for i in range(S // TT)]


def gblock(g):
    return (0, NMETA) if g == 0 else (NMETA + 128 * (g - 1), 128)


def host_consts(window):
    i = np.arange(128)
    c = {}
    c["ident"] = np.eye(128, dtype=np.float32)
    c["ones"] = np.ones((128, 128), np.float32)
    c["strict"] = (i[:, None] < i[None, :]).astype(np.float32)
    c["negun"] = -(i[:, None] >= i[None, :]).astype(np.float32)
    c["triu"] = (i[:, None] <= i[None, :]).astype(np.float32)
    w = window
    s, t = i[:, None], i[None, :]
    band = ((s <= t) & (s > t - w)).astype(np.float32)
    c["mdiag"] = band / w - np.eye(128, dtype=np.float32)
    c["moff"] = ((s - 128) > (t - w)).astype(np.float32) / w
    mf = np.zeros((128, 128), np.float32)
    mf[:16] = (s[:16] > (16 + t - w)).astype(np.float32) / w
    c["mofff"] = mf
    bm = np.zeros((128, 128), np.float32)
    bm[:16, :16] = band[:16, :16]
    c["bmeta"] = bm
    invn = np.zeros((128, 128), np.float32)
    invn[:, :16] = 1.0 / np.minimum(w, np.arange(16) + 1.0)[None, :]
    c["invn"] = invn
    names = ["ident", "ones", "strict", "negun", "triu", "mdiag", "moff", "mofff", "bmeta", "invn"]
    return names, np.concatenate([c[n] for n in names], axis=1)


CONST_NAMES = ["ident", "ones", "strict", "negun", "triu", "mdiag", "moff", "mofff", "bmeta", "invn"]


def emit_B(cx, S, layer, xT_ap, w_ap, g1_ap, lbl_ap, vec_ap, pw_ap, cst_ap, out_ap, pfx="", x_loader=None,
           after_store=None):
    P = cx.P
    Ltot = NMETA + S
    tiles = seq_tiles(S)
    NB = 1 + S // 128
    k = lambda s: pfx + s
    sk = lambda s: "B." + s

    w_bf = cx.sb(k("w_bf"), [128, NCK, 11 * 128], BF16)
    cst_f = cx.sb(k("cst_f"), [128, 10 * 128], F32)
    cst_b = cx.sb(k("cst_b"), [128, 10 * 128], BF16)
    g1 = cx.sb(k("g1"), [128, 8], F32)
    lbl = cx.sb(k("lbl"), [128, 4], F32)
    vec = cx.sb(k("vec"), [128, 8], F32)
    pw_b = cx.sb(k("pw_b"), [128, 128], BF16)
    sm = cx.sb(k("sm"), [128, 16], F32)
    kT = cx.sb(k("kT"), [128, Ltot], BF16)
    vc = cx.sb(k("vc"), [128, NB, 128], BF16)
    Sst = cx.sb(k("Sst"), [128, 128], F32)
    Ssc = cx.sb(k("Ssc"), [128, 128], BF16)
    ubuf = cx.sb(k("ubuf"), [128, TT + 2], F32)
    pv = cx.sb(k("pv"), [128, 5, 128], BF16)
    onesf = cx.sb(k("onesf"), [128, TT], F32)
    xt = [cx.sb(k("xt%d" % i), [128, NCK, TT], F32) for i in range(2)]
    sqb = cx.sb(k("sqb"), [128, NCK, TT], BF16)
    hT = cx.sb(k("hT"), [128, NCK, TT], BF16)
    rstd = cx.sb(k("rstd"), [128, TT], F32)
    proj = cx.sb(k("proj"), [128, 8, TT], F32)
    qT = cx.sb(k("qT"), [128, TT], BF16)
    itok = cx.sb(k("itok"), [128, 4, 128], BF16)
    brout = [cx.sb(k("brout%d" % i), [128, 4, TT], BF16) for i in range(2)]
    t1 = cx.sb(k("t1"), [128, TT], F32)
    t2 = cx.sb(k("t2"), [128, TT], F32)
    fval = cx.sb(k("fval"), [128, TT], F32)
    kk = cx.sb(k("kk"), [128, TT], F32)
    lf = cx.sb(k("lf"), [128, TT], F32)
    Bc = cx.sb(k("Bc"), [128, TT + 1], F32)
    cex = [cx.sb(k("cex%d" % i), [128, 64], F32) for i in range(3)]
    qe = cx.sb(k("qe"), [128, TT], BF16)
    ke = cx.sb(k("ke"), [128, TT], BF16)
    kdT = cx.sb(k("kdT"), [128, TT], BF16)
    kdtok = cx.sb(k("kdtok"), [128, 4, 128], BF16)
    scm = cx.sb(k("scm"), [128, 64], BF16)
    dsm = cx.sb(k("dsm"), [128, 4, 8], F32)
    osb = cx.sb(k("osb"), [128, TT], F32)
    osq = cx.sb(k("osq"), [128, TT], BF16)
    uTb = cx.sb(k("uTb"), [128, 128], BF16)
    uTf = cx.sb(k("uTf"), [128, 16], F32)
    Ef2 = cx.sb(k("Ef2"), [128, 2 * TT], F32)
    spb2 = cx.sb(k("spb2"), [128, 2 * TT], BF16)
    lg2 = cx.sb(k("lg2"), [128, 2 * TT], F32)
    Ab2 = cx.sb(k("Ab2"), [128, 2 * TT], BF16)
    Ef = [Ef2[:, i * TT:(i + 1) * TT] for i in range(2)]
    spb = [spb2[:, i * TT:(i + 1) * TT] for i in range(2)]
    lg = [lg2[:, i * TT:(i + 1) * TT] for i in range(2)]
    Ab = [Ab2[:, i * TT:(i + 1) * TT] for i in range(2)]
    h3 = lambda t_: t_[:, :].rearrange("p (h t) -> p h t", h=2)
    carry = [cx.sb(k("carry%d" % i), [128, TT], F32) for i in range(2)]
    ob = cx.sb(k("ob"), [128, 4 * 128], BF16)
    zbf = cx.sb(k("zbf"), [128, TT], BF16)
    M = [cx.ps(k("pM%d" % i), [128, TT]) for i in range(2)]
    Z = [cx.ps(k("pZ%d" % i), [128, TT]) for i in range(2)]
    G = cx.ps(k("pG"), [128, TT])
    CS = cx.ps(k("pCS"), [128, TT])
    O = cx.ps(k("pO"), [128, TT])
    TB = cx.ps(k("pTB"), [128, 2 * TT], BF16)

    def C(name, rows=128, cols=128, bf=True):
        i = CONST_NAMES.index(name)
        src = cst_b if bf else cst_f
        return src[0:rows, i * 128:i * 128 + cols]

    cx.dma("sp", cst_f[:], cst_ap, [], ["cst_f"], sk("cst_f"))
    cx.dma("pool", cst_b[:], cst_ap, [], ["cst_b"], sk("cst_b"))
    cx.dma("sp", g1[:], g1_ap, [], ["g1"], sk("g1"))
    cx.dma("sp", lbl[:], lbl_ap, [], ["lbl"], sk("lbl"))
    cx.dma("sp", vec[:], vec_ap, [], ["vec"], sk("vec"))
    cx.dma("pool", pw_b[:], pw_ap, [], ["pw_b"], sk("pw_b"))
    for c in range(NCK):
        cx.dma("pool", w_bf[:, c, :], w_ap[c * 128:(c + 1) * 128, :], [], [("w", c)], sk("w%d" % c))
    cx.memset("pool", onesf[:], 1.0, ["onesf"])
    cx.memset("pool", zbf[:], 0.0, ["zbf"])
    cx.memset("pool", Sst[:], 0.0, ["S"])
    cx.memset("pool", ubuf[:, 0:2], 0.0, ["ubuf"])
    cx.memset("pool", Bc[:, 0:1], 0.0, ["Bc"])
    P.op("dve", lambda e: e.reduce_max(out=sm[:, 0:1], in_=lbl[:], axis=mybir.AxisListType.X), reads=["lbl"], writes=["sm"])
    cx.ts("dve", sm[:, 1:2], sm[:, 0:1], -1.0, ALU.mult, ["sm"], ["sm"])
    cx.act(sm[:, 8:12], lbl[:], AF.Exp, ["sm", "lbl"], ["sm"], bias=sm[:, 1:2])
    P.op("dve", lambda e: e.reduce_sum(out=sm[:, 2:3], in_=sm[:, 8:12], axis=mybir.AxisListType.X), reads=["sm"], writes=["sm"])
    cx.recip(sm[:, 3:4], sm[:, 2:3], ["sm"], ["sm"])
    if layer == 0:
        cx.memset("dve", sm[:, 4:5], 0.0, ["sm"])
    else:
        P.op("dve", lambda e: e.reduce_sum(out=sm[:, 4:5], in_=sm[:, 9:9 + layer], axis=mybir.AxisListType.X), reads=["sm"], writes=["sm"])
        cx.tt("dve", sm[:, 4:5], sm[:, 4:5], sm[:, 3:4], ALU.mult, ["sm"], ["sm"])
    cx.ts("dve", sm[:, 5:6], sm[:, 4:5], -1.0, ALU.mult, ["sm"], ["sm"], s2=1.0, op1=ALU.add)
    lb_ap, oml_ap = sm[:, 4:5], sm[:, 5:6]

    def load_x(ti):
        off, T = tiles[ti]
        buf = xt[ti % 2]
        if x_loader is not None:
            x_loader(cx, ti, off, T, buf, ("xt", ti % 2), sk("xt%d" % (ti % 2)))
            return
        cx.dma("sp", buf[:, :, 0:T], xT_ap(off, T).rearrange("(c p) t -> p c t", p=128), ["xg"], [("xt", ti % 2)],
               sk("xt%d" % (ti % 2)))

    load_x(0)
    for ti, (off, T) in enumerate(tiles):
        if ti + 1 < len(tiles):
            load_x(ti + 1)
        x = xt[ti % 2]
        xk = ("xt", ti % 2)
        meta = (ti == 0)
        if meta:
            blocks = [(0, NMETA)]
            gb0 = 0
        else:
            blocks = [(128 * m, 128) for m in range(T // 128)]
            gb0 = 4 * (ti - 1) + 1
        bo = brout[ti % 2]
        bok = ("brout", ti % 2)


        cx.act(sqb[:, :, 0:T], x[:, :, 0:T], AF.Square, [xk], ["sqb"])
        for c in range(NCK):
            cx.mm(M[0][:, 0:T], C("ones"), sqb[:, c, 0:T], c == 0, c == NCK - 1, ["sqb", "cst_b"], ["M0"])
        cx.act(rstd[:, 0:T], M[0][:, 0:T], AF.Ln, ["M0"], ["rstd"], bias=EPS, scale=1.0 / D)
        cx.act(rstd[:, 0:T], rstd[:, 0:T], AF.Exp, ["rstd"], ["rstd"], scale=-0.5)
        for c in range(NCK):
            cx.stt("dve", hT[:, c, 0:T], x[:, c, 0:T], g1[:, c:c + 1], rstd[:, 0:T],
                   ALU.mult, ALU.mult, [xk, "rstd", "g1"], [("hT", c)])
        hTk = [("hT", c) for c in range(NCK)]
        wk = [("w", c) for c in range(NCK)]

        for n in range(8):
            pm = M[n % 2]
            pk = "M%d" % (n % 2)
            for c in range(NCK):
                cx.mm(pm[:, 0:T], w_bf[:, c, n * 128:(n + 1) * 128], hT[:, c, 0:T], c == 0, c == NCK - 1,
                      hTk + wk, [pk])
            if n == 3:
                cx.copy("act", qT[:, 0:T], pm[:, 0:T], [pk], ["qT"])
            elif n == 4:
                cx.act(kT[:, off:off + T], pm[:, 0:T], AF.Copy, [pk], [("kT", ti)], scale=0.125)
            else:
                cx.copy("act", proj[:, n, 0:T], pm[:, 0:T], [pk], [("proj", n)])

        for m, (bs, bl) in enumerate(blocks):
            pm = M[m % 2]
            pk = "M%d" % (m % 2)
            for c in range(NCK):
                cx.mm(pm[0:bl, 0:384], hT[:, c, bs:bs + bl], w_bf[:, c, 1024:1408], c == 0, c == NCK - 1,
                      hTk + wk, [pk])
            cx.copy("act", itok[0:bl, m, :], pm[0:bl, 0:128], [pk], [("itok", m)])
            cx.copy("dve", pv[0:bl, 1 + m, :], pm[0:bl, 128:256], [pk], [("pv", 1 + m)])
            cx.copy("act", vc[0:bl, gb0 + m, :], pm[0:bl, 256:384], [pk], [("vc", gb0 + m)])


        if "conv" in PARTS:
            cx.tt("pool", ubuf[:, 2:2 + T], proj[:, 7, 0:T], proj[:, 5, 0:T], ALU.mult, [("proj", 7), ("proj", 5)], ["ubuf"])
            cx.ts("pool", t2[:, 0:T], ubuf[:, 2:2 + T], vec[:, 4:5], ALU.mult, ["ubuf", "vec"], ["t2"])
            cx.stt("dve", t2[:, 0:T], ubuf[:, 1:1 + T], vec[:, 3:4], t2[:, 0:T], ALU.mult, ALU.add, ["ubuf", "vec", "t2"], ["t2"])
            cx.stt("dve", t2[:, 0:T], ubuf[:, 0:T], vec[:, 2:3], t2[:, 0:T], ALU.mult, ALU.add, ["ubuf", "vec", "t2"], ["t2"])
            cx.tt("pool", bo[:, 3, 0:T], t2[:, 0:T], proj[:, 6, 0:T], ALU.mult, ["t2", ("proj", 6)], [bok])
            cx.copy("pool", ubuf[:, 0:2], ubuf[:, T:T + 2], ["ubuf"], ["ubuf"])

        if "pool" in PARTS:
            for m, (bs, bl) in enumerate(blocks):
                pm = M[m % 2]
                pk = "M%d" % (m % 2)
                if meta:
                    cx.mm(pm[:, 0:16], pv[0:16, 1, :], C("bmeta", 16, 16), True, True, [("pv", 1), "cst_b"], [pk])
                    cx.mm(pm[:, 16:32], pv[0:16, 1, :], C("ident", 16, 16), True, True, [("pv", 1), "cst_b"], [pk])
                    cx.tt("dve", uTf[:, 0:16], pm[:, 0:16], C("invn", 128, 16, bf=False), ALU.mult, [pk, "cst_f"], ["uTf"])
                    cx.tt("dve", uTb[:, 0:16], uTf[:, 0:16], pm[:, 16:32], ALU.subtract, [pk, "uTf"], ["uTb"])
                else:
                    prev_rows = 16 if (ti == 1 and m == 0) else 128
                    moff = C("mofff", 16, 128) if (ti == 1 and m == 0) else C("moff")
                    cx.mm(pm[:, 0:128], pv[:, 1 + m, :], C("mdiag"), True, False, [("pv", 1 + m), "cst_b"], [pk])
                    cx.mm(pm[:, 0:128], pv[0:prev_rows, m, :], moff, False, True, [("pv", m), "cst_b"], [pk])
                    cx.copy("dve", uTb[:, 0:bl], pm[:, 0:bl], [pk], ["uTb"])
                cx.mm(pm[:, 128:128 + bl], pw_b[:], uTb[:, 0:bl], True, True, ["uTb", "pw_b"], [pk])
                cx.act(bo[:, 1, bs:bs + bl], pm[:, 128:128 + bl], AF.Copy, [pk, "vec"], [bok], scale=vec[:, 1:2])
            lastm = len(blocks) - 1
            lbl_rows = blocks[lastm][1]
            cx.copy("pool", pv[0:lbl_rows, 0, :], pv[0:lbl_rows, 1 + lastm, :], [("pv", 1 + lastm), ("pv", 0)], [("pv", 0)])

        if "hgrn" in PARTS:
            cx.act(t1[:, 0:T], proj[:, 1, 0:T], AF.Exp, [("proj", 1)], ["t1"], scale=-1.0)
            cx.ts("dve", t1[:, 0:T], t1[:, 0:T], 1.0, ALU.add, ["t1"], ["t1"])
            cx.recip(t1[:, 0:T], t1[:, 0:T], ["t1"], ["t1"])
            cx.ts("dve", fval[:, 0:T], t1[:, 0:T], oml_ap, ALU.mult, ["t1", "sm"], ["fval"], s2=lb_ap, op1=ALU.add)
            cx.act(lf[:, 0:T], fval[:, 0:T], AF.Ln, ["fval"], ["lf"])
            cx.ts("pool", kk[:, 0:T], fval[:, 0:T], -1.0, ALU.mult, ["fval"], ["kk"], s2=1.0, op1=ALU.add)
            P.op("dve", lambda e, T=T: e.tensor_tensor_scan(out=Bc[:, 1:1 + T], data0=onesf[:, 0:T], data1=lf[:, 0:T],
                                                            initial=0.0, op0=ALU.mult, op1=ALU.add),
                 reads=["lf", "onesf"], writes=["Bc"])
            CL = 16 if meta else 64
            nch = T // CL
            Bv = Bc[:, 1:1 + T].rearrange("p (c l) -> p c l", l=CL)
            Bp = Bc[:, 0:T].rearrange("p (c l) -> p c l", l=CL)[:, :, 0:1]
            Bm = Bv[:, :, CL // 2 - 1:CL // 2]
            Be = Bv[:, :, CL - 1:CL]
            v3 = lambda t_: t_[:, 0:T].rearrange("p (c l) -> p c l", l=CL)
            cx.tt("dve", dsm[:, 2, 0:nch].rearrange("p (c o) -> p c o", o=1), Bm, Bp, ALU.subtract, ["Bc"], ["dsm"])
            cx.tt("dve", dsm[:, 3, 0:nch].rearrange("p (c o) -> p c o", o=1), Be, Bp, ALU.subtract, ["Bc"], ["dsm"])
            cx.act(dsm[:, 0:2, 0:nch], dsm[:, 2:4, 0:nch], AF.Exp, ["dsm"], ["dsm"])
            cx.tt("dve", v3(lf), Bv, Bm.to_broadcast([128, nch, CL]), ALU.subtract, ["Bc", "lf"], ["lf"])
            cx.tt("dve", v3(fval), Bv, Be.to_broadcast([128, nch, CL]), ALU.subtract, ["Bc", "fval", "kk"], ["fval"])
            cx.act(t1[:, 0:T], lf[:, 0:T], AF.Exp, ["lf"], ["t1"])
            cx.tt("dve", qe[:, 0:T], t1[:, 0:T], proj[:, 0, 0:T], ALU.mult, ["t1", ("proj", 0)], ["qe"])
            cx.act(t2[:, 0:T], lf[:, 0:T], AF.Exp, ["lf"], ["t2"], scale=-1.0)
            cx.tt("pool", ke[:, 0:T], t2[:, 0:T], kk[:, 0:T], ALU.mult, ["t2", "kk"], ["ke"])
            cx.act(t1[:, 0:T], fval[:, 0:T], AF.Exp, ["fval"], ["t1"], scale=-1.0)
            cx.tt("pool", kdT[:, 0:T], t1[:, 0:T], kk[:, 0:T], ALU.mult, ["t1", "kk"], ["kdT"])
            for ci in range(nch):
                c0 = ci * CL
                m = c0 // 128
                r0 = c0 % 128
                cx.tr(TB[r0:r0 + CL, 0:128], kdT[:, c0:c0 + CL], C("ident"), ["kdT", "cst_b"], ["TB"])
                cx.copy("act", kdtok[r0:r0 + CL, m, :], TB[r0:r0 + CL, 0:128], ["TB"], ["kdtok"])
                zb = Z[ci % 2]
                zk = "Z%d" % (ci % 2)
                cx.mm(zb[r0:r0 + CL, 0:CL], ke[:, c0:c0 + CL], qe[:, c0:c0 + CL], True, True, ["ke", "qe"], [zk])
                cx.tt("dve", scm[r0:r0 + CL, 0:CL], zb[r0:r0 + CL, 0:CL], C("triu", 128, 128, bf=False)[r0:r0 + CL, r0:r0 + CL],
                      ALU.mult, [zk, "cst_f"], ["scm"])
                cx.ts("dve", Ssc[:], Sst[:], dsm[:, 0, ci:ci + 1], ALU.mult, ["S", "dsm"], ["Ssc"])
                cx.mm(G[:, c0:c0 + CL], Ssc[:], qe[:, c0:c0 + CL], True, False, ["Ssc", "qe"], ["G"])
                cx.mm(G[:, c0:c0 + CL], itok[r0:r0 + CL, m, :], scm[r0:r0 + CL, 0:CL], False, True, [("itok", m), "scm"], ["G"])
                mb = M[ci % 2]
                mk = "M%d" % (ci % 2)
                cx.mm(mb[:, 0:128], kdtok[r0:r0 + CL, m, :], itok[r0:r0 + CL, m, :], True, True, ["kdtok", ("itok", m)], [mk])
                cx.stt("dve", Sst[:], Sst[:], dsm[:, 1, ci:ci + 1], mb[:, 0:128], ALU.mult, ALU.add, ["S", "dsm", mk], ["S"])
            cx.copy("act", osb[:, 0:T], G[:, 0:T], ["G"], ["osb"])
            cx.act(osq[:, 0:T], osb[:, 0:T], AF.Square, ["osb"], ["osq"])
            cx.mm(CS[:, 0:T], C("ones"), osq[:, 0:T], True, True, ["osq", "cst_b"], ["CS"])
            cx.act(t2[:, 0:T], CS[:, 0:T], AF.Ln, ["CS"], ["t2"], bias=EPS, scale=1.0 / 128)
            cx.act(t2[:, 0:T], t2[:, 0:T], AF.Exp, ["t2"], ["t2"], scale=-0.5)
            cx.act(t1[:, 0:T], proj[:, 2, 0:T], AF.Exp, [("proj", 2)], ["t1"], scale=-1.0)
            cx.ts("dve", t1[:, 0:T], t1[:, 0:T], 1.0, ALU.add, ["t1"], ["t1"])
            cx.recip(t1[:, 0:T], t1[:, 0:T], ["t1"], ["t1"])
            cx.stt("dve", osb[:, 0:T], osb[:, 0:T], vec[:, 0:1], t2[:, 0:T], ALU.mult, ALU.mult, ["osb", "vec", "t2"], ["osb"])
            cx.tt("dve", bo[:, 0, 0:T], osb[:, 0:T], t1[:, 0:T], ALU.mult, ["osb", "t1"], [bok])

        if "attn" in PARTS:
            nsub = len(blocks)
            SW = blocks[0][1]
            last_gb = gb0 + nsub - 1
            cx.mm(O[:, 0:T], zbf[:, 0:128], zbf[:, 0:T], True, False, ["zbf"], ["O"])
            its = list(range(last_gb, -1, -1))
            n_it = len(its)
            for h in range(2):
                cx.memset("pool", carry[h][:, 0:T], 0.0, [("carry", h)])
            Zb = [[(Z[0], "Z0"), (G, "G")], [(Z[1], "Z1"), (M[0], "M0")]]
            CSb = [(CS, "CS"), (M[1], "M1")]

            def geom(i):
                kb = its[i]
                ks, KL = gblock(kb)
                kti = 0 if kb == 0 else (kb - 1) // 4 + 1
                diag = ks >= off
                qc0 = max(off, ks) - off
                return kb, ks, KL, kti, diag, qc0, T - qc0

            def phA1(i, h):
                kb, ks, KL, kti, diag, qc0, N = geom(i)
                hp = slice(64 * h, 64 * h + 64)
                zb, zk = Zb[h][i % 2]
                cx.mm(zb[0:KL, 0:N], kT[hp, ks:ks + KL], qT[hp, qc0:qc0 + N], True, True, [("kT", kti), "qT"], [zk])

            def phA(i, h):
                kb, ks, KL, kti, diag, qc0, N = geom(i)
                zb, zk = Zb[h][i % 2]
                ef, efk = Ef[h], ("Ef", h)
                sp, spk = spb[h], ("spb", h)
                cx.act(ef[0:KL, 0:N], zb[0:KL, 0:N], AF.Exp, [zk], [efk])
                cx.act(sp[0:KL, 0:N], ef[0:KL, 0:N], AF.Ln, [efk], [spk], bias=1.0)
                if diag:
                    cx.tt("pool", sp[0:KL, 0:SW], sp[0:KL, 0:SW], C("strict", KL, SW), ALU.mult, [spk, "cst_b"], [spk])

            def phA_both(i):
                return

                kb, ks, KL, kti, diag, qc0, N = geom(i)
                cx.act(h3(spb2)[0:KL, :, 0:N], h3(Ef2)[0:KL, :, 0:N], AF.Ln, [("Ef", 0), ("Ef", 1)],
                       [("spb", 0), ("spb", 1)], bias=1.0)
                if diag:
                    for h in range(2):
                        sp, spk = spb[h], ("spb", h)
                        cx.tt("pool", sp[0:KL, 0:SW], sp[0:KL, 0:SW], C("strict", KL, SW), ALU.mult, [spk, "cst_b"], [spk])

            def phC_both(i):
                return
                kb, ks, KL, kti, diag, qc0, N = geom(i)
                cx.act(h3(Ab2)[0:KL, :, 0:N], h3(lg2)[0:KL, :, 0:N], AF.Exp, [("lg", 0), ("lg", 1)],
                       [("Ab", 0), ("Ab", 1)])

            def phB(i, h):
                kb, ks, KL, kti, diag, qc0, N = geom(i)
                sp, spk = spb[h], ("spb", h)
                lgt, lgk = lg[h], ("lg", h)
                cr, crk = carry[h], ("carry", h)
                gb, gk = Zb[h][i % 2]
                cb, ck_ = CSb[h]
                cx.mm(gb[0:KL, 0:N], C("negun", KL, KL), sp[0:KL, 0:N], False, True, [spk, "cst_b"], [gk], skip=True)
                if kb > 0:
                    cx.mm(cb[:, 0:N], C("ones", KL, 128), sp[0:KL, 0:N], True, True, [spk, "cst_b"], [ck_])
                cx.tt("dve", lgt[0:KL, 0:N], gb[0:KL, 0:N], cr[0:KL, qc0:qc0 + N], ALU.subtract, [gk, crk], [lgk])
                if kb > 0:
                    cx.tt("dve", cr[:, qc0:qc0 + N], cr[:, qc0:qc0 + N], cb[:, 0:N], ALU.add, [ck_, crk], [crk])

            def phC(i, h):
                kb, ks, KL, kti, diag, qc0, N = geom(i)
                hp = slice(64 * h, 64 * h + 64)
                lgt, lgk = lg[h], ("lg", h)
                ab, abk = Ab[h], ("Ab", h)
                cx.act(ab[0:KL, 0:N], lgt[0:KL, 0:N], AF.Exp, [lgk], [abk])
                if diag:
                    cx.tt("pool", ab[0:KL, 0:SW], ab[0:KL, 0:SW], C("strict", KL, SW), ALU.mult, [abk, "cst_b"], [abk])
                cx.mm(O[hp, qc0:qc0 + N], vc[0:KL, kb, hp], ab[0:KL, 0:N], False, (kb == 0),
                      [abk, ("vc", kb)], ["O"])

            for t in range(n_it + 2):
                for h in range(2):
                    if t < n_it:
                        phA1(t, h)
                if 0 <= t - 2 < n_it:
                    phC_both(t - 2)
                for h in range(2):
                    if 0 <= t - 2 < n_it:
                        phC(t - 2, h)
                for h in range(2):
                    if 0 <= t - 1 < n_it:
                        phB(t - 1, h)
                for h in range(2):
                    if t < n_it:
                        phA(t, h)
                if t < n_it:
                    phA_both(t)
            cx.copy("dve", bo[:, 2, 0:T], O[:, 0:T], ["O"], [bok])

        cx.dma("sp", out_ap(off, T).rearrange("(n p) t -> p n t", p=128), bo[:, :, 0:T], [bok], [("brout_d", ti)],
               sk("bo%d" % (ti % 2)), is_out=True)
        if after_store is not None:
            after_store(cx, ti)


def emit_C(cx, layer, halves, x_in, br_in, g1_ap, g2_ap, gf_ap, cst_ap, wg_ap, wb_ap, wo_ap, wu_ap, wd_ap,
           x_out, final, pfx="", br_eng="sp", after_xout=None):
    P = cx.P
    k = lambda s: pfx + s
    sk = lambda s: "C." + s
    HT = max(sum(T for _, T in h) for h in halves)
    xh = cx.sb(k("xh"), [128, NCK, HT], F32)
    hh = cx.sb(k("hh"), [128, NCK, HT], BF16)
    big = cx.sb(k("big"), [128, 32, HT], BF16)
    mixb = cx.sb(k("mixb"), [128, NCK, HT], BF16)
    macc = cx.sb(k("macc"), [128, HT], F32)
    g1 = cx.sb(k("cg1"), [128, 8], F32)
    g2 = cx.sb(k("cg2"), [128, 8], F32)
    gf = cx.sb(k("cgf"), [128, 8], F32)
    ones_b = cx.sb(k("cones"), [128, 128], BF16)
    sqb = cx.sb(k("csqb"), [128, NCK, TT], BF16)
    rstd = cx.sb(k("crstd"), [128, TT], F32)
    gt = [cx.sb(k("gt%d" % i), [128, TT], F32) for i in range(2)]
    tmp = [cx.sb(k("ctmp%d" % i), [128, TT], F32) for i in range(2)]
    wg = [cx.sb(k("wg%d" % i), [128, NCK * 128], BF16) for i in range(3)]
    wb = [cx.sb(k("wb%d" % i), [128, 4 * 128], BF16) for i in range(3)]
    wd = [cx.sb(k("wd%d" % i), [128, 32 * 128], BF16) for i in range(2)]
    banks = [cx.ps(k("cp%d" % i), [128, TT]) for i in range(8)]
    rr = [0]

    def bank():
        i = rr[0] % 8
        rr[0] += 1
        return banks[i], "cp%d" % i

    ci = CONST_NAMES.index("ones")
    cx.dma("pool", ones_b[:], cst_ap[:, ci * 128:(ci + 1) * 128], [], ["cones"], sk("cones"))
    cx.dma("sp", g1[:], g1_ap, [], ["cg1"], sk("cg1"))
    cx.dma("sp", g2[:], g2_ap, [], ["cg2"], sk("cg2"))
    cx.dma("sp", gf[:], gf_ap, [], ["cgf"], sk("cgf"))
    wcnt = {"wg": 0, "wb": 0, "wd": 0}
    bigk = [("big", i) for i in range(4)]

    def loadw(kind, bufs, ap, n):
        i = wcnt[kind] % len(bufs)
        wcnt[kind] += 1
        cx.dma("pool", bufs[i][:, 0:n], ap, [], [(kind, i)], sk("%s%d" % (kind, i)))
        return bufs[i], (kind, i)

    def rms(tl, lo, g, T, outf):
        cx.act(sqb[:, :, 0:T], xh[:, :, lo:lo + T], AF.Square, ["xh"], ["csqb"])
        pb, pk = bank()
        for c in range(NCK):
            cx.mm(pb[:, 0:T], ones_b[:], sqb[:, c, 0:T], c == 0, c == NCK - 1, ["csqb", "cones"], [pk])
        cx.act(rstd[:, 0:T], pb[:, 0:T], AF.Ln, [pk], ["crstd"], bias=EPS, scale=1.0 / D)
        cx.act(rstd[:, 0:T], rstd[:, 0:T], AF.Exp, ["crstd"], ["crstd"], scale=-0.5)
        for c in range(NCK):
            o, ok = outf(c)
            cx.stt("dve", o, xh[:, c, lo:lo + T], g[:, c:c + 1], rstd[:, 0:T],
                   ALU.mult, ALU.mult, ["xh", "crstd", "cg1", "cg2", "cgf"], [ok])

    for hi, tiles in enumerate(halves):
        lo = 0
        ltiles = []
        for (off, T) in tiles:
            ltiles.append((lo, off, T))
            lo += T
        for ti2, (lo, off, T) in enumerate(ltiles):
            cx.dma("sp", xh[:, :, lo:lo + T], x_in(off, T).rearrange("(c p) t -> p c t", p=128), ["xint"], ["xh"], sk("xh"))
            cx.dma(br_eng, big[:, 0:16, lo:lo + T], br_in(off, T), ["brg"], [("big", ti2 % 4)], sk("big%d" % (ti2 % 4)))
        for (lo, off, T) in ltiles:
            rms(None, lo, g1, T, lambda c, lo=lo, T=T: (hh[:, c, lo:lo + T], "hh"))
        for dc in range(NCK):
            for n in range(4):
                wgb, wgk = loadw("wg", wg, wg_ap[dc, n], NCK * 128)
                wbb, wbk = loadw("wb", wb, wb_ap[dc, n], 4 * 128)
                for ti, (lo, off, T) in enumerate(ltiles):
                    pa, pak = bank()
                    pbk_ = bank()
                    pb, pbk = pbk_
                    for c in range(NCK):
                        cx.mm(pa[:, 0:T], wgb[:, c * 128:(c + 1) * 128], hh[:, c, lo:lo + T], c == 0, c == NCK - 1,
                              [wgk, "hh"], [pak])
                    for j in range(4):
                        cx.mm(pb[:, 0:T], wbb[:, j * 128:(j + 1) * 128], big[:, j * 4 + n, lo:lo + T], j == 0, j == 3,
                              [wbk] + bigk, [pbk])
                    g_, gk = gt[ti % 2], ("gt", ti % 2)
                    t_, tk = tmp[ti % 2], ("ctmp", ti % 2)
                    cx.act(g_[:, 0:T], pa[:, 0:T], AF.Sigmoid, [pak], [gk])
                    if n == 0:
                        cx.tt("dve", macc[:, lo:lo + T], g_[:, 0:T], pb[:, 0:T], ALU.mult, [gk, pbk], [("macc", ti)])
                    else:
                        cx.tt("dve", t_[:, 0:T], g_[:, 0:T], pb[:, 0:T], ALU.mult, [gk, pbk], [tk])
                        if n < 3:
                            cx.tt("dve", macc[:, lo:lo + T], macc[:, lo:lo + T], t_[:, 0:T], ALU.add,
                                  [tk, ("macc", ti)], [("macc", ti)])
                        else:
                            cx.tt("dve", mixb[:, dc, lo:lo + T], macc[:, lo:lo + T], t_[:, 0:T], ALU.add,
                                  [tk, ("macc", ti)], [("mixb", dc)])
        mixk = [("mixb", c) for c in range(NCK)]
        for dc in range(NCK):
            wgb, wgk = loadw("wg", wg, wo_ap[dc], NCK * 128)
            for ti, (lo, off, T) in enumerate(ltiles):
                pa, pak = bank()
                for c in range(NCK):
                    cx.mm(pa[:, 0:T], wgb[:, c * 128:(c + 1) * 128], mixb[:, c, lo:lo + T], c == 0, c == NCK - 1,
                          [wgk] + mixk, [pak])
                cx.tt("dve", xh[:, dc, lo:lo + T], xh[:, dc, lo:lo + T], pa[:, 0:T], ALU.add, [pak, "xh"], ["xh"])
        for (lo, off, T) in ltiles:
            rms(None, lo, g2, T, lambda c, lo=lo, T=T: (hh[:, c, lo:lo + T], "hh"))
        for f in range(32):
            wgb, wgk = loadw("wg", wg, wu_ap[f], NCK * 128)
            for ti, (lo, off, T) in enumerate(ltiles):
                pa, pak = bank()
                for c in range(NCK):
                    cx.mm(pa[:, 0:T], wgb[:, c * 128:(c + 1) * 128], hh[:, c, lo:lo + T], c == 0, c == NCK - 1,
                          [wgk, "hh"], [pak])
                g_, gk = gt[ti % 2], ("gt", ti % 2)
                cx.act(g_[:, 0:T], pa[:, 0:T], AF.Relu, [pak], [gk])
                cx.tt("dve", big[:, f, lo:lo + T], g_[:, 0:T], g_[:, 0:T], ALU.mult, [gk], bigk)
        for dc in range(NCK):
            wdb, wdk = loadw("wd", wd, wd_ap[dc], 32 * 128)
            for ti, (lo, off, T) in enumerate(ltiles):
                pa, pak = bank()
                for f in range(32):
                    cx.mm(pa[:, 0:T], wdb[:, f * 128:(f + 1) * 128], big[:, f, lo:lo + T], f == 0, f == 31,
                          [wdk] + bigk, [pak])
                cx.tt("dve", xh[:, dc, lo:lo + T], xh[:, dc, lo:lo + T], pa[:, 0:T], ALU.add, [pak, "xh"], ["xh"])
        for (lo, off, T) in ltiles:
            if final:
                rms(None, lo, gf, T, lambda c, lo=lo, T=T: (xh[:, c, lo:lo + T], "xh"))
            cx.dma("sp", x_out(off, T).rearrange("(c p) t -> p c t", p=128), xh[:, :, lo:lo + T], ["xh"],
                   [("xout", off)], sk("xo"), is_out=True)
            if after_xout is not None:
                after_xout(cx, off)


FM_SPLITS = [0, 1, 3, 5, 6, 8, 9, 10]
TM_SPLITS = [2, 4, 7]


def prep_B_inputs(layer, j, w_in, norm1_g, lb_logits, hg_norm_g, pool_w, pool_scale, conv_w):
    cols = []
    for sidx in FM_SPLITS + TM_SPLITS:
        cols.append(w_in[layer][:, sidx * 512 + j * 128: sidx * 512 + (j + 1) * 128])
    w = np.ascontiguousarray(np.concatenate(cols, axis=1), dtype=np.float32)
    g1 = np.ascontiguousarray(norm1_g[layer].reshape(NCK, 128).T, dtype=np.float32)
    lbl = np.ascontiguousarray(lb_logits[:, j * 128:(j + 1) * 128].T, dtype=np.float32)
    vec = np.zeros((128, 8), np.float32)
    sl = slice(j * 128, (j + 1) * 128)
    vec[:, 0] = hg_norm_g[layer, sl]
    vec[:, 1] = pool_scale[layer, sl]
    vec[:, 2] = conv_w[layer, 0, sl]
    vec[:, 3] = conv_w[layer, 1, sl]
    vec[:, 4] = conv_w[layer, 2, sl]
    pw = np.ascontiguousarray(pool_w[layer, j], dtype=np.float32)
    _, cst = host_consts(POOL_WINDOWS[j])
    return {"w": w, "g1": g1, "lbl": lbl, "vec": vec, "pw": pw, "cst": np.ascontiguousarray(cst)}


def build_B(S, layer):
    nc = bass.Bass("TRN2", target_bir_lowering=False)
    Ltot = NMETA + S
    xT = nc.dram_tensor("xT", [D, Ltot], F32, kind="ExternalInput").ap()
    w = nc.dram_tensor("w", [D, 11 * 128], F32, kind="ExternalInput").ap()
    g1 = nc.dram_tensor("g1", [128, 8], F32, kind="ExternalInput").ap()
    lbl = nc.dram_tensor("lbl", [128, 4], F32, kind="ExternalInput").ap()
    vec = nc.dram_tensor("vec", [128, 8], F32, kind="ExternalInput").ap()
    pw = nc.dram_tensor("pw", [128, 128], F32, kind="ExternalInput").ap()
    cst = nc.dram_tensor("cst", [128, 10 * 128], F32, kind="ExternalInput").ap()
    br = nc.dram_tensor("br", [512, Ltot], BF16, kind="ExternalOutput").ap()
    P = Prog(nc)
    with ExitStack() as es:
        cx = Ctx(nc, P, es)
        emit_B(cx, S, layer, lambda off, T: xT[:, off:off + T], w, g1, lbl, vec, pw, cst,
               lambda off, T: br[:, off:off + T], pfx="b_")
        P.emit()
    return nc


def prep_C_weights(layer, w_in, w_branch, w_o, w_up, w_down):
    wg = w_in[layer][:, 11 * 512:].reshape(NCK, 128, 4, NCK, 128)
    wg = np.ascontiguousarray(wg.transpose(3, 2, 1, 0, 4)).reshape(NCK, 4, 128, NCK * 128)
    wb = w_branch[layer].reshape(4, 4, 128, NCK, 128)
    wb = np.ascontiguousarray(wb.transpose(3, 0, 2, 1, 4)).reshape(NCK, 4, 128, 4 * 128)
    wo = w_o[layer].reshape(NCK, 128, NCK, 128)
    wo = np.ascontiguousarray(wo.transpose(2, 1, 0, 3)).reshape(NCK, 128, NCK * 128)
    wu = w_up[layer].reshape(NCK, 128, 32, 128)
    wu = np.ascontiguousarray(wu.transpose(2, 1, 0, 3)).reshape(32, 128, NCK * 128)
    wd = w_down[layer].reshape(32, 128, NCK, 128)
    wd = np.ascontiguousarray(wd.transpose(2, 1, 0, 3)).reshape(NCK, 128, 32 * 128)
    return {"wg": wg, "wb": wb, "wo": wo, "wu": wu, "wd": wd}


def c_halves(ntok_x):
    tiles = [(0, NMETA)] + [(NMETA + TT * i, TT) for i in range(ntok_x // TT)]
    nh = (len(tiles) + 1) // 2
    return [tiles[:nh], tiles[nh:]] if len(tiles) > nh else [tiles]


def build_C(ntok_x, layer, final):
    nc = bass.Bass("TRN2", target_bir_lowering=False)
    NT = NMETA + ntok_x
    x = nc.dram_tensor("x", [D, NT], F32, kind="ExternalInput").ap()
    br = nc.dram_tensor("brc", [16, 128, NT], BF16, kind="ExternalInput").ap()
    g1 = nc.dram_tensor("g1", [128, 8], F32, kind="ExternalInput").ap()
    g2 = nc.dram_tensor("g2", [128, 8], F32, kind="ExternalInput").ap()
    gf = nc.dram_tensor("gf", [128, 8], F32, kind="ExternalInput").ap()
    cst = nc.dram_tensor("cst", [128, 10 * 128], F32, kind="ExternalInput").ap()
    wg = nc.dram_tensor("wg", [NCK, 4, 128, NCK * 128], F32, kind="ExternalInput").ap()
    wb = nc.dram_tensor("wb", [NCK, 4, 128, 4 * 128], F32, kind="ExternalInput").ap()
    wo = nc.dram_tensor("wo", [NCK, 128, NCK * 128], F32, kind="ExternalInput").ap()
    wu = nc.dram_tensor("wu", [32, 128, NCK * 128], F32, kind="ExternalInput").ap()
    wd = nc.dram_tensor("wd", [NCK, 128, 32 * 128], F32, kind="ExternalInput").ap()
    xo = nc.dram_tensor("xo", [D, NT], F32, kind="ExternalOutput").ap()
    P = Prog(nc)
    with ExitStack() as es:
        cx = Ctx(nc, P, es)
        emit_C(cx, layer, c_halves(ntok_x), lambda off, T: x[:, off:off + T],
               lambda off, T: br[:, :, off:off + T].rearrange("c p t -> p c t"), g1, g2, gf, cst, wg, wb, wo, wu, wd,
               lambda off, T: xo[:, off:off + T], final, pfx="c_")
        P.emit()
    return nc


def _r8(v):
    return np.ascontiguousarray(np.asarray(v, np.float32).reshape(NCK, 128).T)


def kernel_unfused(x, meta_tokens, lb_logits, norm1_g, w_in, hg_norm_g, pool_w, pool_scale, conv_w,
           w_branch, w_o, norm2_g, w_up, w_down, final_norm_g):
    f = lambda a: np.asarray(a, dtype=np.float32)
    x, meta_tokens, lb_logits, norm1_g, w_in = f(x), f(meta_tokens), f(lb_logits), f(norm1_g), f(w_in)
    hg_norm_g, pool_w, pool_scale, conv_w = f(hg_norm_g), f(pool_w), f(pool_scale), f(conv_w)
    w_branch, w_o, norm2_g, w_up, w_down, final_norm_g = f(w_branch), f(w_o), f(norm2_g), f(w_up), f(w_down), f(final_norm_g)
    B_, S, _ = x.shape
    NQ = 8 // B_
    SQ = S // NQ
    cores = list(range(8))
    xT = [np.ascontiguousarray(np.concatenate([meta_tokens, x[b]], axis=0).T) for b in range(B_)]
    cst2 = np.ascontiguousarray(host_consts(2)[1])
    for layer in range(DEPTH):
        ncB = build_B(S, layer)
        in_maps = []
        for r in cores:
            b, j = r // NQ, r % NQ
            im = prep_B_inputs(layer, j, w_in, norm1_g, lb_logits, hg_norm_g, pool_w, pool_scale, conv_w)
            im["xT"] = xT[b]
            in_maps.append(im)
        resB = run_bass_kernel_spmd(ncB, in_maps, core_ids=cores).results
        brs = [np.asarray(resB[r]["br"]) for r in cores]
        final = layer == DEPTH - 1
        ncC = build_C(SQ, layer, final)
        wts = prep_C_weights(layer, w_in, w_branch, w_o, w_up, w_down)
        g1, g2, gf = _r8(norm1_g[layer]), _r8(norm2_g[layer]), _r8(final_norm_g)
        in_maps = []
        for r in cores:
            b, q = r // NQ, r % NQ
            cols = np.concatenate([np.arange(NMETA), NMETA + q * SQ + np.arange(SQ)])
            im = dict(wts)
            im["x"] = np.ascontiguousarray(xT[b][:, cols])
            brc = np.empty((16, 128, NMETA + SQ), dtype=brs[0].dtype)
            for n in range(4):
                for j in range(4):
                    brc[j * 4 + n] = brs[b * NQ + j][n * 128:(n + 1) * 128][:, cols]
            im["brc"] = brc
            im["g1"], im["g2"], im["gf"], im["cst"] = g1, g2, gf, cst2
            in_maps.append(im)
        resC = run_bass_kernel_spmd(ncC, in_maps, core_ids=cores).results
        for b in range(B_):
            new = np.empty_like(xT[b])
            new[:, 0:NMETA] = np.asarray(resC[b * NQ]["xo"])[:, 0:NMETA]
            for q in range(NQ):
                new[:, NMETA + q * SQ: NMETA + (q + 1) * SQ] = np.asarray(resC[b * NQ + q]["xo"])[:, NMETA:]
            xT[b] = new
    out = np.stack([np.ascontiguousarray(xT[b][:, NMETA:].T) for b in range(B_)], axis=0)
    return out.astype(np.float32)


I32 = mybir.dt.int32
GROUPS = [[0, 1, 2, 3], [4, 5, 6, 7]]


def build_fused(S, depth=DEPTH):
    nc = bass.Bass("TRN2", target_bir_lowering=False)
    NQ = 4
    SQ = S // NQ
    NK = SQ // TT
    NT = NMETA + SQ
    Ltot = NMETA + S
    NTB = S // TT
    ext = lambda name, shape, dt=F32: nc.dram_tensor(name, shape, dt, kind="ExternalInput").ap()
    x0 = ext("x0", [D, NT])
    qcol = ext("qcol", [1, 8], I32)
    lbl = ext("lbl", [128, 4])
    cstB = ext("cstB", [128, 10 * 128])
    gf = ext("gf", [128, 8])
    L = []
    for l in range(depth):
        L.append(dict(
            wB=ext("wB%d" % l, [D, 11 * 128]), g1=ext("g1_%d" % l, [128, 8]), vec=ext("vec%d" % l, [128, 8]),
            pw=ext("pw%d" % l, [128, 128]), g2=ext("g2_%d" % l, [128, 8]),
            wg=ext("wg%d" % l, [NCK, 4, 128, NCK * 128]), wb=ext("wb%d" % l, [NCK, 4, 128, 4 * 128]),
            wo=ext("wo%d" % l, [NCK, 128, NCK * 128]), wu=ext("wu%d" % l, [32, 128, NCK * 128]),
            wd=ext("wd%d" % l, [NCK, 128, 32 * 128])))
    xo = nc.dram_tensor("xo", [D, NT], F32, kind="ExternalOutput").ap()
    xm = nc.dram_tensor("xm_i", [D, NMETA], F32).ap()
    xx = nc.dram_tensor("xx_i", [NK, 2, 512, TT], F32).ap()
    xg = nc.dram_tensor("xg_i", [NK, 2, NQ * 512, TT], F32).ap()
    brm = nc.dram_tensor("brm_i", [512, NMETA], BF16).ap()
    brx = nc.dram_tensor("brx_i", [NTB, 512, TT], BF16).ap()
    brgm = nc.dram_tensor("brgm_i", [NQ * 512, NMETA], BF16).ap()
    brgx = nc.dram_tensor("brgx_i", [NTB, NQ * 512, TT], BF16).ap()

    def ag(P, src, dst, r, w, key):
        P.op("pool", lambda e: e.collective_compute("AllGather", ALU.bypass, replica_groups=GROUPS,
                                                    ins=[src.opt()], outs=[dst.opt()]),
             reads=r, writes=w, dma_key=key, inc=1)

    def x_tile_ap(k_):
        return xx[k_].rearrange("h r t -> (h r) t")

    def x_in(off, T):
        return xm if off == 0 else x_tile_ap((off - NMETA) // TT)

    def x_loader(cx, ti, off, T, buf, bkey, semkey):
        if ti == 0:
            cx.dma("sp", buf[:, :, 0:T], xm.rearrange("(c p) t -> p c t", p=128), ["xm"], [bkey], semkey)
            return
        g = (ti - 1) * TT
        q, k_ = g // SQ, (g % SQ) // TT
        for h in range(2):
            src = xg[k_, h, q * 512:(q + 1) * 512, :].rearrange("(c p) t -> p c t", p=128)
            cx.dma("sp", buf[:, h * 4:(h + 1) * 4, 0:T], src, [("xg", k_, h)], [bkey], semkey)

    def br_out(off, T):
        return brm if off == 0 else brx[(off - NMETA) // TT]

    halves = c_halves(SQ)
    vals = {}
    es_glob = ExitStack()
    state = None
    for l in range(depth):
        P = Prog(nc, state)
        state = P.state
        with ExitStack() as es:
            cx = Ctx(nc, P, es)
            if l == 0:
                cx.dma("sp", xm, x0[:, 0:NMETA], [], ["xm"], "cpm")
                for k_ in range(NK):
                    cx.dma("sp", x_tile_ap(k_), x0[:, NMETA + k_ * TT:NMETA + (k_ + 1) * TT], [], [("xx", k_)], "cpx%d" % (k_ % 2))
            if l == 0:
                for k_ in range(NK):
                    for h in range(2):
                        ag(P, xx[k_, h], xg[k_, h], [("xx", k_)], [("xg", k_, h)], "ccx")

            def after_store(cx_, ti):
                if ti == 0:
                    ag(cx_.P, brm, brgm, [("brout_d", 0)], [("brg", 0)], "ccb")
                else:
                    ag(cx_.P, brx[ti - 1], brgx[ti - 1], [("brout_d", ti)], [("brg", ti)], "ccb")

            emit_B(cx, S, l, None, L[l]["wB"], L[l]["g1"], lbl, L[l]["vec"], L[l]["pw"], cstB, br_out,
                   pfx="b%d_" % l, x_loader=x_loader, after_store=after_store)
            P.op("sp", lambda e: e.dma_start(out=brm[0:1, 0:2], in_=brm[0:1, 0:2]),
                 reads=[("brg", t_) for t_ in range(NTB + 1)], writes=[], dma_key="fin", is_out=True)
            P.emit()
        P = Prog(nc, state)

        br_eng = "sp" if l < 2 else "act"

        def init_eng(handle, es2, br_eng=br_eng):
            if br_eng in vals:
                return
            vals[br_eng] = {}
            for kk in range(NK):
                reg = es_glob.enter_context(handle.register("qreg_%s%d" % (br_eng, kk)))
                handle.reg_load(reg, qcol[0:1, kk:kk + 1])
                vals[br_eng][kk] = handle.snap(reg, min_val=0, max_val=NTB - 1)

        P.init[br_eng] = init_eng

        def br_in(off, T, br_eng=br_eng):
            if off == 0:
                return brgm.rearrange("(c p) t -> p c t", p=128)
            kk = (off - NMETA) // TT
            return lambda: brgx[bass.ds(vals[br_eng][kk], 1)].rearrange("o (c p) t -> p (o c) t", p=128)

        final = l == depth - 1

        def after_xout(cx_, off):
            if off == 0:
                return
            k_ = (off - NMETA) // TT
            for h in range(2):
                cx_.P.op("pool", lambda e, k_=k_, h=h: e.collective_compute(
                    "AllGather", ALU.bypass, replica_groups=GROUPS, ins=[xx[k_, h].opt()], outs=[xg[k_, h].opt()]),
                    reads=[("xout", off)], writes=[("xg", k_, h)], dma_key="ccx", inc=1, is_out=True)

        with ExitStack() as es:
            cx = Ctx(nc, P, es)
            emit_C(cx, l, halves, x_in, br_in, L[l]["g1"], L[l]["g2"], gf, cstB,
                   L[l]["wg"], L[l]["wb"], L[l]["wo"], L[l]["wu"], L[l]["wd"],
                   (lambda off, T: xo[:, off:off + T]) if final else x_in, final, pfx="c%d_" % l, br_eng=br_eng,
                   after_xout=None if final else after_xout)
            P.emit()
    return nc


def fused_inputs(x, meta_tokens, lb_logits, norm1_g, w_in, hg_norm_g, pool_w, pool_scale, conv_w,
                 w_branch, w_o, norm2_g, w_up, w_down, final_norm_g, depth=DEPTH):
    B_, S, _ = x.shape
    NQ = 8 // B_
    SQ = S // NQ
    gf = _r8(final_norm_g)
    cw = [prep_C_weights(l, w_in, w_branch, w_o, w_up, w_down) for l in range(depth)]
    in_maps = []
    for r in range(8):
        b, q = r // NQ, r % NQ
        im = {}
        im["x0"] = np.ascontiguousarray(np.concatenate([meta_tokens, x[b, q * SQ:(q + 1) * SQ]], axis=0).T)
        qc = np.zeros((1, 8), np.int32)
        for kk in range(SQ // TT):
            qc[0, kk] = q * (SQ // TT) + kk
        im["qcol"] = qc
        im["gf"] = gf
        for l in range(depth):
            pb = prep_B_inputs(l, q, w_in, norm1_g, lb_logits, hg_norm_g, pool_w, pool_scale, conv_w)
            im["wB%d" % l], im["g1_%d" % l], im["vec%d" % l], im["pw%d" % l] = pb["w"], pb["g1"], pb["vec"], pb["pw"]
            im["lbl"], im["cstB"] = pb["lbl"], pb["cst"]
            im["g2_%d" % l] = _r8(norm2_g[l])
            for kname in ("wg", "wb", "wo", "wu", "wd"):
                im["%s%d" % (kname, l)] = cw[l][kname]
        in_maps.append(im)
    return in_maps


def kernel(x, meta_tokens, lb_logits, norm1_g, w_in, hg_norm_g, pool_w, pool_scale, conv_w,
           w_branch, w_o, norm2_g, w_up, w_down, final_norm_g):
    f = lambda a: np.asarray(a, dtype=np.float32)
    args = [f(a) for a in (x, meta_tokens, lb_logits, norm1_g, w_in, hg_norm_g, pool_w, pool_scale, conv_w,
                           w_branch, w_o, norm2_g, w_up, w_down, final_norm_g)]
    x = args[0]
    B_, S, _ = x.shape
    NQ = 8 // B_
    SQ = S // NQ
    nc = build_fused(S)
    in_maps = fused_inputs(*args)
    res = run_bass_kernel_spmd(nc, in_maps, core_ids=list(range(8))).results
    out = np.empty((B_, S, D), np.float32)
    for r in range(8):
        b, q = r // NQ, r % NQ
        out[b, q * SQ:(q + 1) * SQ] = np.asarray(res[r]["xo"])[:, NMETA:].T
    return out
```

```python
from contextlib import ExitStack
import numpy as np
import ml_dtypes
import concourse.bass as bass
import concourse.mybir as mybir
from concourse.bass_utils import run_bass_kernel_spmd

F32 = mybir.dt.float32
BF16 = mybir.dt.bfloat16
ALU = mybir.AluOpType
AF = mybir.ActivationFunctionType

D = 1024
NCK = 8
NMETA = 16
TT = 512
DEPTH = 4
EPS = 1e-6
POOL_WINDOWS = (2, 4, 8, 16)
SEM_CH = 30000
PARTS = {"conv", "pool", "hgrn", "attn"}
NFILL = 4


class Prog:
    ENGS = ("pe", "act", "dve", "pool", "sp")

    def __init__(self, nc, state=None):
        self.nc = nc
        self.ops = {e: [] for e in self.ENGS}
        self.state = state if state is not None else {"cnt": {e: 0 for e in self.ENGS}, "dma_cnt": {}, "handles": {},
                                                      "es": ExitStack()}
        self.cnt = self.state["cnt"]
        self.lastw = {}
        self.readers = {}
        self.known = {e: {} for e in self.ENGS}
        self.dma_cnt = self.state["dma_cnt"]
        self.semkeys = []
        self.semset = set()
        self.out_tokens = []
        self.init = {}
        self.excl = set(["M0", "M1", "Z0", "Z1", "G", "CS", "O", "TB"] + ["cp%d" % i for i in range(8)])

    def _sem(self, key):
        if key not in self.semset:
            self.semset.add(key)
            self.semkeys.append(key)
        return key

    LIMIT = None
    nops = 0

    def op(self, eng, fn, reads=(), writes=(), dma_key=None, is_out=False, inc=16):
        Prog.nops += 1
        if Prog.LIMIT is not None and Prog.nops > Prog.LIMIT and not is_out:
            return None
        deps = []
        for k in reads:
            t = self.lastw.get(k)
            if t is not None:
                deps.append(t)
            if k in self.excl:
                deps.extend(r for r in self.readers.get(k, ()) if r[2] != eng)
        for k in writes:
            t = self.lastw.get(k)
            if t is not None:
                deps.append(t)
            deps.extend(self.readers.get(k, ()))
        waits = {}
        kn = self.known[eng]
        for (sk, val, deng) in deps:
            if deng == eng and dma_key is None and eng == "pe":
                continue
            if kn.get(sk, 0) >= val:
                continue
            if waits.get(sk, 0) < val:
                waits[sk] = val
        for sk, val in waits.items():
            kn[sk] = val
        if dma_key is not None:
            sk = self._sem(("dma", dma_key))
            n = self.dma_cnt.get(sk, 0) + 1
            self.dma_cnt[sk] = n
            done = (sk, inc * n, "dma")
        else:
            idx = self.cnt[eng]
            self.cnt[eng] += 1
            sk = self._sem(("eng", eng, idx // SEM_CH))
            done = (sk, idx % SEM_CH + 1, eng)
            inc = 1
        self.ops[eng].append((list(waits.items()), fn, sk, inc))
        for k in writes:
            self.lastw[k] = done
            self.readers[k] = []
        for k in reads:
            self.readers.setdefault(k, []).append(done)
        if is_out:
            self.out_tokens.append(done)
        return done

    def emit(self):
        nc = self.nc
        final_waits = {}
        for (sk, val, _e) in self.out_tokens:
            if final_waits.get(sk, 0) < val:
                final_waits[sk] = val
        with ExitStack() as es:
            sems = self.state["handles"]
            for sk in self.semkeys:
                if sk not in sems:
                    sems[sk] = self.state["es"].enter_context(nc.semaphore("s%d" % len(sems)))
            block = es.enter_context(nc.Block())

            def run(engname, handle):
                if engname in self.init:
                    self.init[engname](handle, es)
                for (waits, fn, sk, inc) in self.ops[engname]:
                    for wk, wv in waits:
                        handle.wait_ge(sems[wk], wv)
                    fn(handle).then_inc(sems[sk], inc)
                if engname == "sp":
                    for wk, wv in final_waits.items():
                        handle.wait_ge(sems[wk], wv)

            @block.tensor
            def _(e):
                run("pe", e)

            @block.scalar
            def _(e):
                run("act", e)

            @block.vector
            def _(e):
                run("dve", e)

            @block.gpsimd
            def _(e):
                run("pool", e)

            @block.sync
            def _(e):
                run("sp", e)


class Ctx:
    def __init__(self, nc, P, es):
        self.nc, self.P, self.es = nc, P, es

    def sb(self, name, shape, dt):
        return self.es.enter_context(self.nc.sbuf_tensor(name, shape, dt))

    def ps(self, name, shape, dt=F32):
        return self.es.enter_context(self.nc.psum_tensor(name, shape, dt))

    def mm(self, out, lhsT, rhs, start, stop, r, w, skip=False):
        if skip:
            self.P.op("pe", lambda e: e.matmul(out, lhsT, rhs, start=start, stop=stop, skip_group_check=True),
                      reads=r, writes=w)
        else:
            self.P.op("pe", lambda e: e.matmul(out, lhsT, rhs, start=start, stop=stop), reads=r, writes=w)

    def tr(self, out, in_, ident, r, w):
        self.P.op("pe", lambda e: e.transpose(out, in_, ident), reads=r, writes=w)

    def act(self, out, in_, func, r, w, bias=None, scale=None):
        kw = {}
        if bias is not None:
            kw["bias"] = bias
        if scale is not None:
            kw["scale"] = scale
        self.P.op("act", lambda e: e.activation(out=out, in_=in_, func=func, **kw), reads=r, writes=w)

    def tt(self, eng, out, in0, in1, op, r, w):
        self.P.op(eng, lambda e: e.tensor_tensor(out=out, in0=in0, in1=in1, op=op), reads=r, writes=w)

    def ts(self, eng, out, in0, s1, op0, r, w, s2=None, op1=None):
        if op1 is None:
            self.P.op(eng, lambda e: e.tensor_scalar(out=out, in0=in0, scalar1=s1, scalar2=None, op0=op0),
                      reads=r, writes=w)
        else:
            self.P.op(eng, lambda e: e.tensor_scalar(out=out, in0=in0, scalar1=s1, scalar2=s2, op0=op0, op1=op1),
                      reads=r, writes=w)

    def stt(self, eng, out, in0, scalar, in1, op0, op1, r, w):
        self.P.op(eng, lambda e: e.scalar_tensor_tensor(out=out, in0=in0, scalar=scalar, in1=in1, op0=op0, op1=op1),
                  reads=r, writes=w)

    def copy(self, eng, out, in_, r, w):
        if eng == "act":
            self.P.op("act", lambda e: e.activation(out=out, in_=in_, func=AF.Copy), reads=r, writes=w)
        else:
            self.P.op(eng, lambda e: e.tensor_copy(out=out, in_=in_), reads=r, writes=w)

    def recip(self, out, in_, r, w):
        self.P.op("dve", lambda e: e.reciprocal(out=out, in_=in_), reads=r, writes=w)

    def memset(self, eng, ap, val, w):
        self.P.op(eng, lambda e: e.memset(ap, val), writes=w)

    def dma(self, eng, out, in_, r, w, key, is_out=False):
        self.P.op(eng, lambda e: e.dma_start(out=out, in_=(in_() if callable(in_) else in_)), reads=r, writes=w,
                  dma_key=key, is_out=is_out)


def seq_tiles(S):
    return [(0, NMETA)] + [(NMETA + TT * i, TT) for i in range(S // TT)]


def gblock(g):
    return (0, NMETA) if g == 0 else (NMETA + 128 * (g - 1), 128)


def host_consts(window):
    i = np.arange(128)
    c = {}
    c["ident"] = np.eye(128, dtype=np.float32)
    c["ones"] = np.ones((128, 128), np.float32)
    c["strict"] = (i[:, None] < i[None, :]).astype(np.float32)
    c["negun"] = -(i[:, None] >= i[None, :]).astype(np.float32)
    c["triu"] = (i[:, None] <= i[None, :]).astype(np.float32)
    w = window
    s, t = i[:, None], i[None, :]
    band = ((s <= t) & (s > t - w)).astype(np.float32)
    c["mdiag"] = band / w - np.eye(128, dtype=np.float32)
    c["moff"] = ((s - 128) > (t - w)).astype(np.float32) / w
    mf = np.zeros((128, 128), np.float32)
    mf[:16] = (s[:16] > (16 + t - w)).astype(np.float32) / w
    c["mofff"] = mf
    bm = np.zeros((128, 128), np.float32)
    bm[:16, :16] = band[:16, :16]
    c["bmeta"] = bm
    invn = np.zeros((128, 128), np.float32)
    invn[:, :16] = 1.0 / np.minimum(w, np.arange(16) + 1.0)[None, :]
    c["invn"] = invn
    names = ["ident", "ones", "strict", "negun", "triu", "mdiag", "moff", "mofff", "bmeta", "invn"]
    return names, np.concatenate([c[n] for n in names], axis=1)


CONST_NAMES = ["ident", "ones", "strict", "negun", "triu", "mdiag", "moff", "mofff", "bmeta", "invn"]


def emit_B(cx, S, layer, xT_ap, w_ap, g1_ap, lbl_ap, vec_ap, pw_ap, cst_ap, out_ap, pfx="", x_loader=None,
           after_store=None):
    P = cx.P
    Ltot = NMETA + S
    tiles = seq_tiles(S)
    NB = 1 + S // 128
    k = lambda s: pfx + s
    sk = lambda s: "B." + s

    w_bf = cx.sb(k("w_bf"), [128, NCK, 11 * 128], BF16)
    cst_f = cx.sb(k("cst_f"), [128, 10 * 128], F32)
    cst_b = cx.sb(k("cst_b"), [128, 10 * 128], BF16)
    g1 = cx.sb(k("g1"), [128, 8], F32)
    lbl = cx.sb(k("lbl"), [128, 4], F32)
    vec = cx.sb(k("vec"), [128, 8], F32)
    pw_b = cx.sb(k("pw_b"), [128, 128], BF16)
    sm = cx.sb(k("sm"), [128, 16], F32)
    kT = cx.sb(k("kT"), [128, Ltot], BF16)
    vc = cx.sb(k("vc"), [128, NB, 128], BF16)
    Sst = cx.sb(k("Sst"), [128, 128], F32)
    Ssc = cx.sb(k("Ssc"), [128, 128], BF16)
    ubuf = cx.sb(k("ubuf"), [128, TT + 2], F32)
    pv = cx.sb(k("pv"), [128, 5, 128], BF16)
    onesf = cx.sb(k("onesf"), [128, TT], F32)
    xt = [cx.sb(k("xt%d" % i), [128, NCK, TT], F32) for i in range(2)]
    sqb = cx.sb(k("sqb"), [128, NCK, TT], BF16)
    hT = cx.sb(k("hT"), [128, NCK, TT], BF16)
    rstd = cx.sb(k("rstd"), [128, TT], F32)
    proj = cx.sb(k("proj"), [128, 8, TT], F32)
    qT = cx.sb(k("qT"), [128, TT], BF16)
    itok = cx.sb(k("itok"), [128, 4, 128], BF16)
    brout = [cx.sb(k("brout%d" % i), [128, 4, TT], BF16) for i in range(2)]
    t1 = cx.sb(k("t1"), [128, TT], F32)
    t2 = cx.sb(k("t2"), [128, TT], F32)
    fval = cx.sb(k("fval"), [128, TT], F32)
    kk = cx.sb(k("kk"), [128, TT], F32)
    lf = cx.sb(k("lf"), [128, TT], F32)
    Bc = cx.sb(k("Bc"), [128, TT + 1], F32)
    cex = [cx.sb(k("cex%d" % i), [128, 64], F32) for i in range(3)]
    qe = cx.sb(k("qe"), [128, TT], BF16)
    ke = cx.sb(k("ke"), [128, TT], BF16)
    kdT = cx.sb(k("kdT"), [128, TT], BF16)
    kdtok = cx.sb(k("kdtok"), [128, 4, 128], BF16)
    scm = cx.sb(k("scm"), [128, 64], BF16)
    dsm = cx.sb(k("dsm"), [128, 4, 8], F32)
    osb = cx.sb(k("osb"), [128, TT], F32)
    osq = cx.sb(k("osq"), [128, TT], BF16)
    uTb = cx.sb(k("uTb"), [128, 128], BF16)
    uTf = cx.sb(k("uTf"), [128, 16], F32)
    Ef2 = cx.sb(k("Ef2"), [128, 2 * TT], F32)
    spb2 = cx.sb(k("spb2"), [128, 2 * TT], BF16)
    lg2 = cx.sb(k("lg2"), [128, 2 * TT], F32)
    Ab2 = cx.sb(k("Ab2"), [128, 2 * TT], BF16)
    Ef = [Ef2[:, i * TT:(i + 1) * TT] for i in range(2)]
    spb = [spb2[:, i * TT:(i + 1) * TT] for i in range(2)]
    lg = [lg2[:, i * TT:(i + 1) * TT] for i in range(2)]
    Ab = [Ab2[:, i * TT:(i + 1) * TT] for i in range(2)]
    h3 = lambda t_: t_[:, :].rearrange("p (h t) -> p h t", h=2)
    carry = [cx.sb(k("carry%d" % i), [128, TT], F32) for i in range(2)]
    ob = cx.sb(k("ob"), [128, 4 * 128], BF16)
    zbf = cx.sb(k("zbf"), [128, TT], BF16)
    M = [cx.ps(k("pM%d" % i), [128, TT]) for i in range(2)]
    Z = [cx.ps(k("pZ%d" % i), [128, TT]) for i in range(2)]
    G = cx.ps(k("pG"), [128, TT])
    CS = cx.ps(k("pCS"), [128, TT])
    O = cx.ps(k("pO"), [128, TT])
    TB = cx.ps(k("pTB"), [128, 2 * TT], BF16)

    def C(name, rows=128, cols=128, bf=True):
        i = CONST_NAMES.index(name)
        src = cst_b if bf else cst_f
        return src[0:rows, i * 128:i * 128 + cols]

    cx.dma("sp", cst_f[:], cst_ap, [], ["cst_f"], sk("cst_f"))
    cx.dma("pool", cst_b[:], cst_ap, [], ["cst_b"], sk("cst_b"))
    cx.dma("sp", g1[:], g1_ap, [], ["g1"], sk("g1"))
    cx.dma("sp", lbl[:], lbl_ap, [], ["lbl"], sk("lbl"))
    cx.dma("sp", vec[:], vec_ap, [], ["vec"], sk("vec"))
    cx.dma("pool", pw_b[:], pw_ap, [], ["pw_b"], sk("pw_b"))
    for c in range(NCK):
        cx.dma("pool", w_bf[:, c, :], w_ap[c * 128:(c + 1) * 128, :], [], [("w", c)], sk("w%d" % c))
    cx.memset("pool", onesf[:], 1.0, ["onesf"])
    cx.memset("pool", zbf[:], 0.0, ["zbf"])
    cx.memset("pool", Sst[:], 0.0, ["S"])
    cx.memset("pool", ubuf[:, 0:2], 0.0, ["ubuf"])
    cx.memset("pool", Bc[:, 0:1], 0.0, ["Bc"])
    P.op("dve", lambda e: e.reduce_max(out=sm[:, 0:1], in_=lbl[:], axis=mybir.AxisListType.X), reads=["lbl"], writes=["sm"])
    cx.ts("dve", sm[:, 1:2], sm[:, 0:1], -1.0, ALU.mult, ["sm"], ["sm"])
    cx.act(sm[:, 8:12], lbl[:], AF.Exp, ["sm", "lbl"], ["sm"], bias=sm[:, 1:2])
    P.op("dve", lambda e: e.reduce_sum(out=sm[:, 2:3], in_=sm[:, 8:12], axis=mybir.AxisListType.X), reads=["sm"], writes=["sm"])
    cx.recip(sm[:, 3:4], sm[:, 2:3], ["sm"], ["sm"])
    if layer == 0:
        cx.memset("dve", sm[:, 4:5], 0.0, ["sm"])
    else:
        P.op("dve", lambda e: e.reduce_sum(out=sm[:, 4:5], in_=sm[:, 9:9 + layer], axis=mybir.AxisListType.X), reads=["sm"], writes=["sm"])
        cx.tt("dve", sm[:, 4:5], sm[:, 4:5], sm[:, 3:4], ALU.mult, ["sm"], ["sm"])
    cx.ts("dve", sm[:, 5:6], sm[:, 4:5], -1.0, ALU.mult, ["sm"], ["sm"], s2=1.0, op1=ALU.add)
    lb_ap, oml_ap = sm[:, 4:5], sm[:, 5:6]

    def load_x(ti):
        off, T = tiles[ti]
        buf = xt[ti % 2]
        if x_loader is not None:
            x_loader(cx, ti, off, T, buf, ("xt", ti % 2), sk("xt%d" % (ti % 2)))
            return
        cx.dma("sp", buf[:, :, 0:T], xT_ap(off, T).rearrange("(c p) t -> p c t", p=128), ["xg"], [("xt", ti % 2)],
               sk("xt%d" % (ti % 2)))

    load_x(0)
    for ti, (off, T) in enumerate(tiles):
        if ti + 1 < len(tiles):
            load_x(ti + 1)
        x = xt[ti % 2]
        xk = ("xt", ti % 2)
        meta = (ti == 0)
        if meta:
            blocks = [(0, NMETA)]
            gb0 = 0
        else:
            blocks = [(128 * m, 128) for m in range(T // 128)]
            gb0 = 4 * (ti - 1) + 1
        bo = brout[ti % 2]
        bok = ("brout", ti % 2)


        cx.act(sqb[:, :, 0:T], x[:, :, 0:T], AF.Square, [xk], ["sqb"])
        for c in range(NCK):
            cx.mm(M[0][:, 0:T], C("ones"), sqb[:, c, 0:T], c == 0, c == NCK - 1, ["sqb", "cst_b"], ["M0"])
        cx.act(rstd[:, 0:T], M[0][:, 0:T], AF.Ln, ["M0"], ["rstd"], bias=EPS, scale=1.0 / D)
        cx.act(rstd[:, 0:T], rstd[:, 0:T], AF.Exp, ["rstd"], ["rstd"], scale=-0.5)
        for c in range(NCK):
            cx.stt("dve", hT[:, c, 0:T], x[:, c, 0:T], g1[:, c:c + 1], rstd[:, 0:T],
                   ALU.mult, ALU.mult, [xk, "rstd", "g1"], [("hT", c)])
        hTk = [("hT", c) for c in range(NCK)]
        wk = [("w", c) for c in range(NCK)]

        for n in range(8):
            pm = M[n % 2]
            pk = "M%d" % (n % 2)
            for c in range(NCK):
                cx.mm(pm[:, 0:T], w_bf[:, c, n * 128:(n + 1) * 128], hT[:, c, 0:T], c == 0, c == NCK - 1,
                      hTk + wk, [pk])
            if n == 3:
                cx.copy("act", qT[:, 0:T], pm[:, 0:T], [pk], ["qT"])
            elif n == 4:
                cx.act(kT[:, off:off + T], pm[:, 0:T], AF.Copy, [pk], [("kT", ti)], scale=0.125)
            else:
                cx.copy("act", proj[:, n, 0:T], pm[:, 0:T], [pk], [("proj", n)])

        for m, (bs, bl) in enumerate(blocks):
            pm = M[m % 2]
            pk = "M%d" % (m % 2)
            for c in range(NCK):
                cx.mm(pm[0:bl, 0:384], hT[:, c, bs:bs + bl], w_bf[:, c, 1024:1408], c == 0, c == NCK - 1,
                      hTk + wk, [pk])
            cx.copy("act", itok[0:bl, m, :], pm[0:bl, 0:128], [pk], [("itok", m)])
            cx.copy("dve", pv[0:bl, 1 + m, :], pm[0:bl, 128:256], [pk], [("pv", 1 + m)])
            cx.copy("act", vc[0:bl, gb0 + m, :], pm[0:bl, 256:384], [pk], [("vc", gb0 + m)])


        if "conv" in PARTS:
            cx.tt("pool", ubuf[:, 2:2 + T], proj[:, 7, 0:T], proj[:, 5, 0:T], ALU.mult, [("proj", 7), ("proj", 5)], ["ubuf"])
            cx.ts("pool", t2[:, 0:T], ubuf[:, 2:2 + T], vec[:, 4:5], ALU.mult, ["ubuf", "vec"], ["t2"])
            cx.stt("dve", t2[:, 0:T], ubuf[:, 1:1 + T], vec[:, 3:4], t2[:, 0:T], ALU.mult, ALU.add, ["ubuf", "vec", "t2"], ["t2"])
            cx.stt("dve", t2[:, 0:T], ubuf[:, 0:T], vec[:, 2:3], t2[:, 0:T], ALU.mult, ALU.add, ["ubuf", "vec", "t2"], ["t2"])
            cx.tt("pool", bo[:, 3, 0:T], t2[:, 0:T], proj[:, 6, 0:T], ALU.mult, ["t2", ("proj", 6)], [bok])
            cx.copy("pool", ubuf[:, 0:2], ubuf[:, T:T + 2], ["ubuf"], ["ubuf"])

        if "pool" in PARTS:
            for m, (bs, bl) in enumerate(blocks):
                pm = M[m % 2]
                pk = "M%d" % (m % 2)
                if meta:
                    cx.mm(pm[:, 0:16], pv[0:16, 1, :], C("bmeta", 16, 16), True, True, [("pv", 1), "cst_b"], [pk])
                    cx.mm(pm[:, 16:32], pv[0:16, 1, :], C("ident", 16, 16), True, True, [("pv", 1), "cst_b"], [pk])
                    cx.tt("dve", uTf[:, 0:16], pm[:, 0:16], C("invn", 128, 16, bf=False), ALU.mult, [pk, "cst_f"], ["uTf"])
                    cx.tt("dve", uTb[:, 0:16], uTf[:, 0:16], pm[:, 16:32], ALU.subtract, [pk, "uTf"], ["uTb"])
                else:
                    prev_rows = 16 if (ti == 1 and m == 0) else 128
                    moff = C("mofff", 16, 128) if (ti == 1 and m == 0) else C("moff")
                    cx.mm(pm[:, 0:128], pv[:, 1 + m, :], C("mdiag"), True, False, [("pv", 1 + m), "cst_b"], [pk])
                    cx.mm(pm[:, 0:128], pv[0:prev_rows, m, :], moff, False, True, [("pv", m), "cst_b"], [pk])
                    cx.copy("dve", uTb[:, 0:bl], pm[:, 0:bl], [pk], ["uTb"])
                cx.mm(pm[:, 128:128 + bl], pw_b[:], uTb[:, 0:bl], True, True, ["uTb", "pw_b"], [pk])
                cx.act(bo[:, 1, bs:bs + bl], pm[:, 128:128 + bl], AF.Copy, [pk, "vec"], [bok], scale=vec[:, 1:2])
            lastm = len(blocks) - 1
            lbl_rows = blocks[lastm][1]
            cx.copy("pool", pv[0:lbl_rows, 0, :], pv[0:lbl_rows, 1 + lastm, :], [("pv", 1 + lastm), ("pv", 0)], [("pv", 0)])

        if "hgrn" in PARTS:
            cx.act(t1[:, 0:T], proj[:, 1, 0:T], AF.Exp, [("proj", 1)], ["t1"], scale=-1.0)
            cx.ts("dve", t1[:, 0:T], t1[:, 0:T], 1.0, ALU.add, ["t1"], ["t1"])
            cx.recip(t1[:, 0:T], t1[:, 0:T], ["t1"], ["t1"])
            cx.ts("dve", fval[:, 0:T], t1[:, 0:T], oml_ap, ALU.mult, ["t1", "sm"], ["fval"], s2=lb_ap, op1=ALU.add)
            cx.act(lf[:, 0:T], fval[:, 0:T], AF.Ln, ["fval"], ["lf"])
            cx.ts("pool", kk[:, 0:T], fval[:, 0:T], -1.0, ALU.mult, ["fval"], ["kk"], s2=1.0, op1=ALU.add)
            P.op("dve", lambda e, T=T: e.tensor_tensor_scan(out=Bc[:, 1:1 + T], data0=onesf[:, 0:T], data1=lf[:, 0:T],
                                                            initial=0.0, op0=ALU.mult, op1=ALU.add),
                 reads=["lf", "onesf"], writes=["Bc"])
            CL = 16 if meta else 64
            nch = T // CL
            Bv = Bc[:, 1:1 + T].rearrange("p (c l) -> p c l", l=CL)
            Bp = Bc[:, 0:T].rearrange("p (c l) -> p c l", l=CL)[:, :, 0:1]
            Bm = Bv[:, :, CL // 2 - 1:CL // 2]
            Be = Bv[:, :, CL - 1:CL]
            v3 = lambda t_: t_[:, 0:T].rearrange("p (c l) -> p c l", l=CL)
            cx.tt("dve", dsm[:, 2, 0:nch].rearrange("p (c o) -> p c o", o=1), Bm, Bp, ALU.subtract, ["Bc"], ["dsm"])
            cx.tt("dve", dsm[:, 3, 0:nch].rearrange("p (c o) -> p c o", o=1), Be, Bp, ALU.subtract, ["Bc"], ["dsm"])
            cx.act(dsm[:, 0:2, 0:nch], dsm[:, 2:4, 0:nch], AF.Exp, ["dsm"], ["dsm"])
            cx.tt("dve", v3(lf), Bv, Bm.to_broadcast([128, nch, CL]), ALU.subtract, ["Bc", "lf"], ["lf"])
            cx.tt("dve", v3(fval), Bv, Be.to_broadcast([128, nch, CL]), ALU.subtract, ["Bc", "fval", "kk"], ["fval"])
            cx.act(t1[:, 0:T], lf[:, 0:T], AF.Exp, ["lf"], ["t1"])
            cx.tt("dve", qe[:, 0:T], t1[:, 0:T], proj[:, 0, 0:T], ALU.mult, ["t1", ("proj", 0)], ["qe"])
            cx.act(t2[:, 0:T], lf[:, 0:T], AF.Exp, ["lf"], ["t2"], scale=-1.0)
            cx.tt("pool", ke[:, 0:T], t2[:, 0:T], kk[:, 0:T], ALU.mult, ["t2", "kk"], ["ke"])
            cx.act(t1[:, 0:T], fval[:, 0:T], AF.Exp, ["fval"], ["t1"], scale=-1.0)
            cx.tt("pool", kdT[:, 0:T], t1[:, 0:T], kk[:, 0:T], ALU.mult, ["t1", "kk"], ["kdT"])
            for ci in range(nch):
                c0 = ci * CL
                m = c0 // 128
                r0 = c0 % 128
                cx.tr(TB[r0:r0 + CL, 0:128], kdT[:, c0:c0 + CL], C("ident"), ["kdT", "cst_b"], ["TB"])
                cx.copy("act", kdtok[r0:r0 + CL, m, :], TB[r0:r0 + CL, 0:128], ["TB"], ["kdtok"])
                zb = Z[ci % 2]
                zk = "Z%d" % (ci % 2)
                cx.mm(zb[r0:r0 + CL, 0:CL], ke[:, c0:c0 + CL], qe[:, c0:c0 + CL], True, True, ["ke", "qe"], [zk])
                cx.tt("dve", scm[r0:r0 + CL, 0:CL], zb[r0:r0 + CL, 0:CL], C("triu", 128, 128, bf=False)[r0:r0 + CL, r0:r0 + CL],
                      ALU.mult, [zk, "cst_f"], ["scm"])
                cx.ts("dve", Ssc[:], Sst[:], dsm[:, 0, ci:ci + 1], ALU.mult, ["S", "dsm"], ["Ssc"])
                cx.mm(G[:, c0:c0 + CL], Ssc[:], qe[:, c0:c0 + CL], True, False, ["Ssc", "qe"], ["G"])
                cx.mm(G[:, c0:c0 + CL], itok[r0:r0 + CL, m, :], scm[r0:r0 + CL, 0:CL], False, True, [("itok", m), "scm"], ["G"])
                mb = M[ci % 2]
                mk = "M%d" % (ci % 2)
                cx.mm(mb[:, 0:128], kdtok[r0:r0 + CL, m, :], itok[r0:r0 + CL, m, :], True, True, ["kdtok", ("itok", m)], [mk])
                cx.stt("dve", Sst[:], Sst[:], dsm[:, 1, ci:ci + 1], mb[:, 0:128], ALU.mult, ALU.add, ["S", "dsm", mk], ["S"])
            cx.copy("act", osb[:, 0:T], G[:, 0:T], ["G"], ["osb"])
            cx.act(osq[:, 0:T], osb[:, 0:T], AF.Square, ["osb"], ["osq"])
            cx.mm(CS[:, 0:T], C("ones"), osq[:, 0:T], True, True, ["osq", "cst_b"], ["CS"])
            cx.act(t2[:, 0:T], CS[:, 0:T], AF.Ln, ["CS"], ["t2"], bias=EPS, scale=1.0 / 128)
            cx.act(t2[:, 0:T], t2[:, 0:T], AF.Exp, ["t2"], ["t2"], scale=-0.5)
            cx.act(t1[:, 0:T], proj[:, 2, 0:T], AF.Exp, [("proj", 2)], ["t1"], scale=-1.0)
            cx.ts("dve", t1[:, 0:T], t1[:, 0:T], 1.0, ALU.add, ["t1"], ["t1"])
            cx.recip(t1[:, 0:T], t1[:, 0:T], ["t1"], ["t1"])
            cx.stt("dve", osb[:, 0:T], osb[:, 0:T], vec[:, 0:1], t2[:, 0:T], ALU.mult, ALU.mult, ["osb", "vec", "t2"], ["osb"])
            cx.tt("dve", bo[:, 0, 0:T], osb[:, 0:T], t1[:, 0:T], ALU.mult, ["osb", "t1"], [bok])

        if "attn" in PARTS:
            nsub = len(blocks)
            SW = blocks[0][1]
            last_gb = gb0 + nsub - 1
            cx.mm(O[:, 0:T], zbf[:, 0:128], zbf[:, 0:T], True, False, ["zbf"], ["O"])
            its = list(range(last_gb, -1, -1))
            n_it = len(its)
            for h in range(2):
                cx.memset("pool", carry[h][:, 0:T], 0.0, [("carry", h)])
            Zb = [[(Z[0], "Z0"), (G, "G")], [(Z[1], "Z1"), (M[0], "M0")]]
            CSb = [(CS, "CS"), (M[1], "M1")]

            def geom(i):
                kb = its[i]
                ks, KL = gblock(kb)
                kti = 0 if kb == 0 else (kb - 1) // 4 + 1
                diag = ks >= off
                qc0 = max(off, ks) - off
                return kb, ks, KL, kti, diag, qc0, T - qc0

            def phA1(i, h):
                kb, ks, KL, kti, diag, qc0, N = geom(i)
                hp = slice(64 * h, 64 * h + 64)
                zb, zk = Zb[h][i % 2]
                cx.mm(zb[0:KL, 0:N], kT[hp, ks:ks + KL], qT[hp, qc0:qc0 + N], True, True, [("kT", kti), "qT"], [zk])

            def phA(i, h):
                kb, ks, KL, kti, diag, qc0, N = geom(i)
                zb, zk = Zb[h][i % 2]
                ef, efk = Ef[h], ("Ef", h)
                sp, spk = spb[h], ("spb", h)
                cx.act(ef[0:KL, 0:N], zb[0:KL, 0:N], AF.Exp, [zk], [efk])
                cx.act(sp[0:KL, 0:N], ef[0:KL, 0:N], AF.Ln, [efk], [spk], bias=1.0)
                if diag:
                    cx.tt("pool", sp[0:KL, 0:SW], sp[0:KL, 0:SW], C("strict", KL, SW), ALU.mult, [spk, "cst_b"], [spk])

            def phA_both(i):
                return

                kb, ks, KL, kti, diag, qc0, N = geom(i)
                cx.act(h3(spb2)[0:KL, :, 0:N], h3(Ef2)[0:KL, :, 0:N], AF.Ln, [("Ef", 0), ("Ef", 1)],
                       [("spb", 0), ("spb", 1)], bias=1.0)
                if diag:
                    for h in range(2):
                        sp, spk = spb[h], ("spb", h)
                        cx.tt("pool", sp[0:KL, 0:SW], sp[0:KL, 0:SW], C("strict", KL, SW), ALU.mult, [spk, "cst_b"], [spk])

            def phC_both(i):
                return
                kb, ks, KL, kti, diag, qc0, N = geom(i)
                cx.act(h3(Ab2)[0:KL, :, 0:N], h3(lg2)[0:KL, :, 0:N], AF.Exp, [("lg", 0), ("lg", 1)],
                       [("Ab", 0), ("Ab", 1)])

            def phB(i, h):
                kb, ks, KL, kti, diag, qc0, N = geom(i)
                sp, spk = spb[h], ("spb", h)
                lgt, lgk = lg[h], ("lg", h)
                cr, crk = carry[h], ("carry", h)
                gb, gk = Zb[h][i % 2]
                cb, ck_ = CSb[h]
                cx.mm(gb[0:KL, 0:N], C("negun", KL, KL), sp[0:KL, 0:N], False, True, [spk, "cst_b"], [gk], skip=True)
                if kb > 0:
                    cx.mm(cb[:, 0:N], C("ones", KL, 128), sp[0:KL, 0:N], True, True, [spk, "cst_b"], [ck_])
                cx.tt("dve", lgt[0:KL, 0:N], gb[0:KL, 0:N], cr[0:KL, qc0:qc0 + N], ALU.subtract, [gk, crk], [lgk])
                if kb > 0:
                    cx.tt("dve", cr[:, qc0:qc0 + N], cr[:, qc0:qc0 + N], cb[:, 0:N], ALU.add, [ck_, crk], [crk])

            def phC(i, h):
                kb, ks, KL, kti, diag, qc0, N = geom(i)
                hp = slice(64 * h, 64 * h + 64)
                lgt, lgk = lg[h], ("lg", h)
                ab, abk = Ab[h], ("Ab", h)
                cx.act(ab[0:KL, 0:N], lgt[0:KL, 0:N], AF.Exp, [lgk], [abk])
                if diag:
                    cx.tt("pool", ab[0:KL, 0:SW], ab[0:KL, 0:SW], C("strict", KL, SW), ALU.mult, [abk, "cst_b"], [abk])
                cx.mm(O[hp, qc0:qc0 + N], vc[0:KL, kb, hp], ab[0:KL, 0:N], False, (kb == 0),
                      [abk, ("vc", kb)], ["O"])

            for t in range(n_it + 2):
                for h in range(2):
                    if t < n_it:
                        phA1(t, h)
                if 0 <= t - 2 < n_it:
                    phC_both(t - 2)
                for h in range(2):
                    if 0 <= t - 2 < n_it:
                        phC(t - 2, h)
                for h in range(2):
                    if 0 <= t - 1 < n_it:
                        phB(t - 1, h)
                for h in range(2):
                    if t < n_it:
                        phA(t, h)
                for _ in range(NFILL):
                    cx.mm(TB[:, :].bitcast(F32)[:, 0:T], C("ones"), qT[:, 0:T], True, True, ["qT", "cst_b"], ["TB"])
            cx.copy("dve", bo[:, 2, 0:T], O[:, 0:T], ["O"], [bok])

        cx.dma("sp", out_ap(off, T).rearrange("(n p) t -> p n t", p=128), bo[:, :, 0:T], [bok], [("brout_d", ti)],
               sk("bo%d" % (ti % 2)), is_out=True)
        if after_store is not None:
            after_store(cx, ti)


def emit_C(cx, layer, halves, x_in, br_in, g1_ap, g2_ap, gf_ap, cst_ap, wg_ap, wb_ap, wo_ap, wu_ap, wd_ap,
           x_out, final, pfx="", br_eng="sp", after_xout=None):
    P = cx.P
    k = lambda s: pfx + s
    sk = lambda s: "C." + s
    HT = max(sum(T for _, T in h) for h in halves)
    xh = cx.sb(k("xh"), [128, NCK, HT], F32)
    hh = cx.sb(k("hh"), [128, NCK, HT], BF16)
    big = cx.sb(k("big"), [128, 32, HT], BF16)
    mixb = cx.sb(k("mixb"), [128, NCK, HT], BF16)
    macc = cx.sb(k("macc"), [128, HT], F32)
    g1 = cx.sb(k("cg1"), [128, 8], F32)
    g2 = cx.sb(k("cg2"), [128, 8], F32)
    gf = cx.sb(k("cgf"), [128, 8], F32)
    ones_b = cx.sb(k("cones"), [128, 128], BF16)
    sqb = cx.sb(k("csqb"), [128, NCK, TT], BF16)
    rstd = cx.sb(k("crstd"), [128, TT], F32)
    gt = [cx.sb(k("gt%d" % i), [128, TT], F32) for i in range(2)]
    tmp = [cx.sb(k("ctmp%d" % i), [128, TT], F32) for i in range(2)]
    wg = [cx.sb(k("wg%d" % i), [128, NCK * 128], BF16) for i in range(3)]
    wb = [cx.sb(k("wb%d" % i), [128, 4 * 128], BF16) for i in range(3)]
    wd = [cx.sb(k("wd%d" % i), [128, 32 * 128], BF16) for i in range(2)]
    banks = [cx.ps(k("cp%d" % i), [128, TT]) for i in range(8)]
    rr = [0]

    def bank():
        i = rr[0] % 8
        rr[0] += 1
        return banks[i], "cp%d" % i

    ci = CONST_NAMES.index("ones")
    cx.dma("pool", ones_b[:], cst_ap[:, ci * 128:(ci + 1) * 128], [], ["cones"], sk("cones"))
    cx.dma("sp", g1[:], g1_ap, [], ["cg1"], sk("cg1"))
    cx.dma("sp", g2[:], g2_ap, [], ["cg2"], sk("cg2"))
    cx.dma("sp", gf[:], gf_ap, [], ["cgf"], sk("cgf"))
    wcnt = {"wg": 0, "wb": 0, "wd": 0}
    bigk = [("big", i) for i in range(4)]

    def loadw(kind, bufs, ap, n):
        i = wcnt[kind] % len(bufs)
        wcnt[kind] += 1
        cx.dma("pool", bufs[i][:, 0:n], ap, [], [(kind, i)], sk("%s%d" % (kind, i)))
        return bufs[i], (kind, i)

    def rms(tl, lo, g, T, outf):
        cx.act(sqb[:, :, 0:T], xh[:, :, lo:lo + T], AF.Square, ["xh"], ["csqb"])
        pb, pk = bank()
        for c in range(NCK):
            cx.mm(pb[:, 0:T], ones_b[:], sqb[:, c, 0:T], c == 0, c == NCK - 1, ["csqb", "cones"], [pk])
        cx.act(rstd[:, 0:T], pb[:, 0:T], AF.Ln, [pk], ["crstd"], bias=EPS, scale=1.0 / D)
        cx.act(rstd[:, 0:T], rstd[:, 0:T], AF.Exp, ["crstd"], ["crstd"], scale=-0.5)
        for c in range(NCK):
            o, ok = outf(c)
            cx.stt("dve", o, xh[:, c, lo:lo + T], g[:, c:c + 1], rstd[:, 0:T],
                   ALU.mult, ALU.mult, ["xh", "crstd", "cg1", "cg2", "cgf"], [ok])

    for hi, tiles in enumerate(halves):
        lo = 0
        ltiles = []
        for (off, T) in tiles:
            ltiles.append((lo, off, T))
            lo += T
        for ti2, (lo, off, T) in enumerate(ltiles):
            cx.dma("sp", xh[:, :, lo:lo + T], x_in(off, T).rearrange("(c p) t -> p c t", p=128), ["xint"], ["xh"], sk("xh"))
            cx.dma(br_eng, big[:, 0:16, lo:lo + T], br_in(off, T), ["brg"], [("big", ti2 % 4)], sk("big%d" % (ti2 % 4)))
        for (lo, off, T) in ltiles:
            rms(None, lo, g1, T, lambda c, lo=lo, T=T: (hh[:, c, lo:lo + T], "hh"))
        for dc in range(NCK):
            for n in range(4):
                wgb, wgk = loadw("wg", wg, wg_ap[dc, n], NCK * 128)
                wbb, wbk = loadw("wb", wb, wb_ap[dc, n], 4 * 128)
                for ti, (lo, off, T) in enumerate(ltiles):
                    pa, pak = bank()
                    pbk_ = bank()
                    pb, pbk = pbk_
                    for c in range(NCK):
                        cx.mm(pa[:, 0:T], wgb[:, c * 128:(c + 1) * 128], hh[:, c, lo:lo + T], c == 0, c == NCK - 1,
                              [wgk, "hh"], [pak])
                    for j in range(4):
                        cx.mm(pb[:, 0:T], wbb[:, j * 128:(j + 1) * 128], big[:, j * 4 + n, lo:lo + T], j == 0, j == 3,
                              [wbk] + bigk, [pbk])
                    g_, gk = gt[ti % 2], ("gt", ti % 2)
                    t_, tk = tmp[ti % 2], ("ctmp", ti % 2)
                    cx.act(g_[:, 0:T], pa[:, 0:T], AF.Sigmoid, [pak], [gk])
                    if n == 0:
                        cx.tt("dve", macc[:, lo:lo + T], g_[:, 0:T], pb[:, 0:T], ALU.mult, [gk, pbk], [("macc", ti)])
                    else:
                        cx.tt("dve", t_[:, 0:T], g_[:, 0:T], pb[:, 0:T], ALU.mult, [gk, pbk], [tk])
                        if n < 3:
                            cx.tt("dve", macc[:, lo:lo + T], macc[:, lo:lo + T], t_[:, 0:T], ALU.add,
                                  [tk, ("macc", ti)], [("macc", ti)])
                        else:
                            cx.tt("dve", mixb[:, dc, lo:lo + T], macc[:, lo:lo + T], t_[:, 0:T], ALU.add,
                                  [tk, ("macc", ti)], [("mixb", dc)])
        mixk = [("mixb", c) for c in range(NCK)]
        for dc in range(NCK):
            wgb, wgk = loadw("wg", wg, wo_ap[dc], NCK * 128)
            for ti, (lo, off, T) in enumerate(ltiles):
                pa, pak = bank()
                for c in range(NCK):
                    cx.mm(pa[:, 0:T], wgb[:, c * 128:(c + 1) * 128], mixb[:, c, lo:lo + T], c == 0, c == NCK - 1,
                          [wgk] + mixk, [pak])
                cx.tt("dve", xh[:, dc, lo:lo + T], xh[:, dc, lo:lo + T], pa[:, 0:T], ALU.add, [pak, "xh"], ["xh"])
        for (lo, off, T) in ltiles:
            rms(None, lo, g2, T, lambda c, lo=lo, T=T: (hh[:, c, lo:lo + T], "hh"))
        for f in range(32):
            wgb, wgk = loadw("wg", wg, wu_ap[f], NCK * 128)
            for ti, (lo, off, T) in enumerate(ltiles):
                pa, pak = bank()
                for c in range(NCK):
                    cx.mm(pa[:, 0:T], wgb[:, c * 128:(c + 1) * 128], hh[:, c, lo:lo + T], c == 0, c == NCK - 1,
                          [wgk, "hh"], [pak])
                g_, gk = gt[ti % 2], ("gt", ti % 2)
                cx.act(g_[:, 0:T], pa[:, 0:T], AF.Relu, [pak], [gk])
                cx.tt("dve", big[:, f, lo:lo + T], g_[:, 0:T], g_[:, 0:T], ALU.mult, [gk], bigk)
        for dc in range(NCK):
            wdb, wdk = loadw("wd", wd, wd_ap[dc], 32 * 128)
            for ti, (lo, off, T) in enumerate(ltiles):
                pa, pak = bank()
                for f in range(32):
                    cx.mm(pa[:, 0:T], wdb[:, f * 128:(f + 1) * 128], big[:, f, lo:lo + T], f == 0, f == 31,
                          [wdk] + bigk, [pak])
                cx.tt("dve", xh[:, dc, lo:lo + T], xh[:, dc, lo:lo + T], pa[:, 0:T], ALU.add, [pak, "xh"], ["xh"])
        for (lo, off, T) in ltiles:
            if final:
                rms(None, lo, gf, T, lambda c, lo=lo, T=T: (xh[:, c, lo:lo + T], "xh"))
            cx.dma("sp", x_out(off, T).rearrange("(c p) t -> p c t", p=128), xh[:, :, lo:lo + T], ["xh"],
                   [("xout", off)], sk("xo"), is_out=True)
            if after_xout is not None:
                after_xout(cx, off)


FM_SPLITS = [0, 1, 3, 5, 6, 8, 9, 10]
TM_SPLITS = [2, 4, 7]


def prep_B_inputs(layer, j, w_in, norm1_g, lb_logits, hg_norm_g, pool_w, pool_scale, conv_w):
    cols = []
    for sidx in FM_SPLITS + TM_SPLITS:
        cols.append(w_in[layer][:, sidx * 512 + j * 128: sidx * 512 + (j + 1) * 128])
    w = np.ascontiguousarray(np.concatenate(cols, axis=1), dtype=np.float32)
    g1 = np.ascontiguousarray(norm1_g[layer].reshape(NCK, 128).T, dtype=np.float32)
    lbl = np.ascontiguousarray(lb_logits[:, j * 128:(j + 1) * 128].T, dtype=np.float32)
    vec = np.zeros((128, 8), np.float32)
    sl = slice(j * 128, (j + 1) * 128)
    vec[:, 0] = hg_norm_g[layer, sl]
    vec[:, 1] = pool_scale[layer, sl]
    vec[:, 2] = conv_w[layer, 0, sl]
    vec[:, 3] = conv_w[layer, 1, sl]
    vec[:, 4] = conv_w[layer, 2, sl]
    pw = np.ascontiguousarray(pool_w[layer, j], dtype=np.float32)
    _, cst = host_consts(POOL_WINDOWS[j])
    return {"w": w, "g1": g1, "lbl": lbl, "vec": vec, "pw": pw, "cst": np.ascontiguousarray(cst)}


def build_B(S, layer):
    nc = bass.Bass("TRN2", target_bir_lowering=False)
    Ltot = NMETA + S
    xT = nc.dram_tensor("xT", [D, Ltot], F32, kind="ExternalInput").ap()
    w = nc.dram_tensor("w", [D, 11 * 128], F32, kind="ExternalInput").ap()
    g1 = nc.dram_tensor("g1", [128, 8], F32, kind="ExternalInput").ap()
    lbl = nc.dram_tensor("lbl", [128, 4], F32, kind="ExternalInput").ap()
    vec = nc.dram_tensor("vec", [128, 8], F32, kind="ExternalInput").ap()
    pw = nc.dram_tensor("pw", [128, 128], F32, kind="ExternalInput").ap()
    cst = nc.dram_tensor("cst", [128, 10 * 128], F32, kind="ExternalInput").ap()
    br = nc.dram_tensor("br", [512, Ltot], BF16, kind="ExternalOutput").ap()
    P = Prog(nc)
    with ExitStack() as es:
        cx = Ctx(nc, P, es)
        emit_B(cx, S, layer, lambda off, T: xT[:, off:off + T], w, g1, lbl, vec, pw, cst,
               lambda off, T: br[:, off:off + T], pfx="b_")
        P.emit()
    return nc


def prep_C_weights(layer, w_in, w_branch, w_o, w_up, w_down):
    wg = w_in[layer][:, 11 * 512:].reshape(NCK, 128, 4, NCK, 128)
    wg = np.ascontiguousarray(wg.transpose(3, 2, 1, 0, 4)).reshape(NCK, 4, 128, NCK * 128)
    wb = w_branch[layer].reshape(4, 4, 128, NCK, 128)
    wb = np.ascontiguousarray(wb.transpose(3, 0, 2, 1, 4)).reshape(NCK, 4, 128, 4 * 128)
    wo = w_o[layer].reshape(NCK, 128, NCK, 128)
    wo = np.ascontiguousarray(wo.transpose(2, 1, 0, 3)).reshape(NCK, 128, NCK * 128)
    wu = w_up[layer].reshape(NCK, 128, 32, 128)
    wu = np.ascontiguousarray(wu.transpose(2, 1, 0, 3)).reshape(32, 128, NCK * 128)
    wd = w_down[layer].reshape(32, 128, NCK, 128)
    wd = np.ascontiguousarray(wd.transpose(2, 1, 0, 3)).reshape(NCK, 128, 32 * 128)
    return {"wg": wg, "wb": wb, "wo": wo, "wu": wu, "wd": wd}


def c_halves(ntok_x):
    tiles = [(0, NMETA)] + [(NMETA + TT * i, TT) for i in range(ntok_x // TT)]
    nh = (len(tiles) + 1) // 2
    return [tiles[:nh], tiles[nh:]] if len(tiles) > nh else [tiles]


def build_C(ntok_x, layer, final):
    nc = bass.Bass("TRN2", target_bir_lowering=False)
    NT = NMETA + ntok_x
    x = nc.dram_tensor("x", [D, NT], F32, kind="ExternalInput").ap()
    br = nc.dram_tensor("brc", [16, 128, NT], BF16, kind="ExternalInput").ap()
    g1 = nc.dram_tensor("g1", [128, 8], F32, kind="ExternalInput").ap()
    g2 = nc.dram_tensor("g2", [128, 8], F32, kind="ExternalInput").ap()
    gf = nc.dram_tensor("gf", [128, 8], F32, kind="ExternalInput").ap()
    cst = nc.dram_tensor("cst", [128, 10 * 128], F32, kind="ExternalInput").ap()
    wg = nc.dram_tensor("wg", [NCK, 4, 128, NCK * 128], F32, kind="ExternalInput").ap()
    wb = nc.dram_tensor("wb", [NCK, 4, 128, 4 * 128], F32, kind="ExternalInput").ap()
    wo = nc.dram_tensor("wo", [NCK, 128, NCK * 128], F32, kind="ExternalInput").ap()
    wu = nc.dram_tensor("wu", [32, 128, NCK * 128], F32, kind="ExternalInput").ap()
    wd = nc.dram_tensor("wd", [NCK, 128, 32 * 128], F32, kind="ExternalInput").ap()
    xo = nc.dram_tensor("xo", [D, NT], F32, kind="ExternalOutput").ap()
    P = Prog(nc)
    with ExitStack() as es:
        cx = Ctx(nc, P, es)
        emit_C(cx, layer, c_halves(ntok_x), lambda off, T: x[:, off:off + T],
               lambda off, T: br[:, :, off:off + T].rearrange("c p t -> p c t"), g1, g2, gf, cst, wg, wb, wo, wu, wd,
               lambda off, T: xo[:, off:off + T], final, pfx="c_")
        P.emit()
    return nc


def _r8(v):
    return np.ascontiguousarray(np.asarray(v, np.float32).reshape(NCK, 128).T)


def kernel_unfused(x, meta_tokens, lb_logits, norm1_g, w_in, hg_norm_g, pool_w, pool_scale, conv_w,
           w_branch, w_o, norm2_g, w_up, w_down, final_norm_g):
    f = lambda a: np.asarray(a, dtype=np.float32)
    x, meta_tokens, lb_logits, norm1_g, w_in = f(x), f(meta_tokens), f(lb_logits), f(norm1_g), f(w_in)
    hg_norm_g, pool_w, pool_scale, conv_w = f(hg_norm_g), f(pool_w), f(pool_scale), f(conv_w)
    w_branch, w_o, norm2_g, w_up, w_down, final_norm_g = f(w_branch), f(w_o), f(norm2_g), f(w_up), f(w_down), f(final_norm_g)
    B_, S, _ = x.shape
    NQ = 8 // B_
    SQ = S // NQ
    cores = list(range(8))
    xT = [np.ascontiguousarray(np.concatenate([meta_tokens, x[b]], axis=0).T) for b in range(B_)]
    cst2 = np.ascontiguousarray(host_consts(2)[1])
    for layer in range(DEPTH):
        ncB = build_B(S, layer)
        in_maps = []
        for r in cores:
            b, j = r // NQ, r % NQ
            im = prep_B_inputs(layer, j, w_in, norm1_g, lb_logits, hg_norm_g, pool_w, pool_scale, conv_w)
            im["xT"] = xT[b]
            in_maps.append(im)
        resB = run_bass_kernel_spmd(ncB, in_maps, core_ids=cores).results
        brs = [np.asarray(resB[r]["br"]) for r in cores]
        final = layer == DEPTH - 1
        ncC = build_C(SQ, layer, final)
        wts = prep_C_weights(layer, w_in, w_branch, w_o, w_up, w_down)
        g1, g2, gf = _r8(norm1_g[layer]), _r8(norm2_g[layer]), _r8(final_norm_g)
        in_maps = []
        for r in cores:
            b, q = r // NQ, r % NQ
            cols = np.concatenate([np.arange(NMETA), NMETA + q * SQ + np.arange(SQ)])
            im = dict(wts)
            im["x"] = np.ascontiguousarray(xT[b][:, cols])
            brc = np.empty((16, 128, NMETA + SQ), dtype=brs[0].dtype)
            for n in range(4):
                for j in range(4):
                    brc[j * 4 + n] = brs[b * NQ + j][n * 128:(n + 1) * 128][:, cols]
            im["brc"] = brc
            im["g1"], im["g2"], im["gf"], im["cst"] = g1, g2, gf, cst2
            in_maps.append(im)
        resC = run_bass_kernel_spmd(ncC, in_maps, core_ids=cores).results
        for b in range(B_):
            new = np.empty_like(xT[b])
            new[:, 0:NMETA] = np.asarray(resC[b * NQ]["xo"])[:, 0:NMETA]
            for q in range(NQ):
                new[:, NMETA + q * SQ: NMETA + (q + 1) * SQ] = np.asarray(resC[b * NQ + q]["xo"])[:, NMETA:]
            xT[b] = new
    out = np.stack([np.ascontiguousarray(xT[b][:, NMETA:].T) for b in range(B_)], axis=0)
    return out.astype(np.float32)


I32 = mybir.dt.int32
GROUPS = [[0, 1, 2, 3], [4, 5, 6, 7]]


def build_fused(S, depth=DEPTH):
    nc = bass.Bass("TRN2", target_bir_lowering=False)
    NQ = 4
    SQ = S // NQ
    NK = SQ // TT
    NT = NMETA + SQ
    Ltot = NMETA + S
    NTB = S // TT
    ext = lambda name, shape, dt=F32: nc.dram_tensor(name, shape, dt, kind="ExternalInput").ap()
    x0 = ext("x0", [D, NT])
    qcol = ext("qcol", [1, 8], I32)
    lbl = ext("lbl", [128, 4])
    cstB = ext("cstB", [128, 10 * 128])
    gf = ext("gf", [128, 8])
    L = []
    for l in range(depth):
        L.append(dict(
            wB=ext("wB%d" % l, [D, 11 * 128]), g1=ext("g1_%d" % l, [128, 8]), vec=ext("vec%d" % l, [128, 8]),
            pw=ext("pw%d" % l, [128, 128]), g2=ext("g2_%d" % l, [128, 8]),
            wg=ext("wg%d" % l, [NCK, 4, 128, NCK * 128]), wb=ext("wb%d" % l, [NCK, 4, 128, 4 * 128]),
            wo=ext("wo%d" % l, [NCK, 128, NCK * 128]), wu=ext("wu%d" % l, [32, 128, NCK * 128]),
            wd=ext("wd%d" % l, [NCK, 128, 32 * 128])))
    xo = nc.dram_tensor("xo", [D, NT], F32, kind="ExternalOutput").ap()
    xm = nc.dram_tensor("xm_i", [D, NMETA], F32).ap()
    xx = nc.dram_tensor("xx_i", [NK, 2, 512, TT], F32).ap()
    xg = nc.dram_tensor("xg_i", [NK, 2, NQ * 512, TT], F32).ap()
    brm = nc.dram_tensor("brm_i", [512, NMETA], BF16).ap()
    brx = nc.dram_tensor("brx_i", [NTB, 512, TT], BF16).ap()
    brgm = nc.dram_tensor("brgm_i", [NQ * 512, NMETA], BF16).ap()
    brgx = nc.dram_tensor("brgx_i", [NTB, NQ * 512, TT], BF16).ap()

    def ag(P, src, dst, r, w, key):
        P.op("pool", lambda e: e.collective_compute("AllGather", ALU.bypass, replica_groups=GROUPS,
                                                    ins=[src.opt()], outs=[dst.opt()]),
             reads=r, writes=w, dma_key=key, inc=1)

    def x_tile_ap(k_):
        return xx[k_].rearrange("h r t -> (h r) t")

    def x_in(off, T):
        return xm if off == 0 else x_tile_ap((off - NMETA) // TT)

    def x_loader(cx, ti, off, T, buf, bkey, semkey):
        if ti == 0:
            cx.dma("sp", buf[:, :, 0:T], xm.rearrange("(c p) t -> p c t", p=128), ["xm"], [bkey], semkey)
            return
        g = (ti - 1) * TT
        q, k_ = g // SQ, (g % SQ) // TT
        for h in range(2):
            src = xg[k_, h, q * 512:(q + 1) * 512, :].rearrange("(c p) t -> p c t", p=128)
            cx.dma("sp", buf[:, h * 4:(h + 1) * 4, 0:T], src, [("xg", k_, h)], [bkey], semkey)

    def br_out(off, T):
        return brm if off == 0 else brx[(off - NMETA) // TT]

    halves = c_halves(SQ)
    vals = {}
    es_glob = ExitStack()
    state = None
    for l in range(depth):
        P = Prog(nc, state)
        state = P.state
        with ExitStack() as es:
            cx = Ctx(nc, P, es)
            if l == 0:
                cx.dma("sp", xm, x0[:, 0:NMETA], [], ["xm"], "cpm")
                for k_ in range(NK):
                    cx.dma("sp", x_tile_ap(k_), x0[:, NMETA + k_ * TT:NMETA + (k_ + 1) * TT], [], [("xx", k_)], "cpx%d" % (k_ % 2))
            if l == 0:
                for k_ in range(NK):
                    for h in range(2):
                        ag(P, xx[k_, h], xg[k_, h], [("xx", k_)], [("xg", k_, h)], "ccx")

            def after_store(cx_, ti):
                if ti == 0:
                    ag(cx_.P, brm, brgm, [("brout_d", 0)], [("brg", 0)], "ccb")
                else:
                    ag(cx_.P, brx[ti - 1], brgx[ti - 1], [("brout_d", ti)], [("brg", ti)], "ccb")

            emit_B(cx, S, l, None, L[l]["wB"], L[l]["g1"], lbl, L[l]["vec"], L[l]["pw"], cstB, br_out,
                   pfx="b%d_" % l, x_loader=x_loader, after_store=after_store)
            P.op("sp", lambda e: e.dma_start(out=brm[0:1, 0:2], in_=brm[0:1, 0:2]),
                 reads=[("brg", t_) for t_ in range(NTB + 1)], writes=[], dma_key="fin", is_out=True)
            P.emit()
        P = Prog(nc, state)

        br_eng = "sp" if l < 2 else "act"

        def init_eng(handle, es2, br_eng=br_eng):
            if br_eng in vals:
                return
            vals[br_eng] = {}
            for kk in range(NK):
                reg = es_glob.enter_context(handle.register("qreg_%s%d" % (br_eng, kk)))
                handle.reg_load(reg, qcol[0:1, kk:kk + 1])
                vals[br_eng][kk] = handle.snap(reg, min_val=0, max_val=NTB - 1)

        P.init[br_eng] = init_eng

        def br_in(off, T, br_eng=br_eng):
            if off == 0:
                return brgm.rearrange("(c p) t -> p c t", p=128)
            kk = (off - NMETA) // TT
            return lambda: brgx[bass.ds(vals[br_eng][kk], 1)].rearrange("o (c p) t -> p (o c) t", p=128)

        final = l == depth - 1

        def after_xout(cx_, off):
            if off == 0:
                return
            k_ = (off - NMETA) // TT
            for h in range(2):
                cx_.P.op("pool", lambda e, k_=k_, h=h: e.collective_compute(
                    "AllGather", ALU.bypass, replica_groups=GROUPS, ins=[xx[k_, h].opt()], outs=[xg[k_, h].opt()]),
                    reads=[("xout", off)], writes=[("xg", k_, h)], dma_key="ccx", inc=1, is_out=True)

        with ExitStack() as es:
            cx = Ctx(nc, P, es)
            emit_C(cx, l, halves, x_in, br_in, L[l]["g1"], L[l]["g2"], gf, cstB,
                   L[l]["wg"], L[l]["wb"], L[l]["wo"], L[l]["wu"], L[l]["wd"],
                   (lambda off, T: xo[:, off:off + T]) if final else x_in, final, pfx="c%d_" % l, br_eng=br_eng,
                   after_xout=None if final else after_xout)
            P.emit()
    return nc


def fused_inputs(x, meta_tokens, lb_logits, norm1_g, w_in, hg_norm_g, pool_w, pool_scale, conv_w,
                 w_branch, w_o, norm2_g, w_up, w_down, final_norm_g, depth=DEPTH):
    B_, S, _ = x.shape
    NQ = 8 // B_
    SQ = S // NQ
    gf = _r8(final_norm_g)
    cw = [prep_C_weights(l, w_in, w_branch, w_o, w_up, w_down) for l in range(depth)]
    in_maps = []
    for r in range(8):
        b, q = r // NQ, r % NQ
        im = {}
        im["x0"] = np.ascontiguousarray(np.concatenate([meta_tokens, x[b, q * SQ:(q + 1) * SQ]], axis=0).T)
        qc = np.zeros((1, 8), np.int32)
        for kk in range(SQ // TT):
            qc[0, kk] = q * (SQ // TT) + kk
        im["qcol"] = qc
        im["gf"] = gf
        for l in range(depth):
            pb = prep_B_inputs(l, q, w_in, norm1_g, lb_logits, hg_norm_g, pool_w, pool_scale, conv_w)
            im["wB%d" % l], im["g1_%d" % l], im["vec%d" % l], im["pw%d" % l] = pb["w"], pb["g1"], pb["vec"], pb["pw"]
            im["lbl"], im["cstB"] = pb["lbl"], pb["cst"]
            im["g2_%d" % l] = _r8(norm2_g[l])
            for kname in ("wg", "wb", "wo", "wu", "wd"):
                im["%s%d" % (kname, l)] = cw[l][kname]
        in_maps.append(im)
    return in_maps


def kernel(x, meta_tokens, lb_logits, norm1_g, w_in, hg_norm_g, pool_w, pool_scale, conv_w,
           w_branch, w_o, norm2_g, w_up, w_down, final_norm_g):
    f = lambda a: np.asarray(a, dtype=np.float32)
    args = [f(a) for a in (x, meta_tokens, lb_logits, norm1_g, w_in, hg_norm_g, pool_w, pool_scale, conv_w,
                           w_branch, w_o, norm2_g, w_up, w_down, final_norm_g)]
    x = args[0]
    B_, S, _ = x.shape
    NQ = 8 // B_
    SQ = S // NQ
    nc = build_fused(S)
    in_maps = fused_inputs(*args)
    res = run_bass_kernel_spmd(nc, in_maps, core_ids=list(range(8))).results
    out = np.empty((B_, S, D), np.float32)
    for r in range(8):
        b, q = r // NQ, r % NQ
        out[b, q * SQ:(q + 1) * SQ] = np.asarray(res[r]["xo"])[:, NMETA:].T
    return out
```

```python
from contextlib import ExitStack
import numpy as np
import ml_dtypes
import concourse.bass as bass
import concourse.mybir as mybir
from concourse.bass_utils import run_bass_kernel_spmd

F32 = mybir.dt.float32
BF16 = mybir.dt.bfloat16
ALU = mybir.AluOpType
AF = mybir.ActivationFunctionType

D = 1024
NCK = 8
NMETA = 16
TT = 512
DEPTH = 4
EPS = 1e-6
POOL_WINDOWS = (2, 4, 8, 16)
SEM_CH = 30000
PARTS = {"conv", "pool", "hgrn", "attn"}
NFILL = 4


class Prog:
    ENGS = ("pe", "act", "dve", "pool", "sp")

    def __init__(self, nc, state=None):
        self.nc = nc
        self.ops = {e: [] for e in self.ENGS}
        self.state = state if state is not None else {"cnt": {e: 0 for e in self.ENGS}, "dma_cnt": {}, "handles": {},
                                                      "es": ExitStack()}
        self.cnt = self.state["cnt"]
        self.lastw = {}
        self.readers = {}
        self.known = {e: {} for e in self.ENGS}
        self.dma_cnt = self.state["dma_cnt"]
        self.semkeys = []
        self.semset = set()
        self.out_tokens = []
        self.init = {}
        self.excl = set(["M0", "M1", "Z0", "Z1", "G", "CS", "O", "TB"] + ["cp%d" % i for i in range(8)])

    def _sem(self, key):
        if key not in self.semset:
            self.semset.add(key)
            self.semkeys.append(key)
        return key

    LIMIT = None
    nops = 0

    def op(self, eng, fn, reads=(), writes=(), dma_key=None, is_out=False, inc=16):
        Prog.nops += 1
        if Prog.LIMIT is not None and Prog.nops > Prog.LIMIT and not is_out:
            return None
        deps = []
        for k in reads:
            t = self.lastw.get(k)
            if t is not None:
                deps.append(t)
            if k in self.excl:
                deps.extend(r for r in self.readers.get(k, ()) if r[2] != eng)
        for k in writes:
            t = self.lastw.get(k)
            if t is not None:
                deps.append(t)
            deps.extend(self.readers.get(k, ()))
        waits = {}
        kn = self.known[eng]
        for (sk, val, deng) in deps:
            if deng == eng and dma_key is None and eng == "pe":
                continue
            if kn.get(sk, 0) >= val:
                continue
            if waits.get(sk, 0) < val:
                waits[sk] = val
        for sk, val in waits.items():
            kn[sk] = val
        if dma_key is not None:
            sk = self._sem(("dma", dma_key))
            n = self.dma_cnt.get(sk, 0) + 1
            self.dma_cnt[sk] = n
            done = (sk, inc * n, "dma")
        else:
            idx = self.cnt[eng]
            self.cnt[eng] += 1
            sk = self._sem(("eng", eng, idx // SEM_CH))
            done = (sk, idx % SEM_CH + 1, eng)
            inc = 1
        self.ops[eng].append((list(waits.items()), fn, sk, inc))
        for k in writes:
            self.lastw[k] = done
            self.readers[k] = []
        for k in reads:
            self.readers.setdefault(k, []).append(done)
        if is_out:
            self.out_tokens.append(done)
        return done

    def emit(self):
        nc = self.nc
        final_waits = {}
        for (sk, val, _e) in self.out_tokens:
            if final_waits.get(sk, 0) < val:
                final_waits[sk] = val
        with ExitStack() as es:
            sems = self.state["handles"]
            for sk in self.semkeys:
                if sk not in sems:
                    sems[sk] = self.state["es"].enter_context(nc.semaphore("s%d" % len(sems)))
            block = es.enter_context(nc.Block())

            def run(engname, handle):
                if engname in self.init:
                    self.init[engname](handle, es)
                for (waits, fn, sk, inc) in self.ops[engname]:
                    for wk, wv in waits:
                        handle.wait_ge(sems[wk], wv)
                    fn(handle).then_inc(sems[sk], inc)
                if engname == "sp":
                    for wk, wv in final_waits.items():
                        handle.wait_ge(sems[wk], wv)

            @block.tensor
            def _(e):
                run("pe", e)

            @block.scalar
            def _(e):
                run("act", e)

            @block.vector
            def _(e):
                run("dve", e)

            @block.gpsimd
            def _(e):
                run("pool", e)

            @block.sync
            def _(e):
                run("sp", e)


class Ctx:
    def __init__(self, nc, P, es):
        self.nc, self.P, self.es = nc, P, es

    def sb(self, name, shape, dt):
        return self.es.enter_context(self.nc.sbuf_tensor(name, shape, dt))

    def ps(self, name, shape, dt=F32):
        return self.es.enter_context(self.nc.psum_tensor(name, shape, dt))

    def mm(self, out, lhsT, rhs, start, stop, r, w, skip=False):
        if skip:
            self.P.op("pe", lambda e: e.matmul(out, lhsT, rhs, start=start, stop=stop, skip_group_check=True),
                      reads=r, writes=w)
        else:
            self.P.op("pe", lambda e: e.matmul(out, lhsT, rhs, start=start, stop=stop), reads=r, writes=w)

    def tr(self, out, in_, ident, r, w):
        self.P.op("pe", lambda e: e.transpose(out, in_, ident), reads=r, writes=w)

    def act(self, out, in_, func, r, w, bias=None, scale=None):
        kw = {}
        if bias is not None:
            kw["bias"] = bias
        if scale is not None:
            kw["scale"] = scale
        self.P.op("act", lambda e: e.activation(out=out, in_=in_, func=func, **kw), reads=r, writes=w)

    def tt(self, eng, out, in0, in1, op, r, w):
        self.P.op(eng, lambda e: e.tensor_tensor(out=out, in0=in0, in1=in1, op=op), reads=r, writes=w)

    def ts(self, eng, out, in0, s1, op0, r, w, s2=None, op1=None):
        if op1 is None:
            self.P.op(eng, lambda e: e.tensor_scalar(out=out, in0=in0, scalar1=s1, scalar2=None, op0=op0),
                      reads=r, writes=w)
        else:
            self.P.op(eng, lambda e: e.tensor_scalar(out=out, in0=in0, scalar1=s1, scalar2=s2, op0=op0, op1=op1),
                      reads=r, writes=w)

    def stt(self, eng, out, in0, scalar, in1, op0, op1, r, w):
        self.P.op(eng, lambda e: e.scalar_tensor_tensor(out=out, in0=in0, scalar=scalar, in1=in1, op0=op0, op1=op1),
                  reads=r, writes=w)

    def copy(self, eng, out, in_, r, w):
        if eng == "act":
            self.P.op("act", lambda e: e.activation(out=out, in_=in_, func=AF.Copy), reads=r, writes=w)
        else:
            self.P.op(eng, lambda e: e.tensor_copy(out=out, in_=in_), reads=r, writes=w)

    def recip(self, out, in_, r, w):
        self.P.op("dve", lambda e: e.reciprocal(out=out, in_=in_), reads=r, writes=w)

    def memset(self, eng, ap, val, w):
        self.P.op(eng, lambda e: e.memset(ap, val), writes=w)

    def dma(self, eng, out, in_, r, w, key, is_out=False):
        self.P.op(eng, lambda e: e.dma_start(out=out, in_=(in_() if callable(in_) else in_)), reads=r, writes=w,
                  dma_key=key, is_out=is_out)


def seq_tiles(S):
    return [(0, NMETA)] + [(NMETA + TT * i, TT) for i in range(S // TT)]


def gblock(g):
    return (0, NMETA) if g == 0 else (NMETA + 128 * (g - 1), 128)


def host_consts(window):
    i = np.arange(128)
    c = {}
    c["ident"] = np.eye(128, dtype=np.float32)
    c["ones"] = np.ones((128, 128), np.float32)
    c["strict"] = (i[:, None] < i[None, :]).astype(np.float32)
    c["negun"] = -(i[:, None] >= i[None, :]).astype(np.float32)
    c["triu"] = (i[:, None] <= i[None, :]).astype(np.float32)
    w = window
    s, t = i[:, None], i[None, :]
    band = ((s <= t) & (s > t - w)).astype(np.float32)
    c["mdiag"] = band / w - np.eye(128, dtype=np.float32)
    c["moff"] = ((s - 128) > (t - w)).astype(np.float32) / w
    mf = np.zeros((128, 128), np.float32)
    mf[:16] = (s[:16] > (16 + t - w)).astype(np.float32) / w
    c["mofff"] = mf
    bm = np.zeros((128, 128), np.float32)
    bm[:16, :16] = band[:16, :16]
    c["bmeta"] = bm
    invn = np.zeros((128, 128), np.float32)
    invn[:, :16] = 1.0 / np.minimum(w, np.arange(16) + 1.0)[None, :]
    c["invn"] = invn
    names = ["ident", "ones", "strict", "negun", "triu", "mdiag", "moff", "mofff", "bmeta", "invn"]
    return names, np.concatenate([c[n] for n in names], axis=1)


CONST_NAMES = ["ident", "ones", "strict", "negun", "triu", "mdiag", "moff", "mofff", "bmeta", "invn"]


def emit_B(cx, S, layer, xT_ap, w_ap, g1_ap, lbl_ap, vec_ap, pw_ap, cst_ap, out_ap, pfx="", x_loader=None,
           after_store=None):
    P = cx.P
    Ltot = NMETA + S
    tiles = seq_tiles(S)
    NB = 1 + S // 128
    k = lambda s: pfx + s
    sk = lambda s: "B." + s

    w_bf = cx.sb(k("w_bf"), [128, NCK, 11 * 128], BF16)
    cst_f = cx.sb(k("cst_f"), [128, 10 * 128], F32)
    cst_b = cx.sb(k("cst_b"), [128, 10 * 128], BF16)
    g1 = cx.sb(k("g1"), [128, 8], F32)
    lbl = cx.sb(k("lbl"), [128, 4], F32)
    vec = cx.sb(k("vec"), [128, 8], F32)
    pw_b = cx.sb(k("pw_b"), [128, 128], BF16)
    sm = cx.sb(k("sm"), [128, 16], F32)
    kT = cx.sb(k("kT"), [128, Ltot], BF16)
    vc = cx.sb(k("vc"), [128, NB, 128], BF16)
    Sst = cx.sb(k("Sst"), [128, 128], F32)
    Ssc = cx.sb(k("Ssc"), [128, 128], BF16)
    ubuf = cx.sb(k("ubuf"), [128, TT + 2], F32)
    pv = cx.sb(k("pv"), [128, 5, 128], BF16)
    onesf = cx.sb(k("onesf"), [128, TT], F32)
    xt = [cx.sb(k("xt%d" % i), [128, NCK, TT], F32) for i in range(2)]
    sqb = cx.sb(k("sqb"), [128, NCK, TT], BF16)
    hT = cx.sb(k("hT"), [128, NCK, TT], BF16)
    rstd = cx.sb(k("rstd"), [128, TT], F32)
    proj = cx.sb(k("proj"), [128, 8, TT], F32)
    qT = cx.sb(k("qT"), [128, TT], BF16)
    itok = cx.sb(k("itok"), [128, 4, 128], BF16)
    brout = [cx.sb(k("brout%d" % i), [128, 4, TT], BF16) for i in range(2)]
    t1 = cx.sb(k("t1"), [128, TT], F32)
    t2 = cx.sb(k("t2"), [128, TT], F32)
    fval = cx.sb(k("fval"), [128, TT], F32)
    kk = cx.sb(k("kk"), [128, TT], F32)
    lf = cx.sb(k("lf"), [128, TT], F32)
    Bc = cx.sb(k("Bc"), [128, TT + 1], F32)
    cex = [cx.sb(k("cex%d" % i), [128, 64], F32) for i in range(3)]
    qe = cx.sb(k("qe"), [128, TT], BF16)
    ke = cx.sb(k("ke"), [128, TT], BF16)
    kdT = cx.sb(k("kdT"), [128, TT], BF16)
    kdtok = cx.sb(k("kdtok"), [128, 4, 128], BF16)
    scm = cx.sb(k("scm"), [128, 64], BF16)
    dsm = cx.sb(k("dsm"), [128, 4, 8], F32)
    osb = cx.sb(k("osb"), [128, TT], F32)
    osq = cx.sb(k("osq"), [128, TT], BF16)
    uTb = cx.sb(k("uTb"), [128, 128], BF16)
    uTf = cx.sb(k("uTf"), [128, 16], F32)
    Ef2 = cx.sb(k("Ef2"), [128, 2 * TT], F32)
    spb2 = cx.sb(k("spb2"), [128, 2 * TT], BF16)
    lg2 = cx.sb(k("lg2"), [128, 2 * TT], F32)
    Ab2 = cx.sb(k("Ab2"), [128, 2 * TT], BF16)
    Ef = [Ef2[:, i * TT:(i + 1) * TT] for i in range(2)]
    spb = [spb2[:, i * TT:(i + 1) * TT] for i in range(2)]
    lg = [lg2[:, i * TT:(i + 1) * TT] for i in range(2)]
    Ab = [Ab2[:, i * TT:(i + 1) * TT] for i in range(2)]
    h3 = lambda t_: t_[:, :].rearrange("p (h t) -> p h t", h=2)
    carry = [cx.sb(k("carry%d" % i), [128, TT], F32) for i in range(2)]
    ob = cx.sb(k("ob"), [128, 4 * 128], BF16)
    zbf = cx.sb(k("zbf"), [128, TT], BF16)
    M = [cx.ps(k("pM%d" % i), [128, TT]) for i in range(2)]
    Z = [cx.ps(k("pZ%d" % i), [128, TT]) for i in range(2)]
    G = cx.ps(k("pG"), [128, TT])
    CS = cx.ps(k("pCS"), [128, TT])
    O = cx.ps(k("pO"), [128, TT])
    TB = cx.ps(k("pTB"), [128, 2 * TT], BF16)

    def C(name, rows=128, cols=128, bf=True):
        i = CONST_NAMES.index(name)
        src = cst_b if bf else cst_f
        return src[0:rows, i * 128:i * 128 + cols]

    cx.dma("sp", cst_f[:], cst_ap, [], ["cst_f"], sk("cst_f"))
    cx.dma("pool", cst_b[:], cst_ap, [], ["cst_b"], sk("cst_b"))
    cx.dma("sp", g1[:], g1_ap, [], ["g1"], sk("g1"))
    cx.dma("sp", lbl[:], lbl_ap, [], ["lbl"], sk("lbl"))
    cx.dma("sp", vec[:], vec_ap, [], ["vec"], sk("vec"))
    cx.dma("pool", pw_b[:], pw_ap, [], ["pw_b"], sk("pw_b"))
    for c in range(NCK):
        cx.dma("pool", w_bf[:, c, :], w_ap[c * 128:(c + 1) * 128, :], [], [("w", c)], sk("w%d" % c))
    cx.memset("pool", onesf[:], 1.0, ["onesf"])
    cx.memset("pool", zbf[:], 0.0, ["zbf"])
    cx.memset("pool", Sst[:], 0.0, ["S"])
    cx.memset("pool", ubuf[:, 0:2], 0.0, ["ubuf"])
    cx.memset("pool", Bc[:, 0:1], 0.0, ["Bc"])
    P.op("dve", lambda e: e.reduce_max(out=sm[:, 0:1], in_=lbl[:], axis=mybir.AxisListType.X), reads=["lbl"], writes=["sm"])
    cx.ts("dve", sm[:, 1:2], sm[:, 0:1], -1.0, ALU.mult, ["sm"], ["sm"])
    cx.act(sm[:, 8:12], lbl[:], AF.Exp, ["sm", "lbl"], ["sm"], bias=sm[:, 1:2])
    P.op("dve", lambda e: e.reduce_sum(out=sm[:, 2:3], in_=sm[:, 8:12], axis=mybir.AxisListType.X), reads=["sm"], writes=["sm"])
    cx.recip(sm[:, 3:4], sm[:, 2:3], ["sm"], ["sm"])
    if layer == 0:
        cx.memset("dve", sm[:, 4:5], 0.0, ["sm"])
    else:
        P.op("dve", lambda e: e.reduce_sum(out=sm[:, 4:5], in_=sm[:, 9:9 + layer], axis=mybir.AxisListType.X), reads=["sm"], writes=["sm"])
        cx.tt("dve", sm[:, 4:5], sm[:, 4:5], sm[:, 3:4], ALU.mult, ["sm"], ["sm"])
    cx.ts("dve", sm[:, 5:6], sm[:, 4:5], -1.0, ALU.mult, ["sm"], ["sm"], s2=1.0, op1=ALU.add)
    lb_ap, oml_ap = sm[:, 4:5], sm[:, 5:6]

    def load_x(ti):
        off, T = tiles[ti]
        buf = xt[ti % 2]
        if x_loader is not None:
            x_loader(cx, ti, off, T, buf, ("xt", ti % 2), sk("xt%d" % (ti % 2)))
            return
        cx.dma("sp", buf[:, :, 0:T], xT_ap(off, T).rearrange("(c p) t -> p c t", p=128), ["xg"], [("xt", ti % 2)],
               sk("xt%d" % (ti % 2)))

    load_x(0)
    pending_ag = []
    for ti, (off, T) in enumerate(tiles):
        if ti + 1 < len(tiles):
            load_x(ti + 1)
        x = xt[ti % 2]
        xk = ("xt", ti % 2)
        meta = (ti == 0)
        if meta:
            blocks = [(0, NMETA)]
            gb0 = 0
        else:
            blocks = [(128 * m, 128) for m in range(T // 128)]
            gb0 = 4 * (ti - 1) + 1
        bo = brout[ti % 2]
        bok = ("brout", ti % 2)


        cx.act(sqb[:, :, 0:T], x[:, :, 0:T], AF.Square, [xk], ["sqb"])
        for c in range(NCK):
            cx.mm(M[0][:, 0:T], C("ones"), sqb[:, c, 0:T], c == 0, c == NCK - 1, ["sqb", "cst_b"], ["M0"])
        cx.act(rstd[:, 0:T], M[0][:, 0:T], AF.Ln, ["M0"], ["rstd"], bias=EPS, scale=1.0 / D)
        cx.act(rstd[:, 0:T], rstd[:, 0:T], AF.Exp, ["rstd"], ["rstd"], scale=-0.5)
        for c in range(NCK):
            cx.stt("dve", hT[:, c, 0:T], x[:, c, 0:T], g1[:, c:c + 1], rstd[:, 0:T],
                   ALU.mult, ALU.mult, [xk, "rstd", "g1"], [("hT", c)])
        hTk = [("hT", c) for c in range(NCK)]
        wk = [("w", c) for c in range(NCK)]

        for n in range(8):
            pm = M[n % 2]
            pk = "M%d" % (n % 2)
            for c in range(NCK):
                cx.mm(pm[:, 0:T], w_bf[:, c, n * 128:(n + 1) * 128], hT[:, c, 0:T], c == 0, c == NCK - 1,
                      hTk + wk, [pk])
            if n == 3:
                cx.copy("act", qT[:, 0:T], pm[:, 0:T], [pk], ["qT"])
            elif n == 4:
                cx.act(kT[:, off:off + T], pm[:, 0:T], AF.Copy, [pk], [("kT", ti)], scale=0.125)
            else:
                cx.copy("act", proj[:, n, 0:T], pm[:, 0:T], [pk], [("proj", n)])

        for m, (bs, bl) in enumerate(blocks):
            pm = M[m % 2]
            pk = "M%d" % (m % 2)
            for c in range(NCK):
                cx.mm(pm[0:bl, 0:384], hT[:, c, bs:bs + bl], w_bf[:, c, 1024:1408], c == 0, c == NCK - 1,
                      hTk + wk, [pk])
            cx.copy("act", itok[0:bl, m, :], pm[0:bl, 0:128], [pk], [("itok", m)])
            cx.copy("dve", pv[0:bl, 1 + m, :], pm[0:bl, 128:256], [pk], [("pv", 1 + m)])
            cx.copy("act", vc[0:bl, gb0 + m, :], pm[0:bl, 256:384], [pk], [("vc", gb0 + m)])


        if "conv" in PARTS:
            cx.tt("pool", ubuf[:, 2:2 + T], proj[:, 7, 0:T], proj[:, 5, 0:T], ALU.mult, [("proj", 7), ("proj", 5)], ["ubuf"])
            cx.ts("pool", t2[:, 0:T], ubuf[:, 2:2 + T], vec[:, 4:5], ALU.mult, ["ubuf", "vec"], ["t2"])
            cx.stt("dve", t2[:, 0:T], ubuf[:, 1:1 + T], vec[:, 3:4], t2[:, 0:T], ALU.mult, ALU.add, ["ubuf", "vec", "t2"], ["t2"])
            cx.stt("dve", t2[:, 0:T], ubuf[:, 0:T], vec[:, 2:3], t2[:, 0:T], ALU.mult, ALU.add, ["ubuf", "vec", "t2"], ["t2"])
            cx.tt("pool", bo[:, 3, 0:T], t2[:, 0:T], proj[:, 6, 0:T], ALU.mult, ["t2", ("proj", 6)], [bok])
            cx.copy("pool", ubuf[:, 0:2], ubuf[:, T:T + 2], ["ubuf"], ["ubuf"])

        if "pool" in PARTS:
            for m, (bs, bl) in enumerate(blocks):
                pm = M[m % 2]
                pk = "M%d" % (m % 2)
                if meta:
                    cx.mm(pm[:, 0:16], pv[0:16, 1, :], C("bmeta", 16, 16), True, True, [("pv", 1), "cst_b"], [pk])
                    cx.mm(pm[:, 16:32], pv[0:16, 1, :], C("ident", 16, 16), True, True, [("pv", 1), "cst_b"], [pk])
                    cx.tt("dve", uTf[:, 0:16], pm[:, 0:16], C("invn", 128, 16, bf=False), ALU.mult, [pk, "cst_f"], ["uTf"])
                    cx.tt("dve", uTb[:, 0:16], uTf[:, 0:16], pm[:, 16:32], ALU.subtract, [pk, "uTf"], ["uTb"])
                else:
                    prev_rows = 16 if (ti == 1 and m == 0) else 128
                    moff = C("mofff", 16, 128) if (ti == 1 and m == 0) else C("moff")
                    cx.mm(pm[:, 0:128], pv[:, 1 + m, :], C("mdiag"), True, False, [("pv", 1 + m), "cst_b"], [pk])
                    cx.mm(pm[:, 0:128], pv[0:prev_rows, m, :], moff, False, True, [("pv", m), "cst_b"], [pk])
                    cx.copy("dve", uTb[:, 0:bl], pm[:, 0:bl], [pk], ["uTb"])
                cx.mm(pm[:, 128:128 + bl], pw_b[:], uTb[:, 0:bl], True, True, ["uTb", "pw_b"], [pk])
                cx.act(bo[:, 1, bs:bs + bl], pm[:, 128:128 + bl], AF.Copy, [pk, "vec"], [bok], scale=vec[:, 1:2])
            lastm = len(blocks) - 1
            lbl_rows = blocks[lastm][1]
            cx.copy("pool", pv[0:lbl_rows, 0, :], pv[0:lbl_rows, 1 + lastm, :], [("pv", 1 + lastm), ("pv", 0)], [("pv", 0)])

        if "hgrn" in PARTS:
            cx.act(t1[:, 0:T], proj[:, 1, 0:T], AF.Exp, [("proj", 1)], ["t1"], scale=-1.0)
            cx.ts("dve", t1[:, 0:T], t1[:, 0:T], 1.0, ALU.add, ["t1"], ["t1"])
            cx.recip(t1[:, 0:T], t1[:, 0:T], ["t1"], ["t1"])
            cx.ts("dve", fval[:, 0:T], t1[:, 0:T], oml_ap, ALU.mult, ["t1", "sm"], ["fval"], s2=lb_ap, op1=ALU.add)
            cx.act(lf[:, 0:T], fval[:, 0:T], AF.Ln, ["fval"], ["lf"])
            cx.ts("pool", kk[:, 0:T], fval[:, 0:T], -1.0, ALU.mult, ["fval"], ["kk"], s2=1.0, op1=ALU.add)
            P.op("dve", lambda e, T=T: e.tensor_tensor_scan(out=Bc[:, 1:1 + T], data0=onesf[:, 0:T], data1=lf[:, 0:T],
                                                            initial=0.0, op0=ALU.mult, op1=ALU.add),
                 reads=["lf", "onesf"], writes=["Bc"])
            CL = 16 if meta else 64
            nch = T // CL
            Bv = Bc[:, 1:1 + T].rearrange("p (c l) -> p c l", l=CL)
            Bp = Bc[:, 0:T].rearrange("p (c l) -> p c l", l=CL)[:, :, 0:1]
            Bm = Bv[:, :, CL // 2 - 1:CL // 2]
            Be = Bv[:, :, CL - 1:CL]
            v3 = lambda t_: t_[:, 0:T].rearrange("p (c l) -> p c l", l=CL)
            cx.tt("dve", dsm[:, 2, 0:nch].rearrange("p (c o) -> p c o", o=1), Bm, Bp, ALU.subtract, ["Bc"], ["dsm"])
            cx.tt("dve", dsm[:, 3, 0:nch].rearrange("p (c o) -> p c o", o=1), Be, Bp, ALU.subtract, ["Bc"], ["dsm"])
            cx.act(dsm[:, 0:2, 0:nch], dsm[:, 2:4, 0:nch], AF.Exp, ["dsm"], ["dsm"])
            cx.tt("dve", v3(lf), Bv, Bm.to_broadcast([128, nch, CL]), ALU.subtract, ["Bc", "lf"], ["lf"])
            cx.tt("dve", v3(fval), Bv, Be.to_broadcast([128, nch, CL]), ALU.subtract, ["Bc", "fval", "kk"], ["fval"])
            cx.act(t1[:, 0:T], lf[:, 0:T], AF.Exp, ["lf"], ["t1"])
            cx.tt("dve", qe[:, 0:T], t1[:, 0:T], proj[:, 0, 0:T], ALU.mult, ["t1", ("proj", 0)], ["qe"])
            cx.act(t2[:, 0:T], lf[:, 0:T], AF.Exp, ["lf"], ["t2"], scale=-1.0)
            cx.tt("pool", ke[:, 0:T], t2[:, 0:T], kk[:, 0:T], ALU.mult, ["t2", "kk"], ["ke"])
            cx.act(t1[:, 0:T], fval[:, 0:T], AF.Exp, ["fval"], ["t1"], scale=-1.0)
            cx.tt("pool", kdT[:, 0:T], t1[:, 0:T], kk[:, 0:T], ALU.mult, ["t1", "kk"], ["kdT"])
            for ci in range(nch):
                c0 = ci * CL
                m = c0 // 128
                r0 = c0 % 128
                cx.tr(TB[r0:r0 + CL, 0:128], kdT[:, c0:c0 + CL], C("ident"), ["kdT", "cst_b"], ["TB"])
                cx.copy("act", kdtok[r0:r0 + CL, m, :], TB[r0:r0 + CL, 0:128], ["TB"], ["kdtok"])
                zb = Z[ci % 2]
                zk = "Z%d" % (ci % 2)
                cx.mm(zb[r0:r0 + CL, 0:CL], ke[:, c0:c0 + CL], qe[:, c0:c0 + CL], True, True, ["ke", "qe"], [zk])
                cx.tt("dve", scm[r0:r0 + CL, 0:CL], zb[r0:r0 + CL, 0:CL], C("triu", 128, 128, bf=False)[r0:r0 + CL, r0:r0 + CL],
                      ALU.mult, [zk, "cst_f"], ["scm"])
                cx.ts("dve", Ssc[:], Sst[:], dsm[:, 0, ci:ci + 1], ALU.mult, ["S", "dsm"], ["Ssc"])
                cx.mm(G[:, c0:c0 + CL], Ssc[:], qe[:, c0:c0 + CL], True, False, ["Ssc", "qe"], ["G"])
                cx.mm(G[:, c0:c0 + CL], itok[r0:r0 + CL, m, :], scm[r0:r0 + CL, 0:CL], False, True, [("itok", m), "scm"], ["G"])
                mb = M[ci % 2]
                mk = "M%d" % (ci % 2)
                cx.mm(mb[:, 0:128], kdtok[r0:r0 + CL, m, :], itok[r0:r0 + CL, m, :], True, True, ["kdtok", ("itok", m)], [mk])
                cx.stt("dve", Sst[:], Sst[:], dsm[:, 1, ci:ci + 1], mb[:, 0:128], ALU.mult, ALU.add, ["S", "dsm", mk], ["S"])
            cx.copy("act", osb[:, 0:T], G[:, 0:T], ["G"], ["osb"])
            cx.act(osq[:, 0:T], osb[:, 0:T], AF.Square, ["osb"], ["osq"])
            cx.mm(CS[:, 0:T], C("ones"), osq[:, 0:T], True, True, ["osq", "cst_b"], ["CS"])
            cx.act(t2[:, 0:T], CS[:, 0:T], AF.Ln, ["CS"], ["t2"], bias=EPS, scale=1.0 / 128)
            cx.act(t2[:, 0:T], t2[:, 0:T], AF.Exp, ["t2"], ["t2"], scale=-0.5)
            cx.act(t1[:, 0:T], proj[:, 2, 0:T], AF.Exp, [("proj", 2)], ["t1"], scale=-1.0)
            cx.ts("dve", t1[:, 0:T], t1[:, 0:T], 1.0, ALU.add, ["t1"], ["t1"])
            cx.recip(t1[:, 0:T], t1[:, 0:T], ["t1"], ["t1"])
            cx.stt("dve", osb[:, 0:T], osb[:, 0:T], vec[:, 0:1], t2[:, 0:T], ALU.mult, ALU.mult, ["osb", "vec", "t2"], ["osb"])
            cx.tt("dve", bo[:, 0, 0:T], osb[:, 0:T], t1[:, 0:T], ALU.mult, ["osb", "t1"], [bok])

        if "attn" in PARTS:
            nsub = len(blocks)
            SW = blocks[0][1]
            last_gb = gb0 + nsub - 1
            cx.mm(O[:, 0:T], zbf[:, 0:128], zbf[:, 0:T], True, False, ["zbf"], ["O"])
            while pending_ag:
                after_store(cx, pending_ag.pop(0))
            its = list(range(last_gb, -1, -1))
            n_it = len(its)
            for h in range(2):
                cx.memset("pool", carry[h][:, 0:T], 0.0, [("carry", h)])
            Zb = [[(Z[0], "Z0"), (G, "G")], [(Z[1], "Z1"), (M[0], "M0")]]
            CSb = [(CS, "CS"), (M[1], "M1")]

            def geom(i):
                kb = its[i]
                ks, KL = gblock(kb)
                kti = 0 if kb == 0 else (kb - 1) // 4 + 1
                diag = ks >= off
                qc0 = max(off, ks) - off
                return kb, ks, KL, kti, diag, qc0, T - qc0

            def phA1(i, h):
                kb, ks, KL, kti, diag, qc0, N = geom(i)
                hp = slice(64 * h, 64 * h + 64)
                zb, zk = Zb[h][i % 2]
                cx.mm(zb[0:KL, 0:N], kT[hp, ks:ks + KL], qT[hp, qc0:qc0 + N], True, True, [("kT", kti), "qT"], [zk])

            def phA(i, h):
                kb, ks, KL, kti, diag, qc0, N = geom(i)
                zb, zk = Zb[h][i % 2]
                ef, efk = Ef[h], ("Ef", h)
                sp, spk = spb[h], ("spb", h)
                cx.act(ef[0:KL, 0:N], zb[0:KL, 0:N], AF.Exp, [zk], [efk])
                cx.act(sp[0:KL, 0:N], ef[0:KL, 0:N], AF.Ln, [efk], [spk], bias=1.0)
                if diag:
                    cx.tt("pool", sp[0:KL, 0:SW], sp[0:KL, 0:SW], C("strict", KL, SW), ALU.mult, [spk, "cst_b"], [spk])

            def phA_both(i):
                return

                kb, ks, KL, kti, diag, qc0, N = geom(i)
                cx.act(h3(spb2)[0:KL, :, 0:N], h3(Ef2)[0:KL, :, 0:N], AF.Ln, [("Ef", 0), ("Ef", 1)],
                       [("spb", 0), ("spb", 1)], bias=1.0)
                if diag:
                    for h in range(2):
                        sp, spk = spb[h], ("spb", h)
                        cx.tt("pool", sp[0:KL, 0:SW], sp[0:KL, 0:SW], C("strict", KL, SW), ALU.mult, [spk, "cst_b"], [spk])

            def phC_both(i):
                return
                kb, ks, KL, kti, diag, qc0, N = geom(i)
                cx.act(h3(Ab2)[0:KL, :, 0:N], h3(lg2)[0:KL, :, 0:N], AF.Exp, [("lg", 0), ("lg", 1)],
                       [("Ab", 0), ("Ab", 1)])

            def phB(i, h):
                kb, ks, KL, kti, diag, qc0, N = geom(i)
                sp, spk = spb[h], ("spb", h)
                lgt, lgk = lg[h], ("lg", h)
                cr, crk = carry[h], ("carry", h)
                gb, gk = Zb[h][i % 2]
                cb, ck_ = CSb[h]
                cx.mm(gb[0:KL, 0:N], C("negun", KL, KL), sp[0:KL, 0:N], False, True, [spk, "cst_b"], [gk], skip=True)
                if kb > 0:
                    cx.mm(cb[:, 0:N], C("ones", KL, 128), sp[0:KL, 0:N], True, True, [spk, "cst_b"], [ck_])
                cx.tt("dve", lgt[0:KL, 0:N], gb[0:KL, 0:N], cr[0:KL, qc0:qc0 + N], ALU.subtract, [gk, crk], [lgk])
                if kb > 0:
                    cx.tt("dve", cr[:, qc0:qc0 + N], cr[:, qc0:qc0 + N], cb[:, 0:N], ALU.add, [ck_, crk], [crk])

            def phC(i, h):
                kb, ks, KL, kti, diag, qc0, N = geom(i)
                hp = slice(64 * h, 64 * h + 64)
                lgt, lgk = lg[h], ("lg", h)
                ab, abk = Ab[h], ("Ab", h)
                cx.act(ab[0:KL, 0:N], lgt[0:KL, 0:N], AF.Exp, [lgk], [abk])
                if diag:
                    cx.tt("pool", ab[0:KL, 0:SW], ab[0:KL, 0:SW], C("strict", KL, SW), ALU.mult, [abk, "cst_b"], [abk])
                cx.mm(O[hp, qc0:qc0 + N], vc[0:KL, kb, hp], ab[0:KL, 0:N], False, (kb == 0),
                      [abk, ("vc", kb)], ["O"])

            for t in range(n_it + 2):
                for h in range(2):
                    if t < n_it:
                        phA1(t, h)
                if 0 <= t - 2 < n_it:
                    phC_both(t - 2)
                for h in range(2):
                    if 0 <= t - 2 < n_it:
                        phC(t - 2, h)
                for h in range(2):
                    if 0 <= t - 1 < n_it:
                        phB(t - 1, h)
                for h in range(2):
                    if t < n_it:
                        phA(t, h)
                for _ in range(NFILL):
                    cx.mm(TB[:, :].bitcast(F32)[:, 0:T], C("ones"), qT[:, 0:T], True, True, ["qT", "cst_b"], ["TB"])
            cx.copy("dve", bo[:, 2, 0:T], O[:, 0:T], ["O"], [bok])

        cx.dma("sp", out_ap(off, T).rearrange("(n p) t -> p n t", p=128), bo[:, :, 0:T], [bok], [("brout_d", ti)],
               sk("bo%d" % (ti % 2)), is_out=True)
        if after_store is not None:
            if ti == len(tiles) - 1:
                after_store(cx, ti)
            else:
                pending_ag.append(ti)


def emit_C(cx, layer, halves, x_in, br_in, g1_ap, g2_ap, gf_ap, cst_ap, wg_ap, wb_ap, wo_ap, wu_ap, wd_ap,
           x_out, final, pfx="", br_eng="sp", after_xout=None):
    P = cx.P
    k = lambda s: pfx + s
    sk = lambda s: "C." + s
    HT = max(sum(T for _, T in h) for h in halves)
    xh = cx.sb(k("xh"), [128, NCK, HT], F32)
    hh = cx.sb(k("hh"), [128, NCK, HT], BF16)
    big = cx.sb(k("big"), [128, 32, HT], BF16)
    mixb = cx.sb(k("mixb"), [128, NCK, HT], BF16)
    macc = cx.sb(k("macc"), [128, HT], F32)
    g1 = cx.sb(k("cg1"), [128, 8], F32)
    g2 = cx.sb(k("cg2"), [128, 8], F32)
    gf = cx.sb(k("cgf"), [128, 8], F32)
    ones_b = cx.sb(k("cones"), [128, 128], BF16)
    sqb = cx.sb(k("csqb"), [128, NCK, TT], BF16)
    rstd = cx.sb(k("crstd"), [128, TT], F32)
    gt = [cx.sb(k("gt%d" % i), [128, TT], F32) for i in range(2)]
    tmp = [cx.sb(k("ctmp%d" % i), [128, TT], F32) for i in range(2)]
    wg = [cx.sb(k("wg%d" % i), [128, NCK * 128], BF16) for i in range(3)]
    wb = [cx.sb(k("wb%d" % i), [128, 4 * 128], BF16) for i in range(3)]
    wd = [cx.sb(k("wd%d" % i), [128, 32 * 128], BF16) for i in range(2)]
    banks = [cx.ps(k("cp%d" % i), [128, TT]) for i in range(8)]
    rr = [0]

    def bank():
        i = rr[0] % 8
        rr[0] += 1
        return banks[i], "cp%d" % i

    ci = CONST_NAMES.index("ones")
    cx.dma("pool", ones_b[:], cst_ap[:, ci * 128:(ci + 1) * 128], [], ["cones"], sk("cones"))
    cx.dma("sp", g1[:], g1_ap, [], ["cg1"], sk("cg1"))
    cx.dma("sp", g2[:], g2_ap, [], ["cg2"], sk("cg2"))
    cx.dma("sp", gf[:], gf_ap, [], ["cgf"], sk("cgf"))
    wcnt = {"wg": 0, "wb": 0, "wd": 0}
    bigk = [("big", i) for i in range(4)]

    def loadw(kind, bufs, ap, n):
        i = wcnt[kind] % len(bufs)
        wcnt[kind] += 1
        cx.dma("pool", bufs[i][:, 0:n], ap, [], [(kind, i)], sk("%s%d" % (kind, i)))
        return bufs[i], (kind, i)

    def rms(tl, lo, g, T, outf):
        cx.act(sqb[:, :, 0:T], xh[:, :, lo:lo + T], AF.Square, ["xh"], ["csqb"])
        pb, pk = bank()
        for c in range(NCK):
            cx.mm(pb[:, 0:T], ones_b[:], sqb[:, c, 0:T], c == 0, c == NCK - 1, ["csqb", "cones"], [pk])
        cx.act(rstd[:, 0:T], pb[:, 0:T], AF.Ln, [pk], ["crstd"], bias=EPS, scale=1.0 / D)
        cx.act(rstd[:, 0:T], rstd[:, 0:T], AF.Exp, ["crstd"], ["crstd"], scale=-0.5)
        for c in range(NCK):
            o, ok = outf(c)
            cx.stt("dve", o, xh[:, c, lo:lo + T], g[:, c:c + 1], rstd[:, 0:T],
                   ALU.mult, ALU.mult, ["xh", "crstd", "cg1", "cg2", "cgf"], [ok])

    for hi, tiles in enumerate(halves):
        lo = 0
        ltiles = []
        for (off, T) in tiles:
            ltiles.append((lo, off, T))
            lo += T
        for ti2, (lo, off, T) in enumerate(ltiles):
            cx.dma("sp", xh[:, :, lo:lo + T], x_in(off, T).rearrange("(c p) t -> p c t", p=128), ["xint"], ["xh"], sk("xh"))
            cx.dma(br_eng, big[:, 0:16, lo:lo + T], br_in(off, T), ["brg"], [("big", ti2 % 4)], sk("big%d" % (ti2 % 4)))
        for (lo, off, T) in ltiles:
            rms(None, lo, g1, T, lambda c, lo=lo, T=T: (hh[:, c, lo:lo + T], "hh"))
        for dc in range(NCK):
            for n in range(4):
                wgb, wgk = loadw("wg", wg, wg_ap[dc, n], NCK * 128)
                wbb, wbk = loadw("wb", wb, wb_ap[dc, n], 4 * 128)
                for ti, (lo, off, T) in enumerate(ltiles):
                    pa, pak = bank()
                    pbk_ = bank()
                    pb, pbk = pbk_
                    for c in range(NCK):
                        cx.mm(pa[:, 0:T], wgb[:, c * 128:(c + 1) * 128], hh[:, c, lo:lo + T], c == 0, c == NCK - 1,
                              [wgk, "hh"], [pak])
                    for j in range(4):
                        cx.mm(pb[:, 0:T], wbb[:, j * 128:(j + 1) * 128], big[:, j * 4 + n, lo:lo + T], j == 0, j == 3,
                              [wbk] + bigk, [pbk])
                    g_, gk = gt[ti % 2], ("gt", ti % 2)
                    t_, tk = tmp[ti % 2], ("ctmp", ti % 2)
                    cx.act(g_[:, 0:T], pa[:, 0:T], AF.Sigmoid, [pak], [gk])
                    if n == 0:
                        cx.tt("dve", macc[:, lo:lo + T], g_[:, 0:T], pb[:, 0:T], ALU.mult, [gk, pbk], [("macc", ti)])
                    else:
                        cx.tt("dve", t_[:, 0:T], g_[:, 0:T], pb[:, 0:T], ALU.mult, [gk, pbk], [tk])
                        if n < 3:
                            cx.tt("dve", macc[:, lo:lo + T], macc[:, lo:lo + T], t_[:, 0:T], ALU.add,
                                  [tk, ("macc", ti)], [("macc", ti)])
                        else:
                            cx.tt("dve", mixb[:, dc, lo:lo + T], macc[:, lo:lo + T], t_[:, 0:T], ALU.add,
                                  [tk, ("macc", ti)], [("mixb", dc)])
        mixk = [("mixb", c) for c in range(NCK)]
        for dc in range(NCK):
            wgb, wgk = loadw("wg", wg, wo_ap[dc], NCK * 128)
            for ti, (lo, off, T) in enumerate(ltiles):
                pa, pak = bank()
                for c in range(NCK):
                    cx.mm(pa[:, 0:T], wgb[:, c * 128:(c + 1) * 128], mixb[:, c, lo:lo + T], c == 0, c == NCK - 1,
                          [wgk] + mixk, [pak])
                cx.tt("dve", xh[:, dc, lo:lo + T], xh[:, dc, lo:lo + T], pa[:, 0:T], ALU.add, [pak, "xh"], ["xh"])
        for (lo, off, T) in ltiles:
            rms(None, lo, g2, T, lambda c, lo=lo, T=T: (hh[:, c, lo:lo + T], "hh"))
        for f in range(32):
            wgb, wgk = loadw("wg", wg, wu_ap[f], NCK * 128)
            for ti, (lo, off, T) in enumerate(ltiles):
                pa, pak = bank()
                for c in range(NCK):
                    cx.mm(pa[:, 0:T], wgb[:, c * 128:(c + 1) * 128], hh[:, c, lo:lo + T], c == 0, c == NCK - 1,
                          [wgk, "hh"], [pak])
                g_, gk = gt[ti % 2], ("gt", ti % 2)
                cx.act(g_[:, 0:T], pa[:, 0:T], AF.Relu, [pak], [gk])
                cx.tt("dve", big[:, f, lo:lo + T], g_[:, 0:T], g_[:, 0:T], ALU.mult, [gk], bigk)
        for dc in range(NCK):
            wdb, wdk = loadw("wd", wd, wd_ap[dc], 32 * 128)
            for ti, (lo, off, T) in enumerate(ltiles):
                pa, pak = bank()
                for f in range(32):
                    cx.mm(pa[:, 0:T], wdb[:, f * 128:(f + 1) * 128], big[:, f, lo:lo + T], f == 0, f == 31,
                          [wdk] + bigk, [pak])
                cx.tt("dve", xh[:, dc, lo:lo + T], xh[:, dc, lo:lo + T], pa[:, 0:T], ALU.add, [pak, "xh"], ["xh"])
        for (lo, off, T) in ltiles:
            if final:
                rms(None, lo, gf, T, lambda c, lo=lo, T=T: (xh[:, c, lo:lo + T], "xh"))
            cx.dma("sp", x_out(off, T).rearrange("(c p) t -> p c t", p=128), xh[:, :, lo:lo + T], ["xh"],
                   [("xout", off)], sk("xo"), is_out=True)
            if after_xout is not None:
                after_xout(cx, off)


FM_SPLITS = [0, 1, 3, 5, 6, 8, 9, 10]
TM_SPLITS = [2, 4, 7]


def prep_B_inputs(layer, j, w_in, norm1_g, lb_logits, hg_norm_g, pool_w, pool_scale, conv_w):
    cols = []
    for sidx in FM_SPLITS + TM_SPLITS:
        cols.append(w_in[layer][:, sidx * 512 + j * 128: sidx * 512 + (j + 1) * 128])
    w = np.ascontiguousarray(np.concatenate(cols, axis=1), dtype=np.float32)
    g1 = np.ascontiguousarray(norm1_g[layer].reshape(NCK, 128).T, dtype=np.float32)
    lbl = np.ascontiguousarray(lb_logits[:, j * 128:(j + 1) * 128].T, dtype=np.float32)
    vec = np.zeros((128, 8), np.float32)
    sl = slice(j * 128, (j + 1) * 128)
    vec[:, 0] = hg_norm_g[layer, sl]
    vec[:, 1] = pool_scale[layer, sl]
    vec[:, 2] = conv_w[layer, 0, sl]
    vec[:, 3] = conv_w[layer, 1, sl]
    vec[:, 4] = conv_w[layer, 2, sl]
    pw = np.ascontiguousarray(pool_w[layer, j], dtype=np.float32)
    _, cst = host_consts(POOL_WINDOWS[j])
    return {"w": w, "g1": g1, "lbl": lbl, "vec": vec, "pw": pw, "cst": np.ascontiguousarray(cst)}


def build_B(S, layer):
    nc = bass.Bass("TRN2", target_bir_lowering=False)
    Ltot = NMETA + S
    xT = nc.dram_tensor("xT", [D, Ltot], F32, kind="ExternalInput").ap()
    w = nc.dram_tensor("w", [D, 11 * 128], F32, kind="ExternalInput").ap()
    g1 = nc.dram_tensor("g1", [128, 8], F32, kind="ExternalInput").ap()
    lbl = nc.dram_tensor("lbl", [128, 4], F32, kind="ExternalInput").ap()
    vec = nc.dram_tensor("vec", [128, 8], F32, kind="ExternalInput").ap()
    pw = nc.dram_tensor("pw", [128, 128], F32, kind="ExternalInput").ap()
    cst = nc.dram_tensor("cst", [128, 10 * 128], F32, kind="ExternalInput").ap()
    br = nc.dram_tensor("br", [512, Ltot], BF16, kind="ExternalOutput").ap()
    P = Prog(nc)
    with ExitStack() as es:
        cx = Ctx(nc, P, es)
        emit_B(cx, S, layer, lambda off, T: xT[:, off:off + T], w, g1, lbl, vec, pw, cst,
               lambda off, T: br[:, off:off + T], pfx="b_")
        P.emit()
    return nc


def prep_C_weights(layer, w_in, w_branch, w_o, w_up, w_down):
    wg = w_in[layer][:, 11 * 512:].reshape(NCK, 128, 4, NCK, 128)
    wg = np.ascontiguousarray(wg.transpose(3, 2, 1, 0, 4)).reshape(NCK, 4, 128, NCK * 128)
    wb = w_branch[layer].reshape(4, 4, 128, NCK, 128)
    wb = np.ascontiguousarray(wb.transpose(3, 0, 2, 1, 4)).reshape(NCK, 4, 128, 4 * 128)
    wo = w_o[layer].reshape(NCK, 128, NCK, 128)
    wo = np.ascontiguousarray(wo.transpose(2, 1, 0, 3)).reshape(NCK, 128, NCK * 128)
    wu = w_up[layer].reshape(NCK, 128, 32, 128)
    wu = np.ascontiguousarray(wu.transpose(2, 1, 0, 3)).reshape(32, 128, NCK * 128)
    wd = w_down[layer].reshape(32, 128, NCK, 128)
    wd = np.ascontiguousarray(wd.transpose(2, 1, 0, 3)).reshape(NCK, 128, 32 * 128)
    return {"wg": wg, "wb": wb, "wo": wo, "wu": wu, "wd": wd}


def c_halves(ntok_x):
    tiles = [(0, NMETA)] + [(NMETA + TT * i, TT) for i in range(ntok_x // TT)]
    nh = (len(tiles) + 1) // 2
    return [tiles[:nh], tiles[nh:]] if len(tiles) > nh else [tiles]


def build_C(ntok_x, layer, final):
    nc = bass.Bass("TRN2", target_bir_lowering=False)
    NT = NMETA + ntok_x
    x = nc.dram_tensor("x", [D, NT], F32, kind="ExternalInput").ap()
    br = nc.dram_tensor("brc", [16, 128, NT], BF16, kind="ExternalInput").ap()
    g1 = nc.dram_tensor("g1", [128, 8], F32, kind="ExternalInput").ap()
    g2 = nc.dram_tensor("g2", [128, 8], F32, kind="ExternalInput").ap()
    gf = nc.dram_tensor("gf", [128, 8], F32, kind="ExternalInput").ap()
    cst = nc.dram_tensor("cst", [128, 10 * 128], F32, kind="ExternalInput").ap()
    wg = nc.dram_tensor("wg", [NCK, 4, 128, NCK * 128], F32, kind="ExternalInput").ap()
    wb = nc.dram_tensor("wb", [NCK, 4, 128, 4 * 128], F32, kind="ExternalInput").ap()
    wo = nc.dram_tensor("wo", [NCK, 128, NCK * 128], F32, kind="ExternalInput").ap()
    wu = nc.dram_tensor("wu", [32, 128, NCK * 128], F32, kind="ExternalInput").ap()
    wd = nc.dram_tensor("wd", [NCK, 128, 32 * 128], F32, kind="ExternalInput").ap()
    xo = nc.dram_tensor("xo", [D, NT], F32, kind="ExternalOutput").ap()
    P = Prog(nc)
    with ExitStack() as es:
        cx = Ctx(nc, P, es)
        emit_C(cx, layer, c_halves(ntok_x), lambda off, T: x[:, off:off + T],
               lambda off, T: br[:, :, off:off + T].rearrange("c p t -> p c t"), g1, g2, gf, cst, wg, wb, wo, wu, wd,
               lambda off, T: xo[:, off:off + T], final, pfx="c_")
        P.emit()
    return nc


def _r8(v):
    return np.ascontiguousarray(np.asarray(v, np.float32).reshape(NCK, 128).T)


def kernel_unfused(x, meta_tokens, lb_logits, norm1_g, w_in, hg_norm_g, pool_w, pool_scale, conv_w,
           w_branch, w_o, norm2_g, w_up, w_down, final_norm_g):
    f = lambda a: np.asarray(a, dtype=np.float32)
    x, meta_tokens, lb_logits, norm1_g, w_in = f(x), f(meta_tokens), f(lb_logits), f(norm1_g), f(w_in)
    hg_norm_g, pool_w, pool_scale, conv_w = f(hg_norm_g), f(pool_w), f(pool_scale), f(conv_w)
    w_branch, w_o, norm2_g, w_up, w_down, final_norm_g = f(w_branch), f(w_o), f(norm2_g), f(w_up), f(w_down), f(final_norm_g)
    B_, S, _ = x.shape
    NQ = 8 // B_
    SQ = S // NQ
    cores = list(range(8))
    xT = [np.ascontiguousarray(np.concatenate([meta_tokens, x[b]], axis=0).T) for b in range(B_)]
    cst2 = np.ascontiguousarray(host_consts(2)[1])
    for layer in range(DEPTH):
        ncB = build_B(S, layer)
        in_maps = []
        for r in cores:
            b, j = r // NQ, r % NQ
            im = prep_B_inputs(layer, j, w_in, norm1_g, lb_logits, hg_norm_g, pool_w, pool_scale, conv_w)
            im["xT"] = xT[b]
            in_maps.append(im)
        resB = run_bass_kernel_spmd(ncB, in_maps, core_ids=cores).results
        brs = [np.asarray(resB[r]["br"]) for r in cores]
        final = layer == DEPTH - 1
        ncC = build_C(SQ, layer, final)
        wts = prep_C_weights(layer, w_in, w_branch, w_o, w_up, w_down)
        g1, g2, gf = _r8(norm1_g[layer]), _r8(norm2_g[layer]), _r8(final_norm_g)
        in_maps = []
        for r in cores:
            b, q = r // NQ, r % NQ
            cols = np.concatenate([np.arange(NMETA), NMETA + q * SQ + np.arange(SQ)])
            im = dict(wts)
            im["x"] = np.ascontiguousarray(xT[b][:, cols])
            brc = np.empty((16, 128, NMETA + SQ), dtype=brs[0].dtype)
            for n in range(4):
                for j in range(4):
                    brc[j * 4 + n] = brs[b * NQ + j][n * 128:(n + 1) * 128][:, cols]
            im["brc"] = brc
            im["g1"], im["g2"], im["gf"], im["cst"] = g1, g2, gf, cst2
            in_maps.append(im)
        resC = run_bass_kernel_spmd(ncC, in_maps, core_ids=cores).results
        for b in range(B_):
            new = np.empty_like(xT[b])
            new[:, 0:NMETA] = np.asarray(resC[b * NQ]["xo"])[:, 0:NMETA]
            for q in range(NQ):
                new[:, NMETA + q * SQ: NMETA + (q + 1) * SQ] = np.asarray(resC[b * NQ + q]["xo"])[:, NMETA:]
            xT[b] = new
    out = np.stack([np.ascontiguousarray(xT[b][:, NMETA:].T) for b in range(B_)], axis=0)
    return out.astype(np.float32)


I32 = mybir.dt.int32
GROUPS = [[0, 1, 2, 3], [4, 5, 6, 7]]


def build_fused(S, depth=DEPTH):
    nc = bass.Bass("TRN2", target_bir_lowering=False)
    NQ = 4
    SQ = S // NQ
    NK = SQ // TT
    NT = NMETA + SQ
    Ltot = NMETA + S
    NTB = S // TT
    ext = lambda name, shape, dt=F32: nc.dram_tensor(name, shape, dt, kind="ExternalInput").ap()
    x0 = ext("x0", [D, NT])
    qcol = ext("qcol", [1, 8], I32)
    lbl = ext("lbl", [128, 4])
    cstB = ext("cstB", [128, 10 * 128])
    gf = ext("gf", [128, 8])
    L = []
    for l in range(depth):
        L.append(dict(
            wB=ext("wB%d" % l, [D, 11 * 128]), g1=ext("g1_%d" % l, [128, 8]), vec=ext("vec%d" % l, [128, 8]),
            pw=ext("pw%d" % l, [128, 128]), g2=ext("g2_%d" % l, [128, 8]),
            wg=ext("wg%d" % l, [NCK, 4, 128, NCK * 128]), wb=ext("wb%d" % l, [NCK, 4, 128, 4 * 128]),
            wo=ext("wo%d" % l, [NCK, 128, NCK * 128]), wu=ext("wu%d" % l, [32, 128, NCK * 128]),
            wd=ext("wd%d" % l, [NCK, 128, 32 * 128])))
    xo = nc.dram_tensor("xo", [D, NT], F32, kind="ExternalOutput").ap()
    xm = nc.dram_tensor("xm_i", [D, NMETA], F32).ap()
    xx = nc.dram_tensor("xx_i", [NK, 2, 512, TT], F32).ap()
    xg = nc.dram_tensor("xg_i", [NK, 2, NQ * 512, TT], F32).ap()
    brm = nc.dram_tensor("brm_i", [512, NMETA], BF16).ap()
    brx = nc.dram_tensor("brx_i", [NTB, 512, TT], BF16).ap()
    brgm = nc.dram_tensor("brgm_i", [NQ * 512, NMETA], BF16).ap()
    brgx = nc.dram_tensor("brgx_i", [NTB, NQ * 512, TT], BF16).ap()

    def ag(P, src, dst, r, w, key):
        P.op("pool", lambda e: e.collective_compute("AllGather", ALU.bypass, replica_groups=GROUPS,
                                                    ins=[src.opt()], outs=[dst.opt()]),
             reads=r, writes=w, dma_key=key, inc=1)

    def x_tile_ap(k_):
        return xx[k_].rearrange("h r t -> (h r) t")

    def x_in(off, T):
        return xm if off == 0 else x_tile_ap((off - NMETA) // TT)

    def x_loader(cx, ti, off, T, buf, bkey, semkey):
        if ti == 0:
            cx.dma("sp", buf[:, :, 0:T], xm.rearrange("(c p) t -> p c t", p=128), ["xm"], [bkey], semkey)
            return
        g = (ti - 1) * TT
        q, k_ = g // SQ, (g % SQ) // TT
        for h in range(2):
            src = xg[k_, h, q * 512:(q + 1) * 512, :].rearrange("(c p) t -> p c t", p=128)
            cx.dma("sp", buf[:, h * 4:(h + 1) * 4, 0:T], src, [("xg", k_, h)], [bkey], semkey)

    def br_out(off, T):
        return brm if off == 0 else brx[(off - NMETA) // TT]

    halves = c_halves(SQ)
    vals = {}
    es_glob = ExitStack()
    state = None
    for l in range(depth):
        P = Prog(nc, state)
        state = P.state
        with ExitStack() as es:
            cx = Ctx(nc, P, es)
            if l == 0:
                cx.dma("sp", xm, x0[:, 0:NMETA], [], ["xm"], "cpm")
                for k_ in range(NK):
                    cx.dma("sp", x_tile_ap(k_), x0[:, NMETA + k_ * TT:NMETA + (k_ + 1) * TT], [], [("xx", k_)], "cpx%d" % (k_ % 2))
            if l == 0:
                for k_ in range(NK):
                    for h in range(2):
                        ag(P, xx[k_, h], xg[k_, h], [("xx", k_)], [("xg", k_, h)], "ccx")

            def after_store(cx_, ti):
                if ti == 0:
                    ag(cx_.P, brm, brgm, [("brout_d", 0)], [("brg", 0)], "ccb")
                else:
                    ag(cx_.P, brx[ti - 1], brgx[ti - 1], [("brout_d", ti)], [("brg", ti)], "ccb")

            emit_B(cx, S, l, None, L[l]["wB"], L[l]["g1"], lbl, L[l]["vec"], L[l]["pw"], cstB, br_out,
                   pfx="b%d_" % l, x_loader=x_loader, after_store=after_store)
            P.op("sp", lambda e: e.dma_start(out=brm[0:1, 0:2], in_=brm[0:1, 0:2]),
                 reads=[("brg", t_) for t_ in range(NTB + 1)], writes=[], dma_key="fin", is_out=True)
            P.emit()
        P = Prog(nc, state)

        br_eng = "sp" if l < 2 else "act"

        def init_eng(handle, es2, br_eng=br_eng):
            if br_eng in vals:
                return
            vals[br_eng] = {}
            for kk in range(NK):
                reg = es_glob.enter_context(handle.register("qreg_%s%d" % (br_eng, kk)))
                handle.reg_load(reg, qcol[0:1, kk:kk + 1])
                vals[br_eng][kk] = handle.snap(reg, min_val=0, max_val=NTB - 1)

        P.init[br_eng] = init_eng

        def br_in(off, T, br_eng=br_eng):
            if off == 0:
                return brgm.rearrange("(c p) t -> p c t", p=128)
            kk = (off - NMETA) // TT
            return lambda: brgx[bass.ds(vals[br_eng][kk], 1)].rearrange("o (c p) t -> p (o c) t", p=128)

        final = l == depth - 1

        def after_xout(cx_, off):
            if off == 0:
                return
            k_ = (off - NMETA) // TT
            for h in range(2):
                cx_.P.op("pool", lambda e, k_=k_, h=h: e.collective_compute(
                    "AllGather", ALU.bypass, replica_groups=GROUPS, ins=[xx[k_, h].opt()], outs=[xg[k_, h].opt()]),
                    reads=[("xout", off)], writes=[("xg", k_, h)], dma_key="ccx", inc=1, is_out=True)

        with ExitStack() as es:
            cx = Ctx(nc, P, es)
            emit_C(cx, l, halves, x_in, br_in, L[l]["g1"], L[l]["g2"], gf, cstB,
                   L[l]["wg"], L[l]["wb"], L[l]["wo"], L[l]["wu"], L[l]["wd"],
                   (lambda off, T: xo[:, off:off + T]) if final else x_in, final, pfx="c%d_" % l, br_eng=br_eng,
                   after_xout=None if final else after_xout)
            P.emit()
    return nc


def fused_inputs(x, meta_tokens, lb_logits, norm1_g, w_in, hg_norm_g, pool_w, pool_scale, conv_w,
                 w_branch, w_o, norm2_g, w_up, w_down, final_norm_g, depth=DEPTH):
    B_, S, _ = x.shape
    NQ = 8 // B_
    SQ = S // NQ
    gf = _r8(final_norm_g)
    cw = [prep_C_weights(l, w_in, w_branch, w_o, w_up, w_down) for l in range(depth)]
    in_maps = []
    for r in range(8):
        b, q = r // NQ, r % NQ
        im = {}
        im["x0"] = np.ascontiguousarray(np.concatenate([meta_tokens, x[b, q * SQ:(q + 1) * SQ]], axis=0).T)
        qc = np.zeros((1, 8), np.int32)
        for kk in range(SQ // TT):
            qc[0, kk] = q * (SQ // TT) + kk
        im["qcol"] = qc
        im["gf"] = gf
        for l in range(depth):
            pb = prep_B_inputs(l, q, w_in, norm1_g, lb_logits, hg_norm_g, pool_w, pool_scale, conv_w)
            im["wB%d" % l], im["g1_%d" % l], im["vec%d" % l], im["pw%d" % l] = pb["w"], pb["g1"], pb["vec"], pb["pw"]
            im["lbl"], im["cstB"] = pb["lbl"], pb["cst"]
            im["g2_%d" % l] = _r8(norm2_g[l])
            for kname in ("wg", "wb", "wo", "wu", "wd"):
                im["%s%d" % (kname, l)] = cw[l][kname]
        in_maps.append(im)
    return in_maps


def kernel(x, meta_tokens, lb_logits, norm1_g, w_in, hg_norm_g, pool_w, pool_scale, conv_w,
           w_branch, w_o, norm2_g, w_up, w_down, final_norm_g):
    f = lambda a: np.asarray(a, dtype=np.float32)
    args = [f(a) for a in (x, meta_tokens, lb_logits, norm1_g, w_in, hg_norm_g, pool_w, pool_scale, conv_w,
                           w_branch, w_o, norm2_g, w_up, w_down, final_norm_g)]
    x = args[0]
    B_, S, _ = x.shape
    NQ = 8 // B_
    SQ = S // NQ
    nc = build_fused(S)
    in_maps = fused_inputs(*args)
    res = run_bass_kernel_spmd(nc, in_maps, core_ids=list(range(8))).results
    out = np.empty((B_, S, D), np.float32)
    for r in range(8):
        b, q = r // NQ, r % NQ
        out[b, q * SQ:(q + 1) * SQ] = np.asarray(res[r]["xo"])[:, NMETA:].T
    return out
```

```python
from contextlib import ExitStack
import numpy as np
import ml_dtypes
import concourse.bass as bass
import concourse.mybir as mybir
from concourse.bass_utils import run_bass_kernel_spmd

F32 = mybir.dt.float32
BF16 = mybir.dt.bfloat16
ALU = mybir.AluOpType
AF = mybir.ActivationFunctionType

D = 1024
NCK = 8
NMETA = 16
TT = 512
DEPTH = 4
EPS = 1e-6
POOL_WINDOWS = (2, 4, 8, 16)
SEM_CH = 30000
PARTS = {"conv", "pool", "hgrn", "attn"}
NFILL = 4


class Prog:
    ENGS = ("pe", "act", "dve", "pool", "sp")

    def __init__(self, nc, state=None):
        self.nc = nc
        self.ops = {e: [] for e in self.ENGS}
        self.state = state if state is not None else {"cnt": {e: 0 for e in self.ENGS}, "dma_cnt": {}, "handles": {},
                                                      "es": ExitStack()}
        self.cnt = self.state["cnt"]
        self.lastw = {}
        self.readers = {}
        self.known = {e: {} for e in self.ENGS}
        self.dma_cnt = self.state["dma_cnt"]
        self.semkeys = []
        self.semset = set()
        self.out_tokens = []
        self.init = {}
        self.excl = set(["M0", "M1", "Z0", "Z1", "G", "CS", "O", "TB"] + ["cp%d" % i for i in range(8)])

    def _sem(self, key):
        if key not in self.semset:
            self.semset.add(key)
            self.semkeys.append(key)
        return key

    LIMIT = None
    nops = 0

    def op(self, eng, fn, reads=(), writes=(), dma_key=None, is_out=False, inc=16):
        Prog.nops += 1
        if Prog.LIMIT is not None and Prog.nops > Prog.LIMIT and not is_out:
            return None
        deps = []
        for k in reads:
            t = self.lastw.get(k)
            if t is not None:
                deps.append(t)
            if k in self.excl:
                deps.extend(r for r in self.readers.get(k, ()) if r[2] != eng)
        for k in writes:
            t = self.lastw.get(k)
            if t is not None:
                deps.append(t)
            deps.extend(self.readers.get(k, ()))
        waits = {}
        kn = self.known[eng]
        for (sk, val, deng) in deps:
            if deng == eng and dma_key is None and eng == "pe":
                continue
            if kn.get(sk, 0) >= val:
                continue
            if waits.get(sk, 0) < val:
                waits[sk] = val
        for sk, val in waits.items():
            kn[sk] = val
        if dma_key is not None:
            sk = self._sem(("dma", dma_key))
            n = self.dma_cnt.get(sk, 0) + 1
            self.dma_cnt[sk] = n
            done = (sk, inc * n, "dma")
        else:
            idx = self.cnt[eng]
            self.cnt[eng] += 1
            sk = self._sem(("eng", eng, idx // SEM_CH))
            done = (sk, idx % SEM_CH + 1, eng)
            inc = 1
        self.ops[eng].append((list(waits.items()), fn, sk, inc))
        for k in writes:
            self.lastw[k] = done
            self.readers[k] = []
        for k in reads:
            self.readers.setdefault(k, []).append(done)
        if is_out:
            self.out_tokens.append(done)
        return done

    def emit(self):
        nc = self.nc
        final_waits = {}
        for (sk, val, _e) in self.out_tokens:
            if final_waits.get(sk, 0) < val:
                final_waits[sk] = val
        with ExitStack() as es:
            sems = self.state["handles"]
            for sk in self.semkeys:
                if sk not in sems:
                    sems[sk] = self.state["es"].enter_context(nc.semaphore("s%d" % len(sems)))
            block = es.enter_context(nc.Block())

            def run(engname, handle):
                if engname in self.init:
                    self.init[engname](handle, es)
                for (waits, fn, sk, inc) in self.ops[engname]:
                    for wk, wv in waits:
                        handle.wait_ge(sems[wk], wv)
                    fn(handle).then_inc(sems[sk], inc)
                if engname == "sp":
                    for wk, wv in final_waits.items():
                        handle.wait_ge(sems[wk], wv)

            @block.tensor
            def _(e):
                run("pe", e)

            @block.scalar
            def _(e):
                run("act", e)

            @block.vector
            def _(e):
                run("dve", e)

            @block.gpsimd
            def _(e):
                run("pool", e)

            @block.sync
            def _(e):
                run("sp", e)


class Ctx:
    def __init__(self, nc, P, es):
        self.nc, self.P, self.es = nc, P, es

    def sb(self, name, shape, dt):
        return self.es.enter_context(self.nc.sbuf_tensor(name, shape, dt))

    def ps(self, name, shape, dt=F32):
        return self.es.enter_context(self.nc.psum_tensor(name, shape, dt))

    def mm(self, out, lhsT, rhs, start, stop, r, w, skip=False):
        if skip:
            self.P.op("pe", lambda e: e.matmul(out, lhsT, rhs, start=start, stop=stop, skip_group_check=True),
                      reads=r, writes=w)
        else:
            self.P.op("pe", lambda e: e.matmul(out, lhsT, rhs, start=start, stop=stop), reads=r, writes=w)

    def tr(self, out, in_, ident, r, w):
        self.P.op("pe", lambda e: e.transpose(out, in_, ident), reads=r, writes=w)

    def act(self, out, in_, func, r, w, bias=None, scale=None):
        kw = {}
        if bias is not None:
            kw["bias"] = bias
        if scale is not None:
            kw["scale"] = scale
        self.P.op("act", lambda e: e.activation(out=out, in_=in_, func=func, **kw), reads=r, writes=w)

    def tt(self, eng, out, in0, in1, op, r, w):
        self.P.op(eng, lambda e: e.tensor_tensor(out=out, in0=in0, in1=in1, op=op), reads=r, writes=w)

    def ts(self, eng, out, in0, s1, op0, r, w, s2=None, op1=None):
        if op1 is None:
            self.P.op(eng, lambda e: e.tensor_scalar(out=out, in0=in0, scalar1=s1, scalar2=None, op0=op0),
                      reads=r, writes=w)
        else:
            self.P.op(eng, lambda e: e.tensor_scalar(out=out, in0=in0, scalar1=s1, scalar2=s2, op0=op0, op1=op1),
                      reads=r, writes=w)

    def stt(self, eng, out, in0, scalar, in1, op0, op1, r, w):
        self.P.op(eng, lambda e: e.scalar_tensor_tensor(out=out, in0=in0, scalar=scalar, in1=in1, op0=op0, op1=op1),
                  reads=r, writes=w)

    def copy(self, eng, out, in_, r, w):
        if eng == "act":
            self.P.op("act", lambda e: e.activation(out=out, in_=in_, func=AF.Copy), reads=r, writes=w)
        else:
            self.P.op(eng, lambda e: e.tensor_copy(out=out, in_=in_), reads=r, writes=w)

    def recip(self, out, in_, r, w):
        self.P.op("dve", lambda e: e.reciprocal(out=out, in_=in_), reads=r, writes=w)

    def memset(self, eng, ap, val, w):
        self.P.op(eng, lambda e: e.memset(ap, val), writes=w)

    def dma(self, eng, out, in_, r, w, key, is_out=False):
        self.P.op(eng, lambda e: e.dma_start(out=out, in_=(in_() if callable(in_) else in_)), reads=r, writes=w,
                  dma_key=key, is_out=is_out)


def seq_tiles(S):
    return [(0, NMETA)] + [(NMETA + TT * i, TT) for i in range(S // TT)]


def gblock(g):
    return (0, NMETA) if g == 0 else (NMETA + 128 * (g - 1), 128)


def host_consts(window):
    i = np.arange(128)
    c = {}
    c["ident"] = np.eye(128, dtype=np.float32)
    c["ones"] = np.ones((128, 128), np.float32)
    c["strict"] = (i[:, None] < i[None, :]).astype(np.float32)
    c["negun"] = -(i[:, None] >= i[None, :]).astype(np.float32)
    c["triu"] = (i[:, None] <= i[None, :]).astype(np.float32)
    w = window
    s, t = i[:, None], i[None, :]
    band = ((s <= t) & (s > t - w)).astype(np.float32)
    c["mdiag"] = band / w - np.eye(128, dtype=np.float32)
    c["moff"] = ((s - 128) > (t - w)).astype(np.float32) / w
    mf = np.zeros((128, 128), np.float32)
    mf[:16] = (s[:16] > (16 + t - w)).astype(np.float32) / w
    c["mofff"] = mf
    bm = np.zeros((128, 128), np.float32)
    bm[:16, :16] = band[:16, :16]
    c["bmeta"] = bm
    invn = np.zeros((128, 128), np.float32)
    invn[:, :16] = 1.0 / np.minimum(w, np.arange(16) + 1.0)[None, :]
    c["invn"] = invn
    names = ["ident", "ones", "strict", "negun", "triu", "mdiag", "moff", "mofff", "bmeta", "invn"]
    return names, np.concatenate([c[n] for n in names], axis=1)


CONST_NAMES = ["ident", "ones", "strict", "negun", "triu", "mdiag", "moff", "mofff", "bmeta", "invn"]


def emit_B(cx, S, layer, xT_ap, w_ap, g1_ap, lbl_ap, vec_ap, pw_ap, cst_ap, out_ap, pfx="", x_loader=None,
           after_store=None):
    P = cx.P
    Ltot = NMETA + S
    tiles = seq_tiles(S)
    NB = 1 + S // 128
    k = lambda s: pfx + s
    sk = lambda s: "B." + s

    w_bf = cx.sb(k("w_bf"), [128, NCK, 11 * 128], BF16)
    cst_f = cx.sb(k("cst_f"), [128, 10 * 128], F32)
    cst_b = cx.sb(k("cst_b"), [128, 10 * 128], BF16)
    g1 = cx.sb(k("g1"), [128, 8], F32)
    lbl = cx.sb(k("lbl"), [128, 4], F32)
    vec = cx.sb(k("vec"), [128, 8], F32)
    pw_b = cx.sb(k("pw_b"), [128, 128], BF16)
    sm = cx.sb(k("sm"), [128, 16], F32)
    kT = cx.sb(k("kT"), [128, Ltot], BF16)
    vc = cx.sb(k("vc"), [128, NB, 128], BF16)
    Sst = cx.sb(k("Sst"), [128, 128], F32)
    Ssc = cx.sb(k("Ssc"), [128, 128], BF16)
    ubuf = cx.sb(k("ubuf"), [128, TT + 2], F32)
    pv = cx.sb(k("pv"), [128, 5, 128], BF16)
    onesf = cx.sb(k("onesf"), [128, TT], F32)
    xt = [cx.sb(k("xt%d" % i), [128, NCK, TT], F32) for i in range(2)]
    sqb = cx.sb(k("sqb"), [128, NCK, TT], BF16)
    hT = cx.sb(k("hT"), [128, NCK, TT], BF16)
    rstd = cx.sb(k("rstd"), [128, TT], F32)
    proj = cx.sb(k("proj"), [128, 8, TT], F32)
    qT = cx.sb(k("qT"), [128, TT], BF16)
    itok = cx.sb(k("itok"), [128, 4, 128], BF16)
    brout = [cx.sb(k("brout%d" % i), [128, 4, TT], BF16) for i in range(2)]
    t1 = cx.sb(k("t1"), [128, TT], F32)
    t2 = cx.sb(k("t2"), [128, TT], F32)
    fval = cx.sb(k("fval"), [128, TT], F32)
    kk = cx.sb(k("kk"), [128, TT], F32)
    lf = cx.sb(k("lf"), [128, TT], F32)
    Bc = cx.sb(k("Bc"), [128, TT + 1], F32)
    cex = [cx.sb(k("cex%d" % i), [128, 64], F32) for i in range(3)]
    qe = cx.sb(k("qe"), [128, TT], BF16)
    ke = cx.sb(k("ke"), [128, TT], BF16)
    kdT = cx.sb(k("kdT"), [128, TT], BF16)
    kdtok = cx.sb(k("kdtok"), [128, 4, 128], BF16)
    scm = cx.sb(k("scm"), [128, 64], BF16)
    dsm = cx.sb(k("dsm"), [128, 4, 8], F32)
    osb = cx.sb(k("osb"), [128, TT], F32)
    osq = cx.sb(k("osq"), [128, TT], BF16)
    uTb = cx.sb(k("uTb"), [128, 128], BF16)
    uTf = cx.sb(k("uTf"), [128, 16], F32)
    Ef2 = cx.sb(k("Ef2"), [128, 2 * TT], F32)
    spb2 = cx.sb(k("spb2"), [128, 2 * TT], BF16)
    lg2 = cx.sb(k("lg2"), [128, 2 * TT], F32)
    Ab2 = cx.sb(k("Ab2"), [128, 2 * TT], BF16)
    Ef = [Ef2[:, i * TT:(i + 1) * TT] for i in range(2)]
    spb = [spb2[:, i * TT:(i + 1) * TT] for i in range(2)]
    lg = [lg2[:, i * TT:(i + 1) * TT] for i in range(2)]
    Ab = [Ab2[:, i * TT:(i + 1) * TT] for i in range(2)]
    h3 = lambda t_: t_[:, :].rearrange("p (h t) -> p h t", h=2)
    carry = [cx.sb(k("carry%d" % i), [128, TT], F32) for i in range(2)]
    ob = cx.sb(k("ob"), [128, 4 * 128], BF16)
    zbf = cx.sb(k("zbf"), [128, TT], BF16)
    M = [cx.ps(k("pM%d" % i), [128, TT]) for i in range(2)]
    Z = [cx.ps(k("pZ%d" % i), [128, TT]) for i in range(2)]
    G = cx.ps(k("pG"), [128, TT])
    CS = cx.ps(k("pCS"), [128, TT])
    O = cx.ps(k("pO"), [128, TT])
    TB = cx.ps(k("pTB"), [128, 2 * TT], BF16)

    def C(name, rows=128, cols=128, bf=True):
        i = CONST_NAMES.index(name)
        src = cst_b if bf else cst_f
        return src[0:rows, i * 128:i * 128 + cols]

    cx.dma("sp", cst_f[:], cst_ap, [], ["cst_f"], sk("cst_f"))
    cx.dma("pool", cst_b[:], cst_ap, [], ["cst_b"], sk("cst_b"))
    cx.dma("sp", g1[:], g1_ap, [], ["g1"], sk("g1"))
    cx.dma("sp", lbl[:], lbl_ap, [], ["lbl"], sk("lbl"))
    cx.dma("sp", vec[:], vec_ap, [], ["vec"], sk("vec"))
    cx.dma("pool", pw_b[:], pw_ap, [], ["pw_b"], sk("pw_b"))
    for c in range(NCK):
        cx.dma("pool", w_bf[:, c, :], w_ap[c * 128:(c + 1) * 128, :], [], [("w", c)], sk("w%d" % c))
    cx.memset("pool", onesf[:], 1.0, ["onesf"])
    cx.memset("pool", zbf[:], 0.0, ["zbf"])
    cx.memset("pool", Sst[:], 0.0, ["S"])
    cx.memset("pool", ubuf[:, 0:2], 0.0, ["ubuf"])
    cx.memset("pool", Bc[:, 0:1], 0.0, ["Bc"])
    P.op("dve", lambda e: e.reduce_max(out=sm[:, 0:1], in_=lbl[:], axis=mybir.AxisListType.X), reads=["lbl"], writes=["sm"])
    cx.ts("dve", sm[:, 1:2], sm[:, 0:1], -1.0, ALU.mult, ["sm"], ["sm"])
    cx.act(sm[:, 8:12], lbl[:], AF.Exp, ["sm", "lbl"], ["sm"], bias=sm[:, 1:2])
    P.op("dve", lambda e: e.reduce_sum(out=sm[:, 2:3], in_=sm[:, 8:12], axis=mybir.AxisListType.X), reads=["sm"], writes=["sm"])
    cx.recip(sm[:, 3:4], sm[:, 2:3], ["sm"], ["sm"])
    if layer == 0:
        cx.memset("dve", sm[:, 4:5], 0.0, ["sm"])
    else:
        P.op("dve", lambda e: e.reduce_sum(out=sm[:, 4:5], in_=sm[:, 9:9 + layer], axis=mybir.AxisListType.X), reads=["sm"], writes=["sm"])
        cx.tt("dve", sm[:, 4:5], sm[:, 4:5], sm[:, 3:4], ALU.mult, ["sm"], ["sm"])
    cx.ts("dve", sm[:, 5:6], sm[:, 4:5], -1.0, ALU.mult, ["sm"], ["sm"], s2=1.0, op1=ALU.add)
    lb_ap, oml_ap = sm[:, 4:5], sm[:, 5:6]

    def load_x(ti):
        off, T = tiles[ti]
        buf = xt[ti % 2]
        if x_loader is not None:
            x_loader(cx, ti, off, T, buf, ("xt", ti % 2), sk("xt%d" % (ti % 2)))
            return
        cx.dma("sp", buf[:, :, 0:T], xT_ap(off, T).rearrange("(c p) t -> p c t", p=128), ["xg"], [("xt", ti % 2)],
               sk("xt%d" % (ti % 2)))

    load_x(0)
    pending_ag = []
    for ti, (off, T) in enumerate(tiles):
        if ti + 1 < len(tiles):
            load_x(ti + 1)
        x = xt[ti % 2]
        xk = ("xt", ti % 2)
        meta = (ti == 0)
        if meta:
            blocks = [(0, NMETA)]
            gb0 = 0
        else:
            blocks = [(128 * m, 128) for m in range(T // 128)]
            gb0 = 4 * (ti - 1) + 1
        bo = brout[ti % 2]
        bok = ("brout", ti % 2)


        cx.act(sqb[:, :, 0:T], x[:, :, 0:T], AF.Square, [xk], ["sqb"])
        for c in range(NCK):
            cx.mm(M[0][:, 0:T], C("ones"), sqb[:, c, 0:T], c == 0, c == NCK - 1, ["sqb", "cst_b"], ["M0"])
        cx.act(rstd[:, 0:T], M[0][:, 0:T], AF.Ln, ["M0"], ["rstd"], bias=EPS, scale=1.0 / D)
        cx.act(rstd[:, 0:T], rstd[:, 0:T], AF.Exp, ["rstd"], ["rstd"], scale=-0.5)
        for c in range(NCK):
            cx.stt("dve", hT[:, c, 0:T], x[:, c, 0:T], g1[:, c:c + 1], rstd[:, 0:T],
                   ALU.mult, ALU.mult, [xk, "rstd", "g1"], [("hT", c)])
        hTk = [("hT", c) for c in range(NCK)]
        wk = [("w", c) for c in range(NCK)]

        for n in range(8):
            pm = M[n % 2]
            pk = "M%d" % (n % 2)
            for c in range(NCK):
                cx.mm(pm[:, 0:T], w_bf[:, c, n * 128:(n + 1) * 128], hT[:, c, 0:T], c == 0, c == NCK - 1,
                      hTk + wk, [pk])
            if n == 3:
                cx.copy("act", qT[:, 0:T], pm[:, 0:T], [pk], ["qT"])
            elif n == 4:
                cx.act(kT[:, off:off + T], pm[:, 0:T], AF.Copy, [pk], [("kT", ti)], scale=0.125)
            else:
                cx.copy("act", proj[:, n, 0:T], pm[:, 0:T], [pk], [("proj", n)])

        for m, (bs, bl) in enumerate(blocks):
            pm = M[m % 2]
            pk = "M%d" % (m % 2)
            for c in range(NCK):
                cx.mm(pm[0:bl, 0:384], hT[:, c, bs:bs + bl], w_bf[:, c, 1024:1408], c == 0, c == NCK - 1,
                      hTk + wk, [pk])
            cx.copy("act", itok[0:bl, m, :], pm[0:bl, 0:128], [pk], [("itok", m)])
            cx.copy("dve", pv[0:bl, 1 + m, :], pm[0:bl, 128:256], [pk], [("pv", 1 + m)])
            cx.copy("act", vc[0:bl, gb0 + m, :], pm[0:bl, 256:384], [pk], [("vc", gb0 + m)])


        if "conv" in PARTS:
            cx.tt("pool", ubuf[:, 2:2 + T], proj[:, 7, 0:T], proj[:, 5, 0:T], ALU.mult, [("proj", 7), ("proj", 5)], ["ubuf"])
            cx.ts("pool", t2[:, 0:T], ubuf[:, 2:2 + T], vec[:, 4:5], ALU.mult, ["ubuf", "vec"], ["t2"])
            cx.stt("dve", t2[:, 0:T], ubuf[:, 1:1 + T], vec[:, 3:4], t2[:, 0:T], ALU.mult, ALU.add, ["ubuf", "vec", "t2"], ["t2"])
            cx.stt("dve", t2[:, 0:T], ubuf[:, 0:T], vec[:, 2:3], t2[:, 0:T], ALU.mult, ALU.add, ["ubuf", "vec", "t2"], ["t2"])
            cx.tt("pool", bo[:, 3, 0:T], t2[:, 0:T], proj[:, 6, 0:T], ALU.mult, ["t2", ("proj", 6)], [bok])
            cx.copy("pool", ubuf[:, 0:2], ubuf[:, T:T + 2], ["ubuf"], ["ubuf"])

        if "pool" in PARTS:
            for m, (bs, bl) in enumerate(blocks):
                pm = M[m % 2]
                pk = "M%d" % (m % 2)
                if meta:
                    cx.mm(pm[:, 0:16], pv[0:16, 1, :], C("bmeta", 16, 16), True, True, [("pv", 1), "cst_b"], [pk])
                    cx.mm(pm[:, 16:32], pv[0:16, 1, :], C("ident", 16, 16), True, True, [("pv", 1), "cst_b"], [pk])
                    cx.tt("dve", uTf[:, 0:16], pm[:, 0:16], C("invn", 128, 16, bf=False), ALU.mult, [pk, "cst_f"], ["uTf"])
                    cx.tt("dve", uTb[:, 0:16], uTf[:, 0:16], pm[:, 16:32], ALU.subtract, [pk, "uTf"], ["uTb"])
                else:
                    prev_rows = 16 if (ti == 1 and m == 0) else 128
                    moff = C("mofff", 16, 128) if (ti == 1 and m == 0) else C("moff")
                    cx.mm(pm[:, 0:128], pv[:, 1 + m, :], C("mdiag"), True, False, [("pv", 1 + m), "cst_b"], [pk])
                    cx.mm(pm[:, 0:128], pv[0:prev_rows, m, :], moff, False, True, [("pv", m), "cst_b"], [pk])
                    cx.copy("dve", uTb[:, 0:bl], pm[:, 0:bl], [pk], ["uTb"])
                cx.mm(pm[:, 128:128 + bl], pw_b[:], uTb[:, 0:bl], True, True, ["uTb", "pw_b"], [pk])
                cx.act(bo[:, 1, bs:bs + bl], pm[:, 128:128 + bl], AF.Copy, [pk, "vec"], [bok], scale=vec[:, 1:2])
            lastm = len(blocks) - 1
            lbl_rows = blocks[lastm][1]
            cx.copy("pool", pv[0:lbl_rows, 0, :], pv[0:lbl_rows, 1 + lastm, :], [("pv", 1 + lastm), ("pv", 0)], [("pv", 0)])

        if "hgrn" in PARTS:
            cx.act(t1[:, 0:T], proj[:, 1, 0:T], AF.Exp, [("proj", 1)], ["t1"], scale=-1.0)
            cx.ts("dve", t1[:, 0:T], t1[:, 0:T], 1.0, ALU.add, ["t1"], ["t1"])
            cx.recip(t1[:, 0:T], t1[:, 0:T], ["t1"], ["t1"])
            cx.ts("dve", fval[:, 0:T], t1[:, 0:T], oml_ap, ALU.mult, ["t1", "sm"], ["fval"], s2=lb_ap, op1=ALU.add)
            cx.act(lf[:, 0:T], fval[:, 0:T], AF.Ln, ["fval"], ["lf"])
            cx.ts("pool", kk[:, 0:T], fval[:, 0:T], -1.0, ALU.mult, ["fval"], ["kk"], s2=1.0, op1=ALU.add)
            P.op("dve", lambda e, T=T: e.tensor_tensor_scan(out=Bc[:, 1:1 + T], data0=onesf[:, 0:T], data1=lf[:, 0:T],
                                                            initial=0.0, op0=ALU.mult, op1=ALU.add),
                 reads=["lf", "onesf"], writes=["Bc"])
            CL = 16 if meta else 64
            nch = T // CL
            Bv = Bc[:, 1:1 + T].rearrange("p (c l) -> p c l", l=CL)
            Bp = Bc[:, 0:T].rearrange("p (c l) -> p c l", l=CL)[:, :, 0:1]
            Bm = Bv[:, :, CL // 2 - 1:CL // 2]
            Be = Bv[:, :, CL - 1:CL]
            v3 = lambda t_: t_[:, 0:T].rearrange("p (c l) -> p c l", l=CL)
            cx.tt("dve", dsm[:, 2, 0:nch].rearrange("p (c o) -> p c o", o=1), Bm, Bp, ALU.subtract, ["Bc"], ["dsm"])
            cx.tt("dve", dsm[:, 3, 0:nch].rearrange("p (c o) -> p c o", o=1), Be, Bp, ALU.subtract, ["Bc"], ["dsm"])
            cx.act(dsm[:, 0:2, 0:nch], dsm[:, 2:4, 0:nch], AF.Exp, ["dsm"], ["dsm"])
            cx.tt("dve", v3(lf), Bv, Bm.to_broadcast([128, nch, CL]), ALU.subtract, ["Bc", "lf"], ["lf"])
            cx.tt("dve", v3(fval), Bv, Be.to_broadcast([128, nch, CL]), ALU.subtract, ["Bc", "fval", "kk"], ["fval"])
            cx.act(t1[:, 0:T], lf[:, 0:T], AF.Exp, ["lf"], ["t1"])
            cx.tt("dve", qe[:, 0:T], t1[:, 0:T], proj[:, 0, 0:T], ALU.mult, ["t1", ("proj", 0)], ["qe"])
            cx.act(t2[:, 0:T], lf[:, 0:T], AF.Exp, ["lf"], ["t2"], scale=-1.0)
            cx.tt("pool", ke[:, 0:T], t2[:, 0:T], kk[:, 0:T], ALU.mult, ["t2", "kk"], ["ke"])
            cx.act(t1[:, 0:T], fval[:, 0:T], AF.Exp, ["fval"], ["t1"], scale=-1.0)
            cx.tt("pool", kdT[:, 0:T], t1[:, 0:T], kk[:, 0:T], ALU.mult, ["t1", "kk"], ["kdT"])
            for ci in range(nch):
                c0 = ci * CL
                m = c0 // 128
                r0 = c0 % 128
                cx.tr(TB[r0:r0 + CL, 0:128], kdT[:, c0:c0 + CL], C("ident"), ["kdT", "cst_b"], ["TB"])
                cx.copy("act", kdtok[r0:r0 + CL, m, :], TB[r0:r0 + CL, 0:128], ["TB"], ["kdtok"])
                zb = Z[ci % 2]
                zk = "Z%d" % (ci % 2)
                cx.mm(zb[r0:r0 + CL, 0:CL], ke[:, c0:c0 + CL], qe[:, c0:c0 + CL], True, True, ["ke", "qe"], [zk])
                cx.tt("dve", scm[r0:r0 + CL, 0:CL], zb[r0:r0 + CL, 0:CL], C("triu", 128, 128, bf=False)[r0:r0 + CL, r0:r0 + CL],
                      ALU.mult, [zk, "cst_f"], ["scm"])
                cx.ts("dve", Ssc[:], Sst[:], dsm[:, 0, ci:ci + 1], ALU.mult, ["S", "dsm"], ["Ssc"])
                cx.mm(G[:, c0:c0 + CL], Ssc[:], qe[:, c0:c0 + CL], True, False, ["Ssc", "qe"], ["G"])
                cx.mm(G[:, c0:c0 + CL], itok[r0:r0 + CL, m, :], scm[r0:r0 + CL, 0:CL], False, True, [("itok", m), "scm"], ["G"])
                mb = M[ci % 2]
                mk = "M%d" % (ci % 2)
                cx.mm(mb[:, 0:128], kdtok[r0:r0 + CL, m, :], itok[r0:r0 + CL, m, :], True, True, ["kdtok", ("itok", m)], [mk])
                cx.stt("dve", Sst[:], Sst[:], dsm[:, 1, ci:ci + 1], mb[:, 0:128], ALU.mult, ALU.add, ["S", "dsm", mk], ["S"])
            cx.copy("act", osb[:, 0:T], G[:, 0:T], ["G"], ["osb"])
            cx.act(osq[:, 0:T], osb[:, 0:T], AF.Square, ["osb"], ["osq"])
            cx.mm(CS[:, 0:T], C("ones"), osq[:, 0:T], True, True, ["osq", "cst_b"], ["CS"])
            cx.act(t2[:, 0:T], CS[:, 0:T], AF.Ln, ["CS"], ["t2"], bias=EPS, scale=1.0 / 128)
            cx.act(t2[:, 0:T], t2[:, 0:T], AF.Exp, ["t2"], ["t2"], scale=-0.5)
            cx.act(t1[:, 0:T], proj[:, 2, 0:T], AF.Exp, [("proj", 2)], ["t1"], scale=-1.0)
            cx.ts("dve", t1[:, 0:T], t1[:, 0:T], 1.0, ALU.add, ["t1"], ["t1"])
            cx.recip(t1[:, 0:T], t1[:, 0:T], ["t1"], ["t1"])
            cx.stt("dve", osb[:, 0:T], osb[:, 0:T], vec[:, 0:1], t2[:, 0:T], ALU.mult, ALU.mult, ["osb", "vec", "t2"], ["osb"])
            cx.tt("dve", bo[:, 0, 0:T], osb[:, 0:T], t1[:, 0:T], ALU.mult, ["osb", "t1"], [bok])

        if "attn" in PARTS:
            nsub = len(blocks)
            SW = blocks[0][1]
            last_gb = gb0 + nsub - 1
            cx.mm(O[:, 0:T], zbf[:, 0:128], zbf[:, 0:T], True, False, ["zbf"], ["O"])
            while pending_ag:
                after_store(cx, pending_ag.pop(0))
            its = list(range(last_gb, -1, -1))
            n_it = len(its)
            for h in range(2):
                cx.memset("pool", carry[h][:, 0:T], 0.0, [("carry", h)])
            Zb = [[(Z[0], "Z0"), (G, "G")], [(Z[1], "Z1"), (M[0], "M0")]]
            CSb = [(CS, "CS"), (M[1], "M1")]

            def geom(i):
                kb = its[i]
                ks, KL = gblock(kb)
                kti = 0 if kb == 0 else (kb - 1) // 4 + 1
                diag = ks >= off
                qc0 = max(off, ks) - off
                return kb, ks, KL, kti, diag, qc0, T - qc0

            def phA1(i, h):
                kb, ks, KL, kti, diag, qc0, N = geom(i)
                hp = slice(64 * h, 64 * h + 64)
                zb, zk = Zb[h][i % 2]
                cx.mm(zb[0:KL, 0:N], kT[hp, ks:ks + KL], qT[hp, qc0:qc0 + N], True, True, [("kT", kti), "qT"], [zk])

            def phA(i, h):
                kb, ks, KL, kti, diag, qc0, N = geom(i)
                zb, zk = Zb[h][i % 2]
                ef, efk = Ef[h], ("Ef", h)
                sp, spk = spb[h], ("spb", h)
                cx.act(ef[0:KL, 0:N], zb[0:KL, 0:N], AF.Exp, [zk], [efk])
                cx.act(sp[0:KL, 0:N], ef[0:KL, 0:N], AF.Ln, [efk], [spk], bias=1.0)
                if diag:
                    cx.tt("pool", sp[0:KL, 0:SW], sp[0:KL, 0:SW], C("strict", KL, SW), ALU.mult, [spk, "cst_b"], [spk])

            def phA_both(i):
                return

                kb, ks, KL, kti, diag, qc0, N = geom(i)
                cx.act(h3(spb2)[0:KL, :, 0:N], h3(Ef2)[0:KL, :, 0:N], AF.Ln, [("Ef", 0), ("Ef", 1)],
                       [("spb", 0), ("spb", 1)], bias=1.0)
                if diag:
                    for h in range(2):
                        sp, spk = spb[h], ("spb", h)
                        cx.tt("pool", sp[0:KL, 0:SW], sp[0:KL, 0:SW], C("strict", KL, SW), ALU.mult, [spk, "cst_b"], [spk])

            def phC_both(i):
                return
                kb, ks, KL, kti, diag, qc0, N = geom(i)
                cx.act(h3(Ab2)[0:KL, :, 0:N], h3(lg2)[0:KL, :, 0:N], AF.Exp, [("lg", 0), ("lg", 1)],
                       [("Ab", 0), ("Ab", 1)])

            def phB(i, h):
                kb, ks, KL, kti, diag, qc0, N = geom(i)
                sp, spk = spb[h], ("spb", h)
                lgt, lgk = lg[h], ("lg", h)
                cr, crk = carry[h], ("carry", h)
                gb, gk = Zb[h][i % 2]
                cb, ck_ = CSb[h]
                cx.mm(gb[0:KL, 0:N], C("negun", KL, KL), sp[0:KL, 0:N], False, True, [spk, "cst_b"], [gk], skip=True)
                if kb > 0:
                    cx.mm(cb[:, 0:N], C("ones", KL, 128), sp[0:KL, 0:N], True, True, [spk, "cst_b"], [ck_])
                cx.tt("dve", lgt[0:KL, 0:N], gb[0:KL, 0:N], cr[0:KL, qc0:qc0 + N], ALU.subtract, [gk, crk], [lgk])
                if kb > 0:
                    cx.tt("dve", cr[:, qc0:qc0 + N], cr[:, qc0:qc0 + N], cb[:, 0:N], ALU.add, [ck_, crk], [crk])

            def phC(i, h):
                kb, ks, KL, kti, diag, qc0, N = geom(i)
                hp = slice(64 * h, 64 * h + 64)
                lgt, lgk = lg[h], ("lg", h)
                ab, abk = Ab[h], ("Ab", h)
                cx.act(ab[0:KL, 0:N], lgt[0:KL, 0:N], AF.Exp, [lgk], [abk])
                if diag:
                    cx.tt("pool", ab[0:KL, 0:SW], ab[0:KL, 0:SW], C("strict", KL, SW), ALU.mult, [abk, "cst_b"], [abk])
                cx.mm(O[hp, qc0:qc0 + N], vc[0:KL, kb, hp], ab[0:KL, 0:N], False, (kb == 0),
                      [abk, ("vc", kb)], ["O"])

            for t in range(n_it + 2):
                for h in range(2):
                    if t < n_it:
                        phA1(t, h)
                if 0 <= t - 2 < n_it:
                    phC_both(t - 2)
                for h in range(2):
                    if 0 <= t - 2 < n_it:
                        phC(t - 2, h)
                for h in range(2):
                    if 0 <= t - 1 < n_it:
                        phB(t - 1, h)
                for h in range(2):
                    if t < n_it:
                        phA(t, h)
                for _ in range(NFILL):
                    cx.mm(TB[:, :].bitcast(F32)[:, 0:T], C("ones"), qT[:, 0:T], True, True, ["qT", "cst_b"], ["TB"])
            cx.copy("dve", bo[:, 2, 0:T], O[:, 0:T], ["O"], [bok])

        cx.dma("sp", out_ap(off, T).rearrange("(n p) t -> p n t", p=128), bo[:, :, 0:T], [bok], [("brout_d", ti)],
               sk("bo%d" % (ti % 2)), is_out=True)
        if after_store is not None:
            if ti == len(tiles) - 1:
                after_store(cx, ti)
            else:
                pending_ag.append(ti)


def emit_C(cx, layer, halves, x_in, br_in, g1_ap, g2_ap, gf_ap, cst_ap, wg_ap, wb_ap, wo_ap, wu_ap, wd_ap,
           x_out, final, pfx="", br_eng="sp", after_xout=None):
    P = cx.P
    k = lambda s: pfx + s
    sk = lambda s: "C." + s
    HT = max(sum(T for _, T in h) for h in halves)
    xh = cx.sb(k("xh"), [128, NCK, HT], F32)
    hh = cx.sb(k("hh"), [128, NCK, HT], BF16)
    big = cx.sb(k("big"), [128, 32, HT], BF16)
    mixb = cx.sb(k("mixb"), [128, NCK, HT], BF16)
    macc = cx.sb(k("macc"), [128, HT], F32)
    g1 = cx.sb(k("cg1"), [128, 8], F32)
    g2 = cx.sb(k("cg2"), [128, 8], F32)
    gf = cx.sb(k("cgf"), [128, 8], F32)
    ones_b = cx.sb(k("cones"), [128, 128], BF16)
    sqb = cx.sb(k("csqb"), [128, NCK, TT], BF16)
    rstd = cx.sb(k("crstd"), [128, TT], F32)
    gt = [cx.sb(k("gt%d" % i), [128, TT], F32) for i in range(2)]
    tmp = [cx.sb(k("ctmp%d" % i), [128, TT], F32) for i in range(2)]
    wg = [cx.sb(k("wg%d" % i), [128, NCK * 128], BF16) for i in range(3)]
    wb = [cx.sb(k("wb%d" % i), [128, 4 * 128], BF16) for i in range(3)]
    wd = [cx.sb(k("wd%d" % i), [128, 32 * 128], BF16) for i in range(2)]
    banks = [cx.ps(k("cp%d" % i), [128, TT]) for i in range(8)]
    rr = [0]

    def bank():
        i = rr[0] % 8
        rr[0] += 1
        return banks[i], "cp%d" % i

    ci = CONST_NAMES.index("ones")
    cx.dma("pool", ones_b[:], cst_ap[:, ci * 128:(ci + 1) * 128], [], ["cones"], sk("cones"))
    cx.dma("sp", g1[:], g1_ap, [], ["cg1"], sk("cg1"))
    cx.dma("sp", g2[:], g2_ap, [], ["cg2"], sk("cg2"))
    cx.dma("sp", gf[:], gf_ap, [], ["cgf"], sk("cgf"))
    wcnt = {"wg": 0, "wb": 0, "wd": 0}
    bigk = [("big", i) for i in range(4)]

    def loadw(kind, bufs, ap, n):
        i = wcnt[kind] % len(bufs)
        wcnt[kind] += 1
        cx.dma("pool", bufs[i][:, 0:n], ap, [], [(kind, i)], sk("%s%d" % (kind, i)))
        return bufs[i], (kind, i)

    def rms(tl, lo, g, T, outf):
        cx.act(sqb[:, :, 0:T], xh[:, :, lo:lo + T], AF.Square, ["xh"], ["csqb"])
        pb, pk = bank()
        for c in range(NCK):
            cx.mm(pb[:, 0:T], ones_b[:], sqb[:, c, 0:T], c == 0, c == NCK - 1, ["csqb", "cones"], [pk])
        cx.act(rstd[:, 0:T], pb[:, 0:T], AF.Ln, [pk], ["crstd"], bias=EPS, scale=1.0 / D)
        cx.act(rstd[:, 0:T], rstd[:, 0:T], AF.Exp, ["crstd"], ["crstd"], scale=-0.5)
        for c in range(NCK):
            o, ok = outf(c)
            cx.stt("dve", o, xh[:, c, lo:lo + T], g[:, c:c + 1], rstd[:, 0:T],
                   ALU.mult, ALU.mult, ["xh", "crstd", "cg1", "cg2", "cgf"], [ok])

    for hi, tiles in enumerate(halves):
        lo = 0
        ltiles = []
        for (off, T) in tiles:
            ltiles.append((lo, off, T))
            lo += T
        for ti2, (lo, off, T) in enumerate(ltiles):
            cx.dma("sp", xh[:, :, lo:lo + T], x_in(off, T).rearrange("(c p) t -> p c t", p=128), ["xint"], ["xh"], sk("xh"))
            cx.dma(br_eng, big[:, 0:16, lo:lo + T], br_in(off, T), ["brg"], [("big", ti2 % 4)], sk("big%d" % (ti2 % 4)))
        for (lo, off, T) in ltiles:
            rms(None, lo, g1, T, lambda c, lo=lo, T=T: (hh[:, c, lo:lo + T], "hh"))
        for dc in range(NCK):
            for n in range(4):
                wgb, wgk = loadw("wg", wg, wg_ap[dc, n], NCK * 128)
                wbb, wbk = loadw("wb", wb, wb_ap[dc, n], 4 * 128)
                for ti, (lo, off, T) in enumerate(ltiles):
                    pa, pak = bank()
                    pbk_ = bank()
                    pb, pbk = pbk_
                    for c in range(NCK):
                        cx.mm(pa[:, 0:T], wgb[:, c * 128:(c + 1) * 128], hh[:, c, lo:lo + T], c == 0, c == NCK - 1,
                              [wgk, "hh"], [pak])
                    for j in range(4):
                        cx.mm(pb[:, 0:T], wbb[:, j * 128:(j + 1) * 128], big[:, j * 4 + n, lo:lo + T], j == 0, j == 3,
                              [wbk] + bigk, [pbk])
                    g_, gk = gt[ti % 2], ("gt", ti % 2)
                    t_, tk = tmp[ti % 2], ("ctmp", ti % 2)
                    cx.act(g_[:, 0:T], pa[:, 0:T], AF.Sigmoid, [pak], [gk])
                    if n == 0:
                        cx.tt("dve", macc[:, lo:lo + T], g_[:, 0:T], pb[:, 0:T], ALU.mult, [gk, pbk], [("macc", ti)])
                    else:
                        cx.tt("dve", t_[:, 0:T], g_[:, 0:T], pb[:, 0:T], ALU.mult, [gk, pbk], [tk])
                        if n < 3:
                            cx.tt("dve", macc[:, lo:lo + T], macc[:, lo:lo + T], t_[:, 0:T], ALU.add,
                                  [tk, ("macc", ti)], [("macc", ti)])
                        else:
                            cx.tt("dve", mixb[:, dc, lo:lo + T], macc[:, lo:lo + T], t_[:, 0:T], ALU.add,
                                  [tk, ("macc", ti)], [("mixb", dc)])
        mixk = [("mixb", c) for c in range(NCK)]
        for dc in range(NCK):
            wgb, wgk = loadw("wg", wg, wo_ap[dc], NCK * 128)
            for ti, (lo, off, T) in enumerate(ltiles):
                pa, pak = bank()
                for c in range(NCK):
                    cx.mm(pa[:, 0:T], wgb[:, c * 128:(c + 1) * 128], mixb[:, c, lo:lo + T], c == 0, c == NCK - 1,
                          [wgk] + mixk, [pak])
                cx.tt("dve", xh[:, dc, lo:lo + T], xh[:, dc, lo:lo + T], pa[:, 0:T], ALU.add, [pak, "xh"], ["xh"])
        for (lo, off, T) in ltiles:
            rms(None, lo, g2, T, lambda c, lo=lo, T=T: (hh[:, c, lo:lo + T], "hh"))
        for f in range(32):
            wgb, wgk = loadw("wg", wg, wu_ap[f], NCK * 128)
            for ti, (lo, off, T) in enumerate(ltiles):
                pa, pak = bank()
                for c in range(NCK):
                    cx.mm(pa[:, 0:T], wgb[:, c * 128:(c + 1) * 128], hh[:, c, lo:lo + T], c == 0, c == NCK - 1,
                          [wgk, "hh"], [pak])
                g_, gk = gt[ti % 2], ("gt", ti % 2)
                cx.act(g_[:, 0:T], pa[:, 0:T], AF.Relu, [pak], [gk])
                cx.tt("dve", big[:, f, lo:lo + T], g_[:, 0:T], g_[:, 0:T], ALU.mult, [gk], bigk)
        for dc in range(NCK):
            wdb, wdk = loadw("wd", wd, wd_ap[dc], 32 * 128)
            for ti, (lo, off, T) in enumerate(ltiles):
                pa, pak = bank()
                for f in range(32):
                    cx.mm(pa[:, 0:T], wdb[:, f * 128:(f + 1) * 128], big[:, f, lo:lo + T], f == 0, f == 31,
                          [wdk] + bigk, [pak])
                cx.tt("dve", xh[:, dc, lo:lo + T], xh[:, dc, lo:lo + T], pa[:, 0:T], ALU.add, [pak, "xh"], ["xh"])
        for (lo, off, T) in ltiles:
            if final:
                rms(None, lo, gf, T, lambda c, lo=lo, T=T: (xh[:, c, lo:lo + T], "xh"))
            cx.dma("sp", x_out(off, T).rearrange("(c p) t -> p c t", p=128), xh[:, :, lo:lo + T], ["xh"],
                   [("xout", off)], sk("xo"), is_out=True)
            if after_xout is not None:
                after_xout(cx, off)


FM_SPLITS = [0, 1, 3, 5, 6, 8, 9, 10]
TM_SPLITS = [2, 4, 7]


def prep_B_inputs(layer, j, w_in, norm1_g, lb_logits, hg_norm_g, pool_w, pool_scale, conv_w):
    cols = []
    for sidx in FM_SPLITS + TM_SPLITS:
        cols.append(w_in[layer][:, sidx * 512 + j * 128: sidx * 512 + (j + 1) * 128])
    w = np.ascontiguousarray(np.concatenate(cols, axis=1), dtype=np.float32)
    g1 = np.ascontiguousarray(norm1_g[layer].reshape(NCK, 128).T, dtype=np.float32)
    lbl = np.ascontiguousarray(lb_logits[:, j * 128:(j + 1) * 128].T, dtype=np.float32)
    vec = np.zeros((128, 8), np.float32)
    sl = slice(j * 128, (j + 1) * 128)
    vec[:, 0] = hg_norm_g[layer, sl]
    vec[:, 1] = pool_scale[layer, sl]
    vec[:, 2] = conv_w[layer, 0, sl]
    vec[:, 3] = conv_w[layer, 1, sl]
    vec[:, 4] = conv_w[layer, 2, sl]
    pw = np.ascontiguousarray(pool_w[layer, j], dtype=np.float32)
    _, cst = host_consts(POOL_WINDOWS[j])
    return {"w": w, "g1": g1, "lbl": lbl, "vec": vec, "pw": pw, "cst": np.ascontiguousarray(cst)}


def build_B(S, layer):
    nc = bass.Bass("TRN2", target_bir_lowering=False)
    Ltot = NMETA + S
    xT = nc.dram_tensor("xT", [D, Ltot], F32, kind="ExternalInput").ap()
    w = nc.dram_tensor("w", [D, 11 * 128], F32, kind="ExternalInput").ap()
    g1 = nc.dram_tensor("g1", [128, 8], F32, kind="ExternalInput").ap()
    lbl = nc.dram_tensor("lbl", [128, 4], F32, kind="ExternalInput").ap()
    vec = nc.dram_tensor("vec", [128, 8], F32, kind="ExternalInput").ap()
    pw = nc.dram_tensor("pw", [128, 128], F32, kind="ExternalInput").ap()
    cst = nc.dram_tensor("cst", [128, 10 * 128], F32, kind="ExternalInput").ap()
    br = nc.dram_tensor("br", [512, Ltot], BF16, kind="ExternalOutput").ap()
    P = Prog(nc)
    with ExitStack() as es:
        cx = Ctx(nc, P, es)
        emit_B(cx, S, layer, lambda off, T: xT[:, off:off + T], w, g1, lbl, vec, pw, cst,
               lambda off, T: br[:, off:off + T], pfx="b_")
        P.emit()
    return nc


def prep_C_weights(layer, w_in, w_branch, w_o, w_up, w_down):
    wg = w_in[layer][:, 11 * 512:].reshape(NCK, 128, 4, NCK, 128)
    wg = np.ascontiguousarray(wg.transpose(3, 2, 1, 0, 4)).reshape(NCK, 4, 128, NCK * 128)
    wb = w_branch[layer].reshape(4, 4, 128, NCK, 128)
    wb = np.ascontiguousarray(wb.transpose(3, 0, 2, 1, 4)).reshape(NCK, 4, 128, 4 * 128)
    wo = w_o[layer].reshape(NCK, 128, NCK, 128)
    wo = np.ascontiguousarray(wo.transpose(2, 1, 0, 3)).reshape(NCK, 128, NCK * 128)
    wu = w_up[layer].reshape(NCK, 128, 32, 128)
    wu = np.ascontiguousarray(wu.transpose(2, 1, 0, 3)).reshape(32, 128, NCK * 128)
    wd = w_down[layer].reshape(32, 128, NCK, 128)
    wd = np.ascontiguousarray(wd.transpose(2, 1, 0, 3)).reshape(NCK, 128, 32 * 128)
    return {"wg": wg, "wb": wb, "wo": wo, "wu": wu, "wd": wd}


def c_halves(ntok_x):
    tiles = [(0, NMETA)] + [(NMETA + TT * i, TT) for i in range(ntok_x // TT)]
    nh = (len(tiles) + 1) // 2
    return [tiles[:nh], tiles[nh:]] if len(tiles) > nh else [tiles]


def build_C(ntok_x, layer, final):
    nc = bass.Bass("TRN2", target_bir_lowering=False)
    NT = NMETA + ntok_x
    x = nc.dram_tensor("x", [D, NT], F32, kind="ExternalInput").ap()
    br = nc.dram_tensor("brc", [16, 128, NT], BF16, kind="ExternalInput").ap()
    g1 = nc.dram_tensor("g1", [128, 8], F32, kind="ExternalInput").ap()
    g2 = nc.dram_tensor("g2", [128, 8], F32, kind="ExternalInput").ap()
    gf = nc.dram_tensor("gf", [128, 8], F32, kind="ExternalInput").ap()
    cst = nc.dram_tensor("cst", [128, 10 * 128], F32, kind="ExternalInput").ap()
    wg = nc.dram_tensor("wg", [NCK, 4, 128, NCK * 128], F32, kind="ExternalInput").ap()
    wb = nc.dram_tensor("wb", [NCK, 4, 128, 4 * 128], F32, kind="ExternalInput").ap()
    wo = nc.dram_tensor("wo", [NCK, 128, NCK * 128], F32, kind="ExternalInput").ap()
    wu = nc.dram_tensor("wu", [32, 128, NCK * 128], F32, kind="ExternalInput").ap()
    wd = nc.dram_tensor("wd", [NCK, 128, 32 * 128], F32, kind="ExternalInput").ap()
    xo = nc.dram_tensor("xo", [D, NT], F32, kind="ExternalOutput").ap()
    P = Prog(nc)
    with ExitStack() as es:
        cx = Ctx(nc, P, es)
        emit_C(cx, layer, c_halves(ntok_x), lambda off, T: x[:, off:off + T],
               lambda off, T: br[:, :, off:off + T].rearrange("c p t -> p c t"), g1, g2, gf, cst, wg, wb, wo, wu, wd,
               lambda off, T: xo[:, off:off + T], final, pfx="c_")
        P.emit()
    return nc


def _r8(v):
    return np.ascontiguousarray(np.asarray(v, np.float32).reshape(NCK, 128).T)


def kernel_unfused(x, meta_tokens, lb_logits, norm1_g, w_in, hg_norm_g, pool_w, pool_scale, conv_w,
           w_branch, w_o, norm2_g, w_up, w_down, final_norm_g):
    f = lambda a: np.asarray(a, dtype=np.float32)
    x, meta_tokens, lb_logits, norm1_g, w_in = f(x), f(meta_tokens), f(lb_logits), f(norm1_g), f(w_in)
    hg_norm_g, pool_w, pool_scale, conv_w = f(hg_norm_g), f(pool_w), f(pool_scale), f(conv_w)
    w_branch, w_o, norm2_g, w_up, w_down, final_norm_g = f(w_branch), f(w_o), f(norm2_g), f(w_up), f(w_down), f(final_norm_g)
    B_, S, _ = x.shape
    NQ = 8 // B_
    SQ = S // NQ
    cores = list(range(8))
    xT = [np.ascontiguousarray(np.concatenate([meta_tokens, x[b]], axis=0).T) for b in range(B_)]
    cst2 = np.ascontiguousarray(host_consts(2)[1])
    for layer in range(DEPTH):
        ncB = build_B(S, layer)
        in_maps = []
        for r in cores:
            b, j = r // NQ, r % NQ
            im = prep_B_inputs(layer, j, w_in, norm1_g, lb_logits, hg_norm_g, pool_w, pool_scale, conv_w)
            im["xT"] = xT[b]
            in_maps.append(im)
        resB = run_bass_kernel_spmd(ncB, in_maps, core_ids=cores).results
        brs = [np.asarray(resB[r]["br"]) for r in cores]
        final = layer == DEPTH - 1
        ncC = build_C(SQ, layer, final)
        wts = prep_C_weights(layer, w_in, w_branch, w_o, w_up, w_down)
        g1, g2, gf = _r8(norm1_g[layer]), _r8(norm2_g[layer]), _r8(final_norm_g)
        in_maps = []
        for r in cores:
            b, q = r // NQ, r % NQ
            cols = np.concatenate([np.arange(NMETA), NMETA + q * SQ + np.arange(SQ)])
            im = dict(wts)
            im["x"] = np.ascontiguousarray(xT[b][:, cols])
            brc = np.empty((16, 128, NMETA + SQ), dtype=brs[0].dtype)
            for n in range(4):
                for j in range(4):
                    brc[j * 4 + n] = brs[b * NQ + j][n * 128:(n + 1) * 128][:, cols]
            im["brc"] = brc
            im["g1"], im["g2"], im["gf"], im["cst"] = g1, g2, gf, cst2
            in_maps.append(im)
        resC = run_bass_kernel_spmd(ncC, in_maps, core_ids=cores).results
        for b in range(B_):
            new = np.empty_like(xT[b])
            new[:, 0:NMETA] = np.asarray(resC[b * NQ]["xo"])[:, 0:NMETA]
            for q in range(NQ):
                new[:, NMETA + q * SQ: NMETA + (q + 1) * SQ] = np.asarray(resC[b * NQ + q]["xo"])[:, NMETA:]
            xT[b] = new
    out = np.stack([np.ascontiguousarray(xT[b][:, NMETA:].T) for b in range(B_)], axis=0)
    return out.astype(np.float32)


I32 = mybir.dt.int32
GROUPS = [[0, 1, 2, 3], [4, 5, 6, 7]]


def build_fused(S, depth=DEPTH):
    nc = bass.Bass("TRN2", target_bir_lowering=False)
    NQ = 4
    SQ = S // NQ
    NK = SQ // TT
    NT = NMETA + SQ
    Ltot = NMETA + S
    NTB = S // TT
    ext = lambda name, shape, dt=F32: nc.dram_tensor(name, shape, dt, kind="ExternalInput").ap()
    x0 = ext("x0", [D, NT])
    qcol = ext("qcol", [1, 8], I32)
    lbl = ext("lbl", [128, 4])
    cstB = ext("cstB", [128, 10 * 128])
    gf = ext("gf", [128, 8])
    L = []
    for l in range(depth):
        L.append(dict(
            wB=ext("wB%d" % l, [D, 11 * 128]), g1=ext("g1_%d" % l, [128, 8]), vec=ext("vec%d" % l, [128, 8]),
            pw=ext("pw%d" % l, [128, 128]), g2=ext("g2_%d" % l, [128, 8]),
            wg=ext("wg%d" % l, [NCK, 4, 128, NCK * 128]), wb=ext("wb%d" % l, [NCK, 4, 128, 4 * 128]),
            wo=ext("wo%d" % l, [NCK, 128, NCK * 128]), wu=ext("wu%d" % l, [32, 128, NCK * 128]),
            wd=ext("wd%d" % l, [NCK, 128, 32 * 128])))
    xo = nc.dram_tensor("xo", [D, NT], F32, kind="ExternalOutput").ap()
    xm = nc.dram_tensor("xm_i", [D, NMETA], F32).ap()
    xx = nc.dram_tensor("xx_i", [NK, 2, 512, TT], F32).ap()
    xg = nc.dram_tensor("xg_i", [NK, 2, NQ * 512, TT], F32).ap()
    brm = nc.dram_tensor("brm_i", [512, NMETA], BF16).ap()
    brx = nc.dram_tensor("brx_i", [NTB, 512, TT], BF16).ap()
    brgm = nc.dram_tensor("brgm_i", [NQ * 512, NMETA], BF16).ap()
    brgx = nc.dram_tensor("brgx_i", [NTB, NQ * 512, TT], BF16).ap()

    def ag(P, src, dst, r, w, key):
        P.op("pool", lambda e: e.collective_compute("AllGather", ALU.bypass, replica_groups=GROUPS,
                                                    ins=[src.opt()], outs=[dst.opt()]),
             reads=r, writes=w, dma_key=key, inc=1)

    def x_tile_ap(k_):
        return xx[k_].rearrange("h r t -> (h r) t")

    def x_in(off, T):
        return xm if off == 0 else x_tile_ap((off - NMETA) // TT)

    def x_loader(cx, ti, off, T, buf, bkey, semkey):
        if ti == 0:
            cx.dma("sp", buf[:, :, 0:T], xm.rearrange("(c p) t -> p c t", p=128), ["xm"], [bkey], semkey)
            return
        g = (ti - 1) * TT
        q, k_ = g // SQ, (g % SQ) // TT
        for h in range(2):
            src = xg[k_, h, q * 512:(q + 1) * 512, :].rearrange("(c p) t -> p c t", p=128)
            cx.dma("sp", buf[:, h * 4:(h + 1) * 4, 0:T], src, [("xg", k_, h)], [bkey], semkey)

    def br_out(off, T):
        return brm if off == 0 else brx[(off - NMETA) // TT]

    halves = c_halves(SQ)
    xg_tokens = {}
    vals = {}
    es_glob = ExitStack()
    state = None
    for l in range(depth):
        P = Prog(nc, state)
        state = P.state
        P.lastw.update(xg_tokens)
        xg_tokens.clear()
        with ExitStack() as es:
            cx = Ctx(nc, P, es)
            if l == 0:
                cx.dma("sp", xm, x0[:, 0:NMETA], [], ["xm"], "cpm")
                for k_ in range(NK):
                    cx.dma("sp", x_tile_ap(k_), x0[:, NMETA + k_ * TT:NMETA + (k_ + 1) * TT], [], [("xx", k_)], "cpx%d" % (k_ % 2))
            if l == 0:
                for k_ in range(NK):
                    for h in range(2):
                        ag(P, xx[k_, h], xg[k_, h], [("xx", k_)], [("xg", k_, h)], "ccx")

            def after_store(cx_, ti):
                if ti == 0:
                    ag(cx_.P, brm, brgm, [("brout_d", 0)], [("brg", 0)], "ccb")
                else:
                    ag(cx_.P, brx[ti - 1], brgx[ti - 1], [("brout_d", ti)], [("brg", ti)], "ccb")

            emit_B(cx, S, l, None, L[l]["wB"], L[l]["g1"], lbl, L[l]["vec"], L[l]["pw"], cstB, br_out,
                   pfx="b%d_" % l, x_loader=x_loader, after_store=after_store)
            P.op("sp", lambda e: e.dma_start(out=brm[0:1, 0:2], in_=brm[0:1, 0:2]),
                 reads=[("brg", t_) for t_ in range(NTB + 1)], writes=[], dma_key="fin", is_out=True)
            P.emit()
        P = Prog(nc, state)

        br_eng = "sp" if l < 2 else "act"

        def init_eng(handle, es2, br_eng=br_eng):
            if br_eng in vals:
                return
            vals[br_eng] = {}
            for kk in range(NK):
                reg = es_glob.enter_context(handle.register("qreg_%s%d" % (br_eng, kk)))
                handle.reg_load(reg, qcol[0:1, kk:kk + 1])
                vals[br_eng][kk] = handle.snap(reg, min_val=0, max_val=NTB - 1)

        P.init[br_eng] = init_eng

        def br_in(off, T, br_eng=br_eng):
            if off == 0:
                return brgm.rearrange("(c p) t -> p c t", p=128)
            kk = (off - NMETA) // TT
            return lambda: brgx[bass.ds(vals[br_eng][kk], 1)].rearrange("o (c p) t -> p (o c) t", p=128)

        final = l == depth - 1

        def after_xout(cx_, off):
            if off == 0:
                return
            k_ = (off - NMETA) // TT
            for h in range(2):
                xg_tokens[("xg", k_, h)] = cx_.P.op("pool", lambda e, k_=k_, h=h: e.collective_compute(
                    "AllGather", ALU.bypass, replica_groups=GROUPS, ins=[xx[k_, h].opt()], outs=[xg[k_, h].opt()]),
                    reads=[("xout", off)], writes=[("xg", k_, h)], dma_key="ccx", inc=1)

        with ExitStack() as es:
            cx = Ctx(nc, P, es)
            emit_C(cx, l, halves, x_in, br_in, L[l]["g1"], L[l]["g2"], gf, cstB,
                   L[l]["wg"], L[l]["wb"], L[l]["wo"], L[l]["wu"], L[l]["wd"],
                   (lambda off, T: xo[:, off:off + T]) if final else x_in, final, pfx="c%d_" % l, br_eng=br_eng,
                   after_xout=None if final else after_xout)
            P.emit()
    return nc


def fused_inputs(x, meta_tokens, lb_logits, norm1_g, w_in, hg_norm_g, pool_w, pool_scale, conv_w,
                 w_branch, w_o, norm2_g, w_up, w_down, final_norm_g, depth=DEPTH):
    B_, S, _ = x.shape
    NQ = 8 // B_
    SQ = S // NQ
    gf = _r8(final_norm_g)
    cw = [prep_C_weights(l, w_in, w_branch, w_o, w_up, w_down) for l in range(depth)]
    in_maps = []
    for r in range(8):
        b, q = r // NQ, r % NQ
        im = {}
        im["x0"] = np.ascontiguousarray(np.concatenate([meta_tokens, x[b, q * SQ:(q + 1) * SQ]], axis=0).T)
        qc = np.zeros((1, 8), np.int32)
        for kk in range(SQ // TT):
            qc[0, kk] = q * (SQ // TT) + kk
        im["qcol"] = qc
        im["gf"] = gf
        for l in range(depth):
            pb = prep_B_inputs(l, q, w_in, norm1_g, lb_logits, hg_norm_g, pool_w, pool_scale, conv_w)
            im["wB%d" % l], im["g1_%d" % l], im["vec%d" % l], im["pw%d" % l] = pb["w"], pb["g1"], pb["vec"], pb["pw"]
            im["lbl"], im["cstB"] = pb["lbl"], pb["cst"]
            im["g2_%d" % l] = _r8(norm2_g[l])
            for kname in ("wg", "wb", "wo", "wu", "wd"):
                im["%s%d" % (kname, l)] = cw[l][kname]
        in_maps.append(im)
    return in_maps


def kernel(x, meta_tokens, lb_logits, norm1_g, w_in, hg_norm_g, pool_w, pool_scale, conv_w,
           w_branch, w_o, norm2_g, w_up, w_down, final_norm_g):
    f = lambda a: np.asarray(a, dtype=np.float32)
    args = [f(a) for a in (x, meta_tokens, lb_logits, norm1_g, w_in, hg_norm_g, pool_w, pool_scale, conv_w,
                           w_branch, w_o, norm2_g, w_up, w_down, final_norm_g)]
    x = args[0]
    B_, S, _ = x.shape
    NQ = 8 // B_
    SQ = S // NQ
    nc = build_fused(S)
    in_maps = fused_inputs(*args)
    res = run_bass_kernel_spmd(nc, in_maps, core_ids=list(range(8))).results
    out = np.empty((B_, S, D), np.float32)
    for r in range(8):
        b, q = r // NQ, r % NQ
        out[b, q * SQ:(q + 1) * SQ] = np.asarray(res[r]["xo"])[:, NMETA:].T
    return out
```
